# Optimizing a Trainium2 kernel written in Bass

```python
import math
import jax, jax.numpy as jnp
from jax import lax
import numpy as np

D_MODEL = 1024
BATCH = 8
SEQ = 4096
DEPTH = 1

CHUNK = 64
Q_BLOCK = 128
ROPE_THETA = 10000.0
EPS = 1e-6
DA_HEADS = 8
DA_V_DIM = D_MODEL // DA_HEADS
DA_QK_DIM = DA_V_DIM // 2
DA_WIDTH = DA_HEADS * DA_V_DIM
RET_HEADS = 4
RET_V_DIM = D_MODEL // RET_HEADS
RET_K_DIM = RET_V_DIM // 2
RET_WIDTH = RET_HEADS * RET_V_DIM
D_FF = 4 * D_MODEL
N_BRANCH = 2
N_MOD = 6
COL_SIZES = (
    DA_HEADS * 2 * DA_QK_DIM,
    DA_HEADS * 2 * DA_QK_DIM,
    DA_WIDTH,
    RET_HEADS * RET_K_DIM,
    RET_HEADS * RET_K_DIM,
    RET_WIDTH,
    RET_WIDTH,
    N_BRANCH * D_MODEL,
)
COL_SPLITS = tuple(int(s) for s in np.cumsum(COL_SIZES)[:-1])
IN_COLS = int(sum(COL_SIZES))

kernel_name = "hybrid_diffattn_retention_block"


def rms_norm(x, g):
    x32 = x.astype(jnp.float32)
    y = x32 * lax.rsqrt(jnp.mean(x32 * x32, axis=-1, keepdims=True) + EPS)
    return (y * g.astype(jnp.float32)).astype(x.dtype)


def rope(x, inv_freq):
    s = x.shape[-2]
    ang = jnp.arange(s, dtype=jnp.float32)[:, None] * inv_freq[None, :]
    cos = jnp.cos(ang).astype(x.dtype)
    sin = jnp.sin(ang).astype(x.dtype)
    x1, x2 = jnp.split(x, 2, axis=-1)
    return jnp.concatenate([x1 * cos - x2 * sin, x2 * cos + x1 * sin], axis=-1)


def diff_attention(q, k, v, lam):
    b, h, _, s, d = q.shape
    nb = s // Q_BLOCK
    qb = q.reshape(b, h, 2, nb, Q_BLOCK, d).transpose(3, 0, 1, 2, 4, 5)
    key_chunk = jnp.arange(s) // CHUNK

    def block(args):
        qi, bi = args
        sc = jnp.einsum('bhmqd,bhmkd->bhmqk', qi, k).astype(jnp.float32)
        q_chunk = (bi * Q_BLOCK + jnp.arange(Q_BLOCK)) // CHUNK
        mask = key_chunk[None, :] <= q_chunk[:, None]
        p = jax.nn.softmax(jnp.where(mask, sc, -jnp.inf), axis=-1)
        a = p[:, :, 0] - lam * p[:, :, 1]
        return jnp.einsum('bhqk,bhkd->bhqd', a.astype(v.dtype), v)

    out = lax.map(block, (qb, jnp.arange(nb)))
    return out.transpose(1, 2, 0, 3, 4).reshape(b, h, s, v.shape[-1])


def retention(q, k, v, log_gamma):
    b, h, s, dk = q.shape
    dv = v.shape[-1]
    nc = s // CHUNK
    f32 = jnp.float32

    def to_chunks(t):
        return t.astype(f32).reshape(b, h, nc, CHUNK, t.shape[-1]).transpose(2, 0, 1, 3, 4)

    n = jnp.arange(CHUNK, dtype=f32)
    lg = log_gamma[:, None, None]
    diff = n[:, None] - n[None, :]
    d_intra = jnp.where(diff >= 0, jnp.exp(jnp.maximum(diff, 0.0) * lg), 0.0)
    inner_decay = jnp.exp((n + 1.0)[None, :] * log_gamma[:, None])[..., None]
    kv_decay = jnp.exp((CHUNK - 1.0 - n)[None, :] * log_gamma[:, None])[..., None]
    chunk_decay = jnp.exp(CHUNK * log_gamma)[:, None, None]

    def step(state, inp):
        qc, kc, vc = inp
        sc = jnp.einsum('bhid,bhjd->bhij', qc, kc) * d_intra
        y = (jnp.einsum('bhij,bhjv->bhiv', sc, vc)
             + jnp.einsum('bhid,bhdv->bhiv', qc, state) * inner_decay)
        state = state * chunk_decay + jnp.einsum('bhjd,bhjv->bhdv', kc * kv_decay, vc)
        return state, y

    state0 = jnp.zeros((b, h, dk, dv), f32)
    _, ys = lax.scan(step, state0, (to_chunks(q), to_chunks(k), to_chunks(v)))
    return ys.transpose(1, 2, 0, 3, 4).reshape(b, h, s, dv).astype(v.dtype)


def setup_inputs(seed: int = 0) -> dict:
    key = jax.random.key(seed)
    ks = jax.random.split(key, 20)
    f32 = jnp.float32
    nrm = lambda k, shp, sc: (jax.random.normal(k, shp, f32) * sc).astype(f32)
    gain = lambda k, shp: 1.0 + 0.02 * jax.random.normal(k, shp, f32)
    return {
        "x": nrm(ks[0], (BATCH, SEQ, D_MODEL), 1.0),
        "c": nrm(ks[1], (BATCH, D_MODEL), 1.0),
        "w_ada": nrm(ks[2], (DEPTH, D_MODEL, N_MOD * D_MODEL), 0.2 * D_MODEL ** -0.5),
        "b_ada": nrm(ks[3], (DEPTH, N_MOD * D_MODEL), 0.01),
        "g_norm1": gain(ks[4], (DEPTH, D_MODEL)),
        "w_in": nrm(ks[5], (DEPTH, D_MODEL, IN_COLS), D_MODEL ** -0.5),
        "g_q": gain(ks[6], (DEPTH, DA_QK_DIM)),
        "g_k": gain(ks[7], (DEPTH, DA_QK_DIM)),
        "lambda_q1": nrm(ks[8], (DEPTH, DA_QK_DIM), 0.1),
        "lambda_k1": nrm(ks[9], (DEPTH, DA_QK_DIM), 0.1),
        "lambda_q2": nrm(ks[10], (DEPTH, DA_QK_DIM), 0.1),
        "lambda_k2": nrm(ks[11], (DEPTH, DA_QK_DIM), 0.1),
        "g_da_out": gain(ks[12], (DEPTH, DA_V_DIM)),
        "g_ret_out": gain(ks[13], (DEPTH, RET_V_DIM)),
        "w_out": nrm(ks[14], (DEPTH, D_MODEL, D_MODEL), D_MODEL ** -0.5),
        "g_norm2": gain(ks[15], (DEPTH, D_MODEL)),
        "w_up": nrm(ks[16], (DEPTH, D_MODEL, D_FF), D_MODEL ** -0.5),
        "w_down": nrm(ks[17], (DEPTH, D_FF, D_MODEL), D_FF ** -0.5),
    }


def reference(x, c, w_ada, b_ada, g_norm1, w_in, g_q, g_k, lambda_q1, lambda_k1,
              lambda_q2, lambda_k2, g_da_out, g_ret_out, w_out, g_norm2, w_up, w_down):
    b, s, d = x.shape
    f32 = jnp.float32
    da_inv_freq = ROPE_THETA ** (-jnp.arange(0, DA_QK_DIM, 2, dtype=f32) / DA_QK_DIM)
    ret_inv_freq = 1.0 / (ROPE_THETA ** jnp.linspace(0.0, 1.0, RET_K_DIM // 2, dtype=f32))
    log_gamma = jnp.asarray(np.log(1.0 - 2.0 ** (-5.0 - np.arange(RET_HEADS))).astype(np.float32))

    for l in range(DEPTH):
        lambda_init = 0.8 - 0.6 * math.exp(-0.3 * l)
        mod = jax.nn.silu(c) @ w_ada[l] + b_ada[l]
        shift1, scale1, gate1, shift2, scale2, gate2 = [m[:, None, :] for m in jnp.split(mod, N_MOD, axis=-1)]

        hmix = rms_norm(x, g_norm1[l]) * (1.0 + scale1) + shift1
        proj = hmix @ w_in[l]
        qa, ka, va, qr, kr, vr, gr_swish, merge = jnp.split(proj, COL_SPLITS, axis=-1)

        qa = qa.reshape(b, s, DA_HEADS, 2, DA_QK_DIM).transpose(0, 2, 3, 1, 4)
        ka = ka.reshape(b, s, DA_HEADS, 2, DA_QK_DIM).transpose(0, 2, 3, 1, 4)
        va = va.reshape(b, s, DA_HEADS, DA_V_DIM).transpose(0, 2, 1, 3)
        qa = rope(rms_norm(qa, g_q[l]), da_inv_freq) * (DA_QK_DIM ** -0.5)
        ka = rope(rms_norm(ka, g_k[l]), da_inv_freq)
        lam = (jnp.exp(jnp.sum(lambda_q1[l].astype(f32) * lambda_k1[l].astype(f32)))
               - jnp.exp(jnp.sum(lambda_q2[l].astype(f32) * lambda_k2[l].astype(f32)))
               + lambda_init)
        oa = diff_attention(qa, ka, va, lam)
        oa = rms_norm(oa, g_da_out[l]) * (1.0 - lambda_init)
        oa = oa.transpose(0, 2, 1, 3).reshape(b, s, DA_WIDTH)

        qr = qr.reshape(b, s, RET_HEADS, RET_K_DIM).transpose(0, 2, 1, 3)
        kr = kr.reshape(b, s, RET_HEADS, RET_K_DIM).transpose(0, 2, 1, 3)
        vr = vr.reshape(b, s, RET_HEADS, RET_V_DIM).transpose(0, 2, 1, 3)
        qr = rope(qr, ret_inv_freq)
        kr = rope(kr, ret_inv_freq) * (RET_K_DIM ** -0.5)
        orr = rms_norm(retention(qr, kr, vr, log_gamma), g_ret_out[l])
        orr = orr.transpose(0, 2, 1, 3).reshape(b, s, RET_WIDTH) * jax.nn.silu(gr_swish)

        ga, gb = jnp.split(jax.nn.sigmoid(merge), N_BRANCH, axis=-1)
        y = (ga * oa + gb * orr) @ w_out[l]
        x = x + gate1 * y

        hff = rms_norm(x, g_norm2[l]) * (1.0 + scale2) + shift2
        ff = jnp.square(jax.nn.relu(hff @ w_up[l])) @ w_down[l]
        x = x + gate2 * ff
    return x
```

```python
import math
from contextlib import ExitStack

import numpy as np
import concourse.bass as bass
import concourse.mybir as mybir
from concourse.bass_utils import run_bass_kernel_spmd

F32 = mybir.dt.float32
BF16 = mybir.dt.bfloat16
AF = mybir.ActivationFunctionType
ALU = mybir.AluOpType
AX = mybir.AxisListType

D = 1024
DFF = 4096
EPS = 1e-6
LAMBDA_INIT = 0.2
C_QA, C_KA, C_VA, C_QR, C_KR, C_VR, C_GR, C_GA, C_GB = 0, 1024, 2048, 3072, 3584, 4096, 5120, 6144, 7168
ARENA_BYTES = 212736


class Sched:
    CE = ('pe', 'act', 'dve', 'pool')

    def __init__(self, nc, es):
        self.nc, self.es = nc, es
        self.q = {e: [] for e in ('pe', 'act', 'dve', 'pool', 'sp')}
        self.sem = {e: es.enter_context(nc.semaphore('s_' + e)) for e in self.CE}
        self.cnt = {e: 0 for e in self.CE}
        self.lastw, self.reads = {}, {}
        self.dsem = {}
        self.waited = {e: {} for e in self.q}
        self.bufs, self.bufkeys, self.inherit, self.known = [], {}, {}, set()

    @staticmethod
    def _merge(d, ev):
        k = id(ev[0])
        if k not in d or d[k][1] < ev[1]:
            d[k] = ev

    def new_buffer(self, name, lo, hi):
        inh = {}
        keep = []
        for (lo2, hi2, n2) in self.bufs:
            if lo2 < hi and lo < hi2:
                for k in self.bufkeys.get(n2, ()):
                    ev = self.lastw.get(k)
                    if ev is not None:
                        self._merge(inh, ev)
                    for ev in self.reads.get(k, {}).values():
                        self._merge(inh, ev)
                for ev in self.inherit.get(n2, {}).values():
                    self._merge(inh, ev)
                if lo <= lo2 and hi2 <= hi:
                    continue
            keep.append((lo2, hi2, n2))
        keep.append((lo, hi, name))
        self.bufs = keep
        self.inherit[name] = inh

    def _touch(self, k):
        if k in self.known:
            return
        self.known.add(k)
        name = k[0] if isinstance(k, tuple) else k
        self.bufkeys.setdefault(name, set()).add(k)
        self.reads[k] = dict(self.inherit.get(name, {}))

    def op(self, eng, fn, r=(), w=(), dma=None, ndma=1, sig=True):
        waits = {}

        def need(ev, raw):
            if ev is None:
                return
            sem, val, src = ev
            if src == eng and (not raw or eng == 'pe'):
                return
            k = id(sem)
            if self.waited[eng].get(k, 0) >= val:
                return
            if k not in waits or waits[k][1] < val:
                waits[k] = (sem, val)

        for k in r:
            self._touch(k)
            need(self.lastw.get(k), True)
        for k in w:
            self._touch(k)
            need(self.lastw.get(k), False)
            for ev in self.reads[k].values():
                need(ev, False)
        for k, (sem, val) in waits.items():
            self.waited[eng][k] = val
        if dma is not None:
            if dma not in self.dsem:
                self.dsem[dma] = [self.es.enter_context(self.nc.semaphore('d_' + dma)), 0]
            ds = self.dsem[dma]
            ds[1] += 16 * ndma
            ev = (ds[0], ds[1], 'dma')
            evh = ('dma', ds[0], ndma)
        elif sig:
            self.cnt[eng] += 1
            ev = (self.sem[eng], self.cnt[eng], eng)
            evh = ('ce', self.sem[eng])
        else:
            ev, evh = None, ('none',)
        self.q[eng].append((list(waits.values()), fn, evh))
        for k in r:
            if ev is not None:
                self._merge(self.reads[k], ev)
        for k in w:
            self.lastw[k] = ev
            self.reads[k] = {}
        return ev

    def emit(self, eng, e):
        for waits, fn, evh in self.q[eng]:
            for sem, val in waits:
                e.wait_ge(sem, val)
            ins = fn(e)
            if evh[0] == 'ce':
                ins.then_inc(evh[1], 1)
            elif evh[0] == 'dma':
                if not isinstance(ins, (list, tuple)):
                    ins = [ins]
                assert len(ins) == evh[2], (len(ins), evh[2])
                for i in ins:
                    i.then_inc(evh[1], 16)

    def run_block(self):
        with self.nc.Block() as block:
            @block.tensor
            def _(e):
                self.emit('pe', e)

            @block.scalar
            def _(e):
                self.emit('act', e)

            @block.vector
            def _(e):
                self.emit('dve', e)

            @block.gpsimd
            def _(e):
                self.emit('pool', e)

            @block.sync
            def _(e):
                self.emit('sp', e)


class Buf:
    def __init__(self, name, ap):
        self.name, self.ap = name, ap

    def k(self, *sub):
        return (self.name,) + sub


class KB:
    def __init__(self, nc, es, S, dbg=()):
        self.nc, self.es, self.S, self.dbg = nc, es, S, set(dbg)
        self.NT = S // 128
        self.sch = Sched(nc, es)
        self.arena = es.enter_context(nc.sbuf_tensor('arena', [128, ARENA_BYTES // 4], F32))
        self.top = 0
        self.nalloc = 0
        self.banks = [es.enter_context(nc.psum_tensor('bank%d' % i, [128, 512], F32)) for i in range(8)]
        self.outkeys = []
        self.deferred = []
        self.dr = {}

    def alloc(self, name, shape, dtype):
        esz = 4 if dtype == F32 else 2
        n = int(np.prod(shape[1:]))
        nbytes = (n * esz + 63) // 64 * 64
        lo, hi = self.top, self.top + nbytes
        assert hi <= ARENA_BYTES, ('SBUF arena overflow', name, hi)
        self.top = hi
        self.nalloc += 1
        uname = '%s#%d' % (name, self.nalloc)
        self.sch.new_buffer(uname, lo, hi)
        v = self.arena[:, lo // 4: hi // 4]
        if dtype != F32:
            v = v.bitcast(dtype)
        v = v[0:shape[0], 0:n]
        if len(shape) == 3:
            v = v.rearrange("p (a b) -> p a b", b=shape[2])
        elif len(shape) == 4:
            v = v.rearrange("p (a b c) -> p a b c", b=shape[2], c=shape[3])
        return Buf(uname, v)

    def ps(self, i):
        return ('ps', i)

    def bank_bf(self, i, lo, hi):
        return self.banks[i][:, lo:hi].bitcast(BF16)

    def op(self, *a, **k):
        return self.sch.op(*a, **k)

    def dram_in(self, name, shape, dtype=F32):
        t = self.nc.dram_tensor(name, list(shape), dtype, kind="ExternalInput").ap()
        self.dr[name] = t
        return t

    def dump(self, name, buf, keys, shape, src=None):
        if name not in self.dbg:
            return
        src = buf.ap if src is None else src
        t = self.nc.dram_tensor('dbg_' + name, list(shape), src.dtype, kind="ExternalOutput").ap()
        self.op('sp', lambda e: e.dma_start(out=t, in_=src), r=list(keys), w=[('dbgout', name)], dma='dbg_' + name)
        self.outkeys.append(('dbgout', name))

    @staticmethod
    def k2(buf, t):
        return [buf.k(t, 0), buf.k(t, 1)]

    def pump(self, n=1):
        for _ in range(n):
            if not self.deferred:
                return
            self.deferred.pop(0)()

    def flush(self):
        while self.deferred:
            self.deferred.pop(0)()

    def setup(self):
        nc, S, NT = self.nc, self.S, self.NT
        di = self.dram_in
        self.x = di('x', [S, D])
        di('c_col', [128, 8]); di('w_ada', [D, 6 * D]); di('b_ada_c', [128, 48])
        di('g1c', [128, 8]); di('g2c', [128, 8]); di('w_in', [D, 8192])
        di('gqk_row', [1, 256]); di('lam_row', [1, 256]); di('gda_row', [1, 128]); di('gret_row', [1, 256])
        di('w_out', [D, D]); di('w_up', [D, DFF]); di('w_down', [DFF, D])
        di('ropeA', [128, NT, 2, 64]); di('ropeR', [128, NT, 2, 128])
        di('retmask', [128, 128]); di('retc', [128, 8]); di('identf', [128, 128])
        self.out = nc.dram_tensor('out', [S, D], F32, kind="ExternalOutput").ap()
        self.mT_d = nc.dram_tensor('mT_d', [8, 128, S], BF16, kind="Internal").ap()
        self.wup_s = nc.dram_tensor('wup_s', [8, 128, 8, 512], BF16, kind="Internal").ap()
        self.wdn_s = nc.dram_tensor('wdn_s', [2, 8, 128, 4, 512], BF16, kind="Internal").ap()

        A = self.alloc
        dr = self.dr
        self.c_col = A('c_col', [128, 8], F32)
        self.b_ada_c = A('b_ada_c', [128, 48], F32)
        self.g1c = A('g1c', [128, 8], F32)
        self.g2c = A('g2c', [128, 8], F32)
        self.identf = A('identf', [128, 128], F32)
        self.ident = A('ident', [128, 128], BF16)
        self.onesf = A('onesf', [128, 128], F32)
        self.mhalf = A('mhalf', [128, 4], F32)
        self.modc = A('modc', [128, 48], F32)
        self.A1 = A('A1', [128, 8], F32)
        self.A2 = A('A2', [128, 8], F32)
        self.retmask = A('retmask', [128, 128], F32)
        self.retc = A('retc', [128, 8], F32)
        self.gqk_b = A('gqk_b', [128, 256], F32)
        self.lam_b = A('lam_b', [128, 256], F32)
        self.gda_b = A('gda_b', [128, 128], F32)
        self.gret_b = A('gret_b', [128, 256], F32)
        self.lamc = A('lamc', [128, 1], F32)
        self.lamt = A('lamt', [128, 4], F32)
        self.lamp = A('lamp', [128, 128], F32)

        loads = [(self.c_col, dr['c_col']), (self.b_ada_c, dr['b_ada_c']), (self.g1c, dr['g1c']),
                 (self.g2c, dr['g2c']), (self.identf, dr['identf']), (self.retmask, dr['retmask']),
                 (self.retc, dr['retc']),
                 (self.gqk_b, dr['gqk_row'][0:1, :].broadcast_to([128, 256])),
                 (self.lam_b, dr['lam_row'][0:1, :].broadcast_to([128, 256])),
                 (self.gda_b, dr['gda_row'][0:1, :].broadcast_to([128, 128])),
                 (self.gret_b, dr['gret_row'][0:1, :].broadcast_to([128, 256]))]

        def ld(e):
            return [e.dma_start(out=b.ap, in_=src) for b, src in loads]
        self.op('sp', ld, w=[b.k() for b, _ in loads], dma='const', ndma=len(loads))
        self.op('dve', lambda e: e.tensor_copy(out=self.ident.ap, in_=self.identf.ap),
                r=[self.identf.k()], w=[self.ident.k()])
        self.op('pool', lambda e: e.memset(self.onesf.ap, 1.0), w=[self.onesf.k()])
        self.op('pool', lambda e: e.memset(self.mhalf.ap, -0.5), w=[self.mhalf.k()])
        self.op('dve', lambda e: e.tensor_scalar(out=self.gda_b.ap, in0=self.gda_b.ap,
                                                 scalar1=(1.0 - LAMBDA_INIT) * 0.5, scalar2=None, op0=ALU.mult),
                r=[self.gda_b.k()], w=[self.gda_b.k()])
        self.op('dve', lambda e: e.tensor_scalar(out=self.gret_b.ap, in0=self.gret_b.ap,
                                                 scalar1=0.25, scalar2=None, op0=ALU.mult),
                r=[self.gret_b.k()], w=[self.gret_b.k()])
        lb, lp, lt = self.lam_b, self.lamp, self.lamt
        self.op('dve', lambda e: e.tensor_tensor(out=lp.ap, in0=lb.ap[:, 0:128], in1=lb.ap[:, 128:256], op=ALU.mult),
                r=[lb.k()], w=[lp.k()])
        self.op('dve', lambda e: e.tensor_reduce(out=lt.ap[:, 0:2], in_=lp.ap.rearrange("p (a b) -> p a b", b=64),
                                                 axis=AX.X, op=ALU.add), r=[lp.k()], w=[lt.k(0)])
        self.op('act', lambda e: e.activation(out=lt.ap[:, 2:4], in_=lt.ap[:, 0:2], func=AF.Exp),
                r=[lt.k(0)], w=[lt.k(1)])
        self.op('dve', lambda e: e.tensor_tensor(out=self.lamc.ap, in0=lt.ap[:, 2:3], in1=lt.ap[:, 3:4],
                                                 op=ALU.subtract), r=[lt.k(1)], w=[self.lamc.k()])
        self.op('dve', lambda e: e.tensor_scalar(out=self.lamc.ap, in0=self.lamc.ap, scalar1=LAMBDA_INIT,
                                                 scalar2=None, op0=ALU.add), r=[self.lamc.k()], w=[self.lamc.k()])
        self.persist_top = self.top

    def ffn_scratch(self):
        dr = self.dr

        def up(e):
            ins = []
            for c in range(8):
                src = dr['w_up'][:, c * 512:(c + 1) * 512].rearrange("(dc p) f -> p dc f", p=128)
                ins.append(e.dma_start(out=self.wup_s[c], in_=src))
            return ins
        self.op('pool', up, w=[('wup_s',)], dma='wup_s', ndma=8)

        def dn(e):
            ins = []
            for nb in range(2):
                for c in range(8):
                    src = dr['w_down'][c * 512:(c + 1) * 512, nb * 512:(nb + 1) * 512].rearrange(
                        "(fb p) n -> p fb n", p=128)
                    ins.append(e.dma_start(out=self.wdn_s[nb, c], in_=src))
            return ins
        self.op('pool', dn, w=[('wdn_s',)], dma='wdn_s', ndma=16)

    def phase_mod(self):
        A, op, dr = self.alloc, self.op, self.dr
        th = A('th', [128, 8], F32)
        scf = A('scf', [128, 8], F32)
        scb = A('scb', [128, 8], BF16)
        was = [A('wa0', [128, 8, 1024], BF16), A('wa1', [128, 8, 1024], BF16)]
        cc = self.c_col
        op('act', lambda e: e.activation(out=th.ap, in_=cc.ap, func=AF.Tanh, scale=0.5), r=[cc.k()], w=[th.k()])
        op('dve', lambda e: e.scalar_tensor_tensor(out=scf.ap, in0=th.ap, scalar=1.0, in1=cc.ap,
                                                   op0=ALU.add, op1=ALU.mult), r=[th.k(), cc.k()], w=[scf.k()])
        op('dve', lambda e: e.tensor_scalar(out=scb.ap, in0=scf.ap, scalar1=0.5, scalar2=None, op0=ALU.mult),
           r=[scf.k()], w=[scb.k()])
        bank = self.banks[7]
        for v in range(6):
            wa = was[v % 2]
            src = dr['w_ada'][:, v * 1024:(v + 1) * 1024].rearrange("(kc p) n -> p kc n", p=128)
            op('pool', lambda e, wa=wa, src=src: e.dma_start(out=wa.ap, in_=src), w=[wa.k()], dma='wa%d' % (v % 2))

            def mm(e, v=v, wa=wa):
                for j in range(8):
                    for kc in range(8):
                        last = e.matmul(bank[:, v * 8 + j: v * 8 + j + 1], lhsT=wa.ap[:, kc, j * 128:(j + 1) * 128],
                                        rhs=scb.ap[:, kc:kc + 1], start=(kc == 0), stop=(kc == 7))
                return last
            op('pe', mm, r=[wa.k(), scb.k()], w=[self.ps(7)])
        mc = self.modc
        op('dve', lambda e: e.tensor_tensor(out=mc.ap, in0=bank[:, 0:48], in1=self.b_ada_c.ap, op=ALU.add),
           r=[self.b_ada_c.k()], w=[self.ps(7), mc.k()])
        op('dve', lambda e: e.scalar_tensor_tensor(out=self.A1.ap, in0=mc.ap[:, 8:16], scalar=1.0, in1=self.g1c.ap,
                                                   op0=ALU.add, op1=ALU.mult), r=[mc.k(), self.g1c.k()], w=[self.A1.k()])
        op('dve', lambda e: e.scalar_tensor_tensor(out=self.A2.ap, in0=mc.ap[:, 32:40], scalar=1.0, in1=self.g2c.ap,
                                                   op0=ALU.add, op1=ALU.mult), r=[mc.k(), self.g2c.k()], w=[self.A2.k()])
        self.dump('modc', mc, [mc.k()], [128, 48])

    def norm_to_T(self, xt_ap, xkey, t, stats, xn, A_, Bcol, dst_fn, dst_key, banks, tag):
        op = self.op
        ss, ms, rstd, junk = stats
        op('act', lambda e: e.activation(out=junk.ap, in_=xt_ap, func=AF.Square, accum_out=ss.ap[:, t:t + 1]),
           r=[xkey], w=[junk.k(), ss.k(t)])
        op('pool', lambda e: e.tensor_scalar(out=ms.ap[:, t:t + 1], in0=ss.ap[:, t:t + 1], scalar1=1.0 / D,
                                             scalar2=EPS, op0=ALU.mult, op1=ALU.add), r=[ss.k(t)], w=[ms.k(t)])
        op('pool', lambda e: e.tensor_tensor(out=rstd.ap[:, t:t + 1], in0=ms.ap[:, t:t + 1],
                                             in1=self.mhalf.ap[:, 0:1], op=ALU.pow),
           r=[ms.k(t), self.mhalf.k()], w=[rstd.k(t)])
        op('dve', lambda e: e.tensor_scalar(out=xn.ap, in0=xt_ap, scalar1=rstd.ap[:, t:t + 1], scalar2=None,
                                            op0=ALU.mult), r=[xkey, rstd.k(t)], w=[xn.k()])
        ba, bb = banks
        pa = self.bank_bf(ba, 0, 256).rearrange("p (a b) -> p a b", b=128)
        pb = self.bank_bf(bb, 0, 256).rearrange("p (a b) -> p a b", b=128)

        def tr(e):
            for dc in range(8):
                pt = pa if dc < 4 else pb
                last = e.transpose(out=pt[:, dc % 4, :], in_=xn.ap[:, dc * 128:(dc + 1) * 128], identity=self.ident.ap)
            return last
        op('pe', tr, r=[xn.k(), self.ident.k()], w=[self.ps(ba), self.ps(bb)])
        da = dst_fn(0, 4)
        db = dst_fn(4, 8)

        def ev_act(e):
            for dc in range(4):
                last = e.activation(out=da[:, dc, :], in_=pa[:, dc, :], func=AF.Identity,
                                    scale=A_.ap[:, dc:dc + 1], bias=Bcol[:, dc:dc + 1])
            return last
        op('act', ev_act, r=[A_.k(), self.modc.k()], w=[self.ps(ba), dst_key[0]])

        def ev_dve(e):
            for dc in range(4, 8):
                last = e.tensor_scalar(out=db[:, dc - 4, :], in0=pb[:, dc - 4, :], scalar1=A_.ap[:, dc:dc + 1],
                                       scalar2=Bcol[:, dc:dc + 1], op0=ALU.mult, op1=ALU.add)
            return last
        op('dve', ev_dve, r=[A_.k(), self.modc.k()], w=[self.ps(bb), dst_key[1]])

    def phase_norm1(self):
        A, op, NT = self.alloc, self.op, self.NT
        xbs = [A('xb%d' % i, [128, D], F32) for i in range(3)]
        xns = [A('xn%d' % i, [128, D], BF16) for i in range(2)]
        junk = A('junk', [128, D], BF16)
        ss = A('ss1', [128, NT], F32)
        ms = A('ms1', [128, NT], F32)
        rstd = A('rstd1', [128, NT], F32)
        hm = self.hmixT
        for t in range(NT):
            xb = xbs[t % 3]
            op('sp', lambda e, xb=xb, t=t: e.dma_start(out=xb.ap, in_=self.x[t * 128:(t + 1) * 128, :]),
               w=[xb.k()], dma='xb%d' % (t % 3))
            banks = (0, 1) if t % 2 == 0 else (2, 3)
            self.norm_to_T(xb.ap, xb.k(), t, (ss, ms, rstd, junk), xns[t % 2], self.A1, self.modc.ap[:, 0:8],
                           lambda lo, hi, t=t: hm.ap[:, lo:hi, t * 128:(t + 1) * 128], self.k2(hm, t), banks, 'n1')
        self.dump('hmixT', hm, sum([self.k2(hm, t) for t in range(NT)], []), [128, 8, self.S])

    def load_w(self, slot, blocks):
        dr = self.dr
        off = 0
        srcs = []
        for c0, n in blocks:
            srcs.append((off, n, dr['w_in'][:, c0:c0 + n].rearrange("(dc p) n -> p dc n", p=128)))
            off += n
        ws = self.wslots[slot]

        def ld(e):
            return [e.dma_start(out=ws.ap[:, :, o:o + n], in_=src) for o, n, src in srcs]
        self.op('pool', ld, w=[ws.k()], dma='wslot%d' % slot, ndma=len(srcs))

    def ret_head(self, r, slot):
        A, op, NT, S = self.alloc, self.op, self.NT, self.S
        mark = self.top
        ws = self.wslots[slot]
        hm = self.hmixT
        ropeR = A('ropeR', [128, NT, 2, 128], F32)
        op('sp', lambda e: e.dma_start(out=ropeR.ap, in_=self.dr['ropeR']), w=[ropeR.k()], dma='ropeR')
        nb2 = 2
        tq = [[A('tq%d_%d' % (i, j), [128, 128], F32) for j in range(2)] for i in range(nb2)]
        tk = [[A('tk%d_%d' % (i, j), [128, 128], F32) for j in range(2)] for i in range(nb2)]
        qt = [A('qt%d' % i, [128, 128], BF16) for i in range(nb2)]
        kt = [A('kt%d' % i, [128, 128], BF16) for i in range(nb2)]
        kp = [A('kp%d' % i, [128, 128], BF16) for i in range(nb2)]
        vb = [A('vb%d' % i, [128, 256], BF16) for i in range(nb2)]
        qkT = [A('qkT%d' % i, [128, 256], BF16) for i in range(nb2)]
        scm = [A('scm%d' % i, [128, 128], BF16) for i in range(nb2)]
        St = A('St', [128, 256], F32)
        Sbf = [A('Sbf%d' % i, [128, 256], BF16) for i in range(2)]
        junk = A('rjunk', [128, 256], BF16)
        ssr = A('ssr', [128, NT], F32)
        msr = A('msr', [128, NT], F32)
        rsr = A('rsr', [128, NT], F32)
        g = 1.0 - 2.0 ** (-5.0 - r)
        gC = float(np.exp(128.0 * np.log(g)))
        idec = self.retc.ap[:, 2 * r:2 * r + 1]
        kdec = self.retc.ap[:, 2 * r + 1:2 * r + 2]
        orr = self.orr
        mask = self.retmask

        def swapped(ap):
            a = [list(x) for x in ap.ap]
            return bass.AP(ap.tensor, ap.offset + 64, [a[0], [-64, 2], [1, 64]])

        def st1(c):
            pb = c % 2

            def mm(e):
                for dc in range(8):
                    last = e.matmul(self.banks[pb][:, 0:512], lhsT=hm.ap[:, dc, c * 128:(c + 1) * 128],
                                    rhs=ws.ap[:, dc, 0:512], start=(dc == 0), stop=(dc == 7))
                return last
            op('pe', mm, r=self.k2(hm, c) + [ws.k()], w=[self.ps(pb)])

        def st2(c):
            pb = c % 2
            b = c % nb2
            P = self.banks[pb]
            cc = ropeR.ap[:, c, 0, :]
            sn = ropeR.ap[:, c, 1, :]
            h3 = lambda ap: ap.rearrange("p (h d) -> p h d", h=2)
            for (src, dec, tt, dst) in ((P[:, 0:128], idec, tq[b], qt[b]), (P[:, 128:256], kdec, tk[b], kt[b])):
                op('dve', lambda e, src=src, dec=dec, tt=tt: e.scalar_tensor_tensor(
                    out=tt[0].ap, in0=src, scalar=dec, in1=cc, op0=ALU.mult, op1=ALU.mult),
                   r=[ropeR.k(), self.retc.k()], w=[self.ps(pb), tt[0].k()])
                op('dve', lambda e, src=src, dec=dec, tt=tt: e.scalar_tensor_tensor(
                    out=h3(tt[1].ap), in0=swapped(src), scalar=dec, in1=h3(sn), op0=ALU.mult, op1=ALU.mult),
                   r=[ropeR.k(), self.retc.k()], w=[self.ps(pb), tt[1].k()])
                op('pool', lambda e, tt=tt, dst=dst: e.tensor_tensor(out=dst.ap, in0=tt[0].ap, in1=tt[1].ap, op=ALU.add),
                   r=[tt[0].k(), tt[1].k()], w=[dst.k()])
            op('act', lambda e: e.activation(out=vb[b].ap, in_=P[:, 256:512], func=AF.Copy), w=[self.ps(pb), vb[b].k()])
            op('act', lambda e: e.activation(out=kp[b].ap, in_=kt[b].ap, func=AF.Copy, scale=gC),
               r=[kt[b].k()], w=[kp[b].k()])
            pT = self.bank_bf(4, 0, 128)

            def tr(e):
                e.transpose(out=pT[:, 0:128], in_=qt[b].ap, identity=self.ident.ap)
                return e.transpose(out=pT[:, 128:256], in_=kt[b].ap, identity=self.ident.ap)
            op('pe', tr, r=[qt[b].k(), kt[b].k(), self.ident.k()], w=[self.ps(4)])
            op('dve', lambda e: e.tensor_copy(out=qkT[b].ap, in_=pT), w=[self.ps(4), qkT[b].k()])

        def st3(c):
            b = c % nb2
            op('pe', lambda e: e.matmul(self.banks[5][:, 0:128], lhsT=qkT[b].ap[:, 128:256], rhs=qkT[b].ap[:, 0:128],
                                        start=True, stop=True), r=[qkT[b].k()], w=[self.ps(5)])
            op('dve', lambda e: e.tensor_tensor(out=scm[b].ap, in0=self.banks[5][:, 0:128], in1=mask.ap, op=ALU.mult),
               r=[mask.k()], w=[self.ps(5), scm[b].k()])

            def ymm(e):
                last = e.matmul(self.banks[6][:, 0:256], lhsT=scm[b].ap, rhs=vb[b].ap, start=True, stop=(c == 0))
                if c > 0:
                    last = e.matmul(self.banks[6][:, 0:256], lhsT=qkT[b].ap[:, 0:128], rhs=Sbf[c % 2].ap,
                                    start=False, stop=True)
                return last
            op('pe', ymm, r=[scm[b].k(), vb[b].k(), qkT[b].k()] + ([Sbf[c % 2].k()] if c > 0 else []), w=[self.ps(6)])
            if c < NT - 1:
                op('pe', lambda e: e.matmul(self.banks[7][:, 0:256], lhsT=kp[b].ap, rhs=vb[b].ap, start=True, stop=True),
                   r=[kp[b].k(), vb[b].k()], w=[self.ps(7)])
                if c == 0:
                    op('dve', lambda e: e.tensor_copy(out=St.ap, in_=self.banks[7][:, 0:256]), w=[self.ps(7), St.k()])
                else:
                    op('dve', lambda e: e.scalar_tensor_tensor(out=St.ap, in0=St.ap, scalar=gC, in1=self.banks[7][:, 0:256],
                                                               op0=ALU.mult, op1=ALU.add),
                       r=[St.k()], w=[self.ps(7), St.k()])
                op('act', lambda e: e.activation(out=Sbf[(c + 1) % 2].ap, in_=St.ap, func=AF.Copy),
                   r=[St.k()], w=[Sbf[(c + 1) % 2].k()])
            Y = self.banks[6][:, 0:256]
            op('act', lambda e: e.activation(out=junk.ap, in_=Y, func=AF.Square, accum_out=ssr.ap[:, c:c + 1]),
               w=[self.ps(6), junk.k(), ssr.k(c)])
            op('pool', lambda e: e.tensor_scalar(out=msr.ap[:, c:c + 1], in0=ssr.ap[:, c:c + 1], scalar1=1.0 / 256,
                                                 scalar2=EPS, op0=ALU.mult, op1=ALU.add), r=[ssr.k(c)], w=[msr.k(c)])
            op('pool', lambda e: e.tensor_tensor(out=rsr.ap[:, c:c + 1], in0=msr.ap[:, c:c + 1],
                                                 in1=self.mhalf.ap[:, 0:1], op=ALU.pow),
               r=[msr.k(c), self.mhalf.k()], w=[rsr.k(c)])
            op('dve', lambda e: e.scalar_tensor_tensor(out=orr.ap[:, c, :], in0=Y, scalar=rsr.ap[:, c:c + 1],
                                                       in1=self.gret_b.ap, op0=ALU.mult, op1=ALU.mult),
               r=[rsr.k(c), self.gret_b.k()], w=[self.ps(6), orr.k(c)])

        for i in range(NT + 2):
            if i < NT:
                st1(i)
            if 0 <= i - 2 < NT:
                st3(i - 2)
            if 0 <= i - 1 < NT:
                st2(i - 1)
        self.dump('orr%d' % r, orr, [orr.k(c) for c in range(NT)], [128, NT, 256])
        self.top = mark

    def da_head(self, h, slot):
        A, op, NT, S = self.alloc, self.op, self.NT, self.S
        NQ = S // 512
        mark = self.top
        ws = self.wslots[slot]
        hm = self.hmixT
        ropeA = self.ropeA
        QKT = A('QKT', [128, 2, S], BF16)
        V1 = A('V1', [128, NT, 129], BF16)
        exps = [A('exp%d' % i, [128, 2, 512], BF16) for i in range(3)]
        nb2 = 2
        sqj = [A('sqj%d' % i, [128, 256], F32) for i in range(nb2)]
        u = [A('u%d' % i, [128, 256], F32) for i in range(nb2)]
        t1 = [A('t1%d' % i, [128, 256], F32) for i in range(nb2)]
        t2 = [A('t2%d' % i, [128, 256], F32) for i in range(nb2)]
        o_ = [A('o%d' % i, [128, 256], F32) for i in range(nb2)]
        qk = [A('qk%d' % i, [128, 256], BF16) for i in range(nb2)]
        ss4 = A('ss4', [128, NT, 4], F32)
        ms4 = A('ms4', [128, NT, 4], F32)
        rs4 = A('rs4', [128, NT, 4], F32)
        accS = [A('accS%d' % i, [128, 8, 129], F32) for i in range(2)]
        rr = A('rr', [128, NT, 4], F32)
        sso = A('sso', [128, NT], F32)
        mso = A('mso', [128, NT], F32)
        rso = A('rso', [128, NT], F32)
        tt = [A('tt%d' % i, [128, 128], F32) for i in range(2)]
        oo = [A('oo%d' % i, [128, 128], F32) for i in range(2)]
        ojk = A('ojk', [128, 128], F32)
        on = [A('on%d' % i, [128, 128], F32) for i in range(2)]
        tg = [A('tg%d' % i, [128, 384], F32) for i in range(2)]
        uu = [A('uu%d' % i, [128, 128], F32) for i in range(2)]
        u2 = [A('u2%d' % i, [128, 128], F32) for i in range(2)]
        u3 = [A('u3%d' % i, [128, 128], F32) for i in range(2)]
        m1 = [A('m1%d' % i, [128, 128], F32) for i in range(2)]
        mg = [A('mg%d' % i, [128, 128], BF16) for i in range(2)]
        mst = [A('mst%d' % i, [128, 512], BF16) for i in range(2)]
        orr = self.orr
        hh = h % 2

        op('pool', lambda e: e.memset(V1.ap[:, :, 128:129], 1.0), w=[V1.k('ones')])

        g4 = lambda ap: ap.rearrange("p (g d) -> p g d", g=4)
        g42 = lambda ap: ap.rearrange("p (g h d) -> p g h d", g=4, h=2)

        def swapped(ap):
            a = [list(x) for x in ap.ap]
            return bass.AP(ap.tensor, ap.offset + 32, [a[0], [64, 4], [-32, 2], [1, 32]])

        def p1(t):
            pb = t % 2

            def mm(e):
                for dc in range(8):
                    last = e.matmul(self.banks[pb][:, 0:384], lhsT=hm.ap[:, dc, t * 128:(t + 1) * 128],
                                    rhs=ws.ap[:, dc, 0:384], start=(dc == 0), stop=(dc == 7))
                return last
            op('pe', mm, r=self.k2(hm, t) + [ws.k()], w=[self.ps(pb)])

        def p2(t):
            pb = t % 2
            b = t % nb2
            P = self.banks[pb]
            op('act', lambda e: e.activation(out=sqj[b].ap, in_=P[:, 0:256], func=AF.Square),
               w=[self.ps(pb), sqj[b].k()])
            op('dve', lambda e: e.tensor_tensor(out=u[b].ap, in0=P[:, 0:256], in1=self.gqk_b.ap, op=ALU.mult),
               r=[self.gqk_b.k()], w=[self.ps(pb), u[b].k()])
            op('act', lambda e: e.activation(out=V1.ap[:, t, 0:128], in_=P[:, 256:384], func=AF.Copy),
               w=[self.ps(pb), V1.k(t)])
            op('dve', lambda e: e.tensor_reduce(out=ss4.ap[:, t, :], in_=g4(sqj[b].ap), axis=AX.X, op=ALU.add),
               r=[sqj[b].k()], w=[ss4.k(t)])
            op('pool', lambda e: e.tensor_scalar(out=ms4.ap[:, t, :], in0=ss4.ap[:, t, :], scalar1=1.0 / 64,
                                                 scalar2=EPS, op0=ALU.mult, op1=ALU.add), r=[ss4.k(t)], w=[ms4.k(t)])
            op('pool', lambda e: e.tensor_tensor(out=rs4.ap[:, t, :], in0=ms4.ap[:, t, :], in1=self.mhalf.ap,
                                                 op=ALU.pow), r=[ms4.k(t), self.mhalf.k()], w=[rs4.k(t)])
            cc = ropeA.ap[:, t, 0:1, :].broadcast_to([128, 4, 64])
            sn = ropeA.ap[:, t, 1, :].rearrange("p (h d) -> p h d", h=2).unsqueeze(1).broadcast_to([128, 4, 2, 32])
            op('pool', lambda e: e.tensor_tensor(out=g4(t1[b].ap), in0=g4(u[b].ap), in1=cc, op=ALU.mult),
               r=[u[b].k(), ropeA.k()], w=[t1[b].k()])
            op('dve', lambda e: e.tensor_tensor(out=g42(t2[b].ap), in0=swapped(u[b].ap), in1=sn, op=ALU.mult),
               r=[u[b].k(), ropeA.k()], w=[t2[b].k()])
            op('pool', lambda e: e.tensor_tensor(out=o_[b].ap, in0=t1[b].ap, in1=t2[b].ap, op=ALU.add),
               r=[t1[b].k(), t2[b].k()], w=[o_[b].k()])
            rb = rs4.ap[:, t, :].unsqueeze(2).broadcast_to([128, 4, 64])
            op('dve', lambda e: e.tensor_tensor(out=g4(qk[b].ap), in0=g4(o_[b].ap), in1=rb, op=ALU.mult),
               r=[o_[b].k(), rs4.k(t)], w=[qk[b].k()])
            tb = 2 + (t % 2)
            pT = self.bank_bf(tb, 0, 128).rearrange("p (a b) -> p a b", b=128)

            def tr(e):
                e.transpose(out=pT[:, 0, :], in_=qk[b].ap[:, 0:128], identity=self.ident.ap)
                return e.transpose(out=pT[:, 1, :], in_=qk[b].ap[:, 128:256], identity=self.ident.ap)
            op('pe', tr, r=[qk[b].k(), self.ident.k()], w=[self.ps(tb)])
            op('act', lambda e: e.activation(out=QKT.ap[:, :, t * 128:(t + 1) * 128], in_=pT, func=AF.Copy),
               w=[self.ps(tb), QKT.k(t)])

        for i in range(NT + 1):
            if i < NT:
                p1(i)
            if i >= 1:
                p2(i - 1)
        self.dump('QKT%d' % h, QKT, [QKT.k(t) for t in range(NT)], [128, 2, S])
        self.dump('V1_%d' % h, V1, [V1.k(t) for t in range(NT)] + [V1.k('ones')], [128, NT, 129])

        acc_loc = [(4 + i // 3, (i % 3) * 129) for i in range(8)]

        def acc_ap(qb, m):
            bk, off = acc_loc[qb * 2 + m]
            return self.banks[bk][:, off:off + 129]

        def sS(j, kt, step):
            i = kt - 4 * j
            c0 = 128 * i if i > 0 else 0
            sa, sb_ = (0, 1) if step % 2 == 0 else (2, 3)

            def mm(e):
                e.matmul(self.banks[sa][:, c0:512], lhsT=QKT.ap[0:64, 1, kt * 128:(kt + 1) * 128],
                         rhs=QKT.ap[0:64, 0, j * 512 + c0:(j + 1) * 512], start=True, stop=True)
                return e.matmul(self.banks[sb_][:, c0:512], lhsT=QKT.ap[64:128, 1, kt * 128:(kt + 1) * 128],
                                rhs=QKT.ap[64:128, 0, j * 512 + c0:(j + 1) * 512], start=True, stop=True)
            rk = [QKT.k(kt)] + [QKT.k(4 * j + q) for q in range(4)]
            op('pe', mm, r=rk, w=[self.ps(sa), self.ps(sb_)])
            ex = exps[step % 3]
            op('act', lambda e: e.activation(out=ex.ap[:, 0, c0:512], in_=self.banks[sa][:, c0:512], func=AF.Exp,
                                             scale=0.125), w=[self.ps(sa), ex.k(0)])
            op('act', lambda e: e.activation(out=ex.ap[:, 1, c0:512], in_=self.banks[sb_][:, c0:512], func=AF.Exp,
                                             scale=0.125), w=[self.ps(sb_), ex.k(1)])
            if i >= 0:
                op('pool', lambda e: e.memset(ex.ap[64:128, :, c0:c0 + 64], 0.0), w=[ex.k(0), ex.k(1)])

        def sV(j, kt, step):
            i = kt - 4 * j
            ex = exps[step % 3]
            qb0 = max(i, 0)

            def mm(e):
                for qb in range(qb0, 4):
                    for m in range(2):
                        idx = qb * 2 + m
                        last = e.matmul(acc_ap(qb, m), lhsT=ex.ap[:, m, qb * 128:(qb + 1) * 128], rhs=V1.ap[:, kt, :],
                                        start=(kt == 0 and idx % 3 == 0), stop=(kt == 4 * j + qb),
                                        skip_group_check=True)
                return last
            op('pe', mm, r=[ex.k(0), ex.k(1), V1.k(kt), V1.k('ones')], w=[self.ps(4), self.ps(5), self.ps(6)])

        def merge_steps(j):
            ab = accS[j % 2]
            av = ab.ap
            flat = av.rearrange("p i c -> p (i c)")
            op('dve', lambda e: e.tensor_copy(out=flat[:, 0:387], in_=self.banks[4][:, 0:387]), w=[self.ps(4), ab.k(0)])
            op('dve', lambda e: e.tensor_copy(out=flat[:, 387:774], in_=self.banks[5][:, 0:387]), w=[self.ps(5), ab.k(1)])
            op('dve', lambda e: e.tensor_copy(out=flat[:, 774:1032], in_=self.banks[6][:, 0:258]), w=[self.ps(6), ab.k(2)])
            ms_ = mst[j % 2]
            steps = []
            for qb in range(4):
                tb = 4 * j + qb
                b = qb % 2

                def M1(qb=qb, tb=tb, b=b):
                    op('dve', lambda e: e.reciprocal(out=rr.ap[:, tb, 0:2], in_=av[:, 2 * qb:2 * qb + 2, 128]),
                       r=[ab.k(0), ab.k(1), ab.k(2)], w=[rr.k(tb, 0)])
                    op('dve', lambda e: e.tensor_tensor(out=rr.ap[:, tb, 2:3], in0=rr.ap[:, tb, 1:2], in1=self.lamc.ap,
                                                        op=ALU.mult), r=[rr.k(tb, 0), self.lamc.k()], w=[rr.k(tb, 1)])
                    op('dve', lambda e: e.tensor_scalar(out=tt[b].ap, in0=av[:, 2 * qb + 1, 0:128],
                                                        scalar1=rr.ap[:, tb, 2:3], scalar2=None, op0=ALU.mult),
                       r=[ab.k(0), ab.k(1), ab.k(2), rr.k(tb, 1)], w=[tt[b].k()])
                    op('dve', lambda e: e.scalar_tensor_tensor(out=oo[b].ap, in0=av[:, 2 * qb, 0:128],
                                                               scalar=rr.ap[:, tb, 0:1], in1=tt[b].ap,
                                                               op0=ALU.mult, op1=ALU.subtract),
                       r=[ab.k(0), ab.k(1), ab.k(2), rr.k(tb, 0), tt[b].k()], w=[oo[b].k()])
                    op('dve', lambda e: e.scalar_tensor_tensor(out=ojk.ap, in0=oo[b].ap, scalar=1.0, in1=oo[b].ap,
                                                               op0=ALU.mult, op1=ALU.mult,
                                                               accum_out=sso.ap[:, tb:tb + 1]),
                       r=[oo[b].k()], w=[ojk.k(), sso.k(tb)])
                    op('pool', lambda e: e.tensor_scalar(out=mso.ap[:, tb:tb + 1], in0=sso.ap[:, tb:tb + 1],
                                                         scalar1=1.0 / 128, scalar2=EPS, op0=ALU.mult, op1=ALU.add),
                       r=[sso.k(tb)], w=[mso.k(tb)])
                    op('pool', lambda e: e.tensor_tensor(out=rso.ap[:, tb:tb + 1], in0=mso.ap[:, tb:tb + 1],
                                                         in1=self.mhalf.ap[:, 0:1], op=ALU.pow),
                       r=[mso.k(tb), self.mhalf.k()], w=[rso.k(tb)])
                    op('dve', lambda e: e.scalar_tensor_tensor(out=on[b].ap, in0=oo[b].ap, scalar=rso.ap[:, tb:tb + 1],
                                                               in1=self.gda_b.ap, op0=ALU.mult, op1=ALU.mult),
                       r=[oo[b].k(), rso.k(tb), self.gda_b.k()], w=[on[b].k()])

                def M2(qb=qb, tb=tb, b=b):
                    def mm(e):
                        for dc in range(8):
                            last = e.matmul(self.banks[7][:, 0:384], lhsT=hm.ap[:, dc, tb * 128:(tb + 1) * 128],
                                            rhs=ws.ap[:, dc, 384:768], start=(dc == 0), stop=(dc == 7))
                        return last
                    op('pe', mm, r=self.k2(hm, tb) + [ws.k()], w=[self.ps(7)])
                    op('act', lambda e: e.activation(out=tg[b].ap, in_=self.banks[7][:, 0:384], func=AF.Tanh, scale=0.5),
                       w=[self.ps(7), tg[b].k()])
                    op('dve', lambda e: e.scalar_tensor_tensor(out=uu[b].ap, in0=tg[b].ap[:, 256:384], scalar=1.0,
                                                               in1=self.banks[7][:, 256:384], op0=ALU.add, op1=ALU.mult),
                       r=[tg[b].k()], w=[self.ps(7), uu[b].k()])

                def M3(qb=qb, tb=tb, b=b):
                    op('pool', lambda e: e.tensor_tensor(out=u2[b].ap, in0=uu[b].ap,
                                                         in1=orr.ap[:, tb, hh * 128:(hh + 1) * 128], op=ALU.mult),
                       r=[uu[b].k(), orr.k(tb)], w=[u2[b].k()])
                    op('dve', lambda e: e.scalar_tensor_tensor(out=u3[b].ap, in0=tg[b].ap[:, 128:256], scalar=1.0,
                                                               in1=u2[b].ap, op0=ALU.add, op1=ALU.mult),
                       r=[tg[b].k(), u2[b].k()], w=[u3[b].k()])
                    op('dve', lambda e: e.scalar_tensor_tensor(out=m1[b].ap, in0=tg[b].ap[:, 0:128], scalar=1.0,
                                                               in1=on[b].ap, op0=ALU.add, op1=ALU.mult),
                       r=[tg[b].k(), on[b].k()], w=[m1[b].k()])
                    op('pool', lambda e: e.tensor_tensor(out=mg[b].ap, in0=m1[b].ap, in1=u3[b].ap, op=ALU.add),
                       r=[m1[b].k(), u3[b].k()], w=[mg[b].k()])
                    pT = self.bank_bf(7, 384, 448)
                    op('pe', lambda e: e.transpose(out=pT, in_=mg[b].ap, identity=self.ident.ap),
                       r=[mg[b].k(), self.ident.k()], w=[self.ps(7)])
                    op('dve', lambda e: e.tensor_copy(out=ms_.ap[:, qb * 128:(qb + 1) * 128], in_=pT),
                       w=[self.ps(7), ms_.k(qb)])
                steps += [M1, M2, M3]

            def M4():
                op('sp', lambda e: e.dma_start(out=self.mT_d[h, :, j * 512:(j + 1) * 512], in_=ms_.ap),
                   r=[ms_.k(q) for q in range(4)], w=[('mT_d', h, j)], dma='mst%d' % (j % 2))
            steps.append(M4)
            return steps

        step = 0
        for j in range(NQ):
            nk = 4 * j + 4
            per = max(1, -(-len(self.deferred) // nk))
            sS(j, 0, step)
            for kt in range(nk):
                if kt + 1 < nk:
                    sS(j, kt + 1, step + 1)
                sV(j, kt, step)
                step += 1
                self.pump(per)
            self.flush()
            self.deferred = merge_steps(j)
        self.flush()
        self.top = mark

    def tail(self):
        A, op, S, dr = self.alloc, self.op, self.S, self.dr
        NTT = S // 512
        self.top = self.persist_top
        G1b = A('G1b', [128, D], F32)
        G2b = A('G2b', [128, D], F32)
        wout = A('wout', [128, 8, D], BF16)
        diag = [A('diag%d' % i, [128, 128], F32) for i in range(2)]
        x1 = [A('x1_%d' % i, [128, 4, D], F32) for i in range(2)]
        mTt = [A('mTt%d' % i, [128, 8, 512], BF16) for i in range(2)]
        hffT = A('hffT', [128, 8, 512], BF16)
        hT = A('hT', [128, 32, 512], BF16)
        xn2 = [A('xn2_%d' % i, [128, D], BF16) for i in range(2)]
        rl = [A('rl%d' % i, [128, 512], F32) for i in range(2)]
        yg = [A('yg%d' % i, [128, D], F32) for i in range(2)]
        wupc = [A('wupc%d' % i, [128, 8, 512], BF16) for i in range(3)]
        wdnc = [A('wdnc%d' % i, [128, 4, 512], BF16) for i in range(4)]
        junk = A('tjunk', [128, D], BF16)
        ss = A('ss2', [128, S // 128], F32)
        ms = A('ms2', [128, S // 128], F32)
        rstd = A('rstd2', [128, S // 128], F32)

        src = dr['w_out'].rearrange("(g p) n -> p g n", p=128)

        def ldw(e):
            return [e.dma_start(out=wout.ap[:, g0:g0 + 4, :], in_=src[:, g0:g0 + 4, :]) for g0 in (0, 4)]
        op('pool', ldw, w=[wout.k()], dma='wout', ndma=2)

        for gi, (G, c0) in enumerate(((G1b, 16), (G2b, 40))):
            for half in range(2):
                for q in range(4):
                    dc = half * 4 + q
                    dg = diag[dc % 2]
                    op('dve', lambda e, dg=dg, dc=dc, c0=c0: e.tensor_scalar(
                        out=dg.ap, in0=self.identf.ap, scalar1=self.modc.ap[:, c0 + dc:c0 + dc + 1], scalar2=None,
                        op0=ALU.mult), r=[self.identf.k(), self.modc.k()], w=[dg.k()])
                    op('pe', lambda e, dg=dg, q=q, half=half: e.matmul(
                        self.banks[half][:, q * 128:(q + 1) * 128], lhsT=self.onesf.ap, rhs=dg.ap, start=True, stop=True),
                       r=[self.onesf.k(), dg.k()], w=[self.ps(half)])
                op('act', lambda e, G=G, half=half: e.activation(out=G.ap[:, half * 512:(half + 1) * 512],
                                                                 in_=self.banks[half][:, 0:512], func=AF.Copy),
                   w=[self.ps(half), G.k(half)])

        sctr = {'up': 0, 'dn': 0}

        def load_tile(i):
            xb = x1[i % 2]
            xsrc = self.x[i * 512:(i + 1) * 512, :].rearrange("(s p) d -> p s d", p=128)
            op('pool', lambda e: e.dma_start(out=xb.ap, in_=xsrc), w=[xb.k(s) for s in range(4)], dma='x1_%d' % (i % 2))
            mt = mTt[i % 2]
            msrc = self.mT_d[:, :, i * 512:(i + 1) * 512].rearrange("g p t -> p g t")
            op('pool', lambda e: e.dma_start(out=mt.ap, in_=msrc), r=[('mT_d', g, i) for g in range(8)],
               w=[mt.k()], dma='mTt%d' % (i % 2))

        def O_steps(i):
            xb = x1[i % 2]
            mt = mTt[i % 2]
            steps = []
            for s in range(4):
                def Oa(s=s):
                    def mm(e):
                        for nb in range(2):
                            for g in range(8):
                                last = e.matmul(self.banks[6 + nb][:, 0:512], lhsT=mt.ap[:, g, s * 128:(s + 1) * 128],
                                                rhs=wout.ap[:, g, nb * 512:(nb + 1) * 512], start=(g == 0), stop=(g == 7))
                        return last
                    op('pe', mm, r=[mt.k(), wout.k()], w=[self.ps(6), self.ps(7)])
                    y = yg[s % 2]
                    for nb in range(2):
                        op('dve', lambda e, nb=nb: e.tensor_tensor(out=y.ap[:, nb * 512:(nb + 1) * 512],
                                                                    in0=self.banks[6 + nb][:, 0:512],
                                                                    in1=G1b.ap[:, nb * 512:(nb + 1) * 512], op=ALU.mult),
                           r=[G1b.k(nb)], w=[self.ps(6 + nb), y.k(nb)])
                    op('pool', lambda e: e.tensor_tensor(out=xb.ap[:, s, :], in0=y.ap, in1=xb.ap[:, s, :], op=ALU.add),
                       r=[y.k(0), y.k(1), xb.k(s)], w=[xb.k(s)])

                def Ob(s=s):
                    t = i * 4 + s
                    self.norm_to_T(xb.ap[:, s, :], xb.k(s), t, (ss, ms, rstd, junk), xn2[s % 2], self.A2,
                                   self.modc.ap[:, 24:32],
                                   lambda lo, hi: hffT.ap[:, lo:hi, s * 128:(s + 1) * 128], self.k2(hffT, s), (6, 7), 'n2')
                steps += [Oa, Ob]
            return steps

        def U(i):
            for c in range(8):
                wb = wupc[sctr['up'] % 3]
                sctr['up'] += 1
                op('sp', lambda e, wb=wb, c=c: e.dma_start(out=wb.ap, in_=self.wup_s[c]), r=[('wup_s',)], w=[wb.k()],
                   dma=wb.name)
                for fl in range(4):
                    fb = 4 * c + fl
                    bk = 4 + fb % 2

                    def mm(e, fl=fl, bk=bk, wb=wb):
                        for dc in range(8):
                            last = e.matmul(self.banks[bk][:, 0:512], lhsT=wb.ap[:, dc, fl * 128:(fl + 1) * 128],
                                            rhs=hffT.ap[:, dc, :], start=(dc == 0), stop=(dc == 7))
                        return last
                    op('pe', mm, r=[wb.k()] + sum([self.k2(hffT, s) for s in range(4)], []), w=[self.ps(bk)])
                    r_ = rl[fb % 2]
                    op('act', lambda e, bk=bk, r_=r_: e.activation(out=r_.ap, in_=self.banks[bk][:, 0:512], func=AF.Relu),
                       w=[self.ps(bk), r_.k()])
                    eng = 'dve' if fb % 2 == 0 else 'pool'
                    op(eng, lambda e, r_=r_, fb=fb: e.tensor_tensor(out=hT.ap[:, fb, :], in0=r_.ap, in1=r_.ap, op=ALU.mult),
                       r=[r_.k()], w=[hT.k(fb)])

        def Dn(i):
            xb = x1[i % 2]
            for nb in range(2):
                for c in range(8):
                    wb = wdnc[sctr['dn'] % 4]
                    sctr['dn'] += 1
                    op('sp', lambda e, wb=wb, c=c, nb=nb: e.dma_start(out=wb.ap, in_=self.wdn_s[nb, c]),
                       r=[('wdn_s',)], w=[wb.k()], dma=wb.name)

                    def mm(e, wb=wb, c=c):
                        for s in range(4):
                            for fl in range(4):
                                last = e.matmul(self.banks[s][:, 0:512], lhsT=hT.ap[:, 4 * c + fl, s * 128:(s + 1) * 128],
                                                rhs=wb.ap[:, fl, :], start=(c == 0 and fl == 0), stop=(c == 7 and fl == 3))
                        return last
                    op('pe', mm, r=[wb.k()] + [hT.k(4 * c + fl) for fl in range(4)], w=[self.ps(s) for s in range(4)])
                    self.pump(1)
                for s in range(4):
                    y = yg[s % 2]
                    op('dve', lambda e, s=s, y=y, nb=nb: e.tensor_tensor(out=y.ap[:, 0:512], in0=self.banks[s][:, 0:512],
                                                                         in1=G2b.ap[:, nb * 512:(nb + 1) * 512], op=ALU.mult),
                       r=[G2b.k(nb)], w=[self.ps(s), y.k(0)])
                    op('pool', lambda e, s=s, y=y, nb=nb: e.tensor_tensor(
                        out=xb.ap[:, s, nb * 512:(nb + 1) * 512], in0=y.ap[:, 0:512],
                        in1=xb.ap[:, s, nb * 512:(nb + 1) * 512], op=ALU.add), r=[y.k(0), xb.k(s)], w=[xb.k(s)])
            dst = self.out[i * 512:(i + 1) * 512, :].rearrange("(s p) d -> p s d", p=128)
            op('pool', lambda e: e.dma_start(out=dst, in_=xb.ap), r=[xb.k(s) for s in range(4)], w=[('out', i)],
               dma='ost%d' % (i % 2))
            self.outkeys.append(('out', i))

        load_tile(0)
        for f in O_steps(0):
            f()
        for i in range(NTT):
            if i + 1 < NTT:
                load_tile(i + 1)
                self.deferred = O_steps(i + 1)
            U(i)
            Dn(i)
            self.flush()

    def build(self, upto='all'):
        self.setup()
        A = self.alloc
        self.hmixT = A('hmixT', [128, 8, self.S], BF16)
        self.orr = A('orr', [128, self.NT, 256], BF16)
        self.ropeA = A('ropeA', [128, self.NT, 2, 64], F32)
        self.wslots = [A('wslot0', [128, 8, 768], BF16), A('wslot1', [128, 8, 768], BF16)]
        self.mixer_top = self.top
        self.bg = []
        phases = []
        for r in range(4):
            phases += [('ret', r), ('da', 2 * r), ('da', 2 * r + 1)]
        if upto == 'ret0':
            phases = phases[:1]
        elif upto == 'da0':
            phases = phases[:2]

        def wblocks(ph):
            kind, i = ph
            if kind == 'ret':
                return [(C_QR + i * 128, 128), (C_KR + i * 128, 128), (C_VR + i * 256, 256)]
            return [(C_QA + i * 128, 128), (C_KA + i * 128, 128), (C_VA + i * 128, 128),
                    (C_GA + i * 128, 128), (C_GB + i * 128, 128), (C_GR + i * 128, 128)]
        self.phase_mod()
        self.load_w(0, wblocks(phases[0]))
        self.op('sp', lambda e: e.dma_start(out=self.ropeA.ap, in_=self.dr['ropeA']), w=[self.ropeA.k()], dma='ropeA')
        self.phase_norm1()
        self.top = self.mixer_top
        if upto == 'norm1':
            return self.finish()
        self.ffn_scratch()
        for n, ph in enumerate(phases):
            if n + 1 < len(phases):
                self.load_w((n + 1) % 2, wblocks(phases[n + 1]))
            if ph[0] == 'ret':
                self.ret_head(ph[1], n % 2)
            else:
                self.da_head(ph[1], n % 2)
        if upto != 'all':
            return self.finish()
        self.tail()
        return self.finish()

    def finish(self):
        self.flush()
        keys = list(self.outkeys)
        self.op('sp', lambda e: e.nop(), r=keys + [('wup_s',), ('wdn_s',)], sig=False)
        self.sch.run_block()


def build_program(S=4096, dbg=(), upto='all'):
    nc = bass.Bass("TRN2", target_bir_lowering=False)
    with ExitStack() as es:
        kb = KB(nc, es, S, dbg)
        kb.build(upto)
    return nc


def host_tables(S):
    NT = S // 128
    pos = np.arange(S, dtype=np.float32)
    f32 = np.float32
    invA = (10000.0 ** (-np.arange(0, 64, 2, dtype=f32) / f32(64))).astype(f32)
    angA = (pos[:, None] * invA[None, :]).astype(f32).astype(np.float64)
    cA, sA = np.cos(angA).astype(f32), np.sin(angA).astype(f32)
    ropeA = np.stack([np.concatenate([cA, cA], 1), np.concatenate([-sA, sA], 1)], 1)
    ropeA = np.ascontiguousarray(ropeA.reshape(NT, 128, 2, 64).transpose(1, 0, 2, 3))
    invR = (1.0 / (f32(10000.0) ** np.linspace(0.0, 1.0, 64, dtype=f32))).astype(f32)
    angR = (pos[:, None] * invR[None, :]).astype(f32).astype(np.float64)
    cR, sR = np.cos(angR).astype(f32), np.sin(angR).astype(f32)
    ropeR = np.stack([np.concatenate([cR, cR], 1), np.concatenate([-sR, sR], 1)], 1)
    ropeR = np.ascontiguousarray(ropeR.reshape(NT, 128, 2, 128).transpose(1, 0, 2, 3))
    n = np.arange(128, dtype=np.float64)
    retc = np.zeros((128, 8), f32)
    for r in range(4):
        lg = np.log(1.0 - 2.0 ** (-5.0 - r))
        retc[:, 2 * r] = np.exp((n + 1.0) * lg)
        retc[:, 2 * r + 1] = (128.0 ** -0.5) * np.exp(-(n + 1.0) * lg)
    jj, ii = np.meshgrid(np.arange(128), np.arange(128), indexing='ij')
    retmask = (ii >= jj).astype(f32)
    return dict(ropeA=ropeA, ropeR=ropeR, retc=retc, retmask=retmask, identf=np.eye(128, dtype=f32))


def core_inputs(b, S, inp, tabs):
    f = lambda a: np.ascontiguousarray(np.asarray(a, dtype=np.float32))
    col = lambda v, n: f(np.asarray(v, np.float32).reshape(n, 128).T)
    m = dict(tabs)
    m['x'] = f(inp['x'][b, :S])
    m['c_col'] = col(inp['c'][b], 8)
    m['w_ada'] = f(inp['w_ada'][0])
    m['b_ada_c'] = col(inp['b_ada'][0], 48)
    m['g1c'] = col(inp['g_norm1'][0], 8)
    m['g2c'] = col(inp['g_norm2'][0], 8)
    m['w_in'] = f(inp['w_in'][0])
    gq, gk = np.asarray(inp['g_q'][0], np.float32), np.asarray(inp['g_k'][0], np.float32)
    m['gqk_row'] = f(np.concatenate([gq, gq, gk, gk])[None, :])
    m['lam_row'] = f(np.concatenate([inp['lambda_q1'][0], inp['lambda_q2'][0],
                                     inp['lambda_k1'][0], inp['lambda_k2'][0]])[None, :])
    m['gda_row'] = f(np.asarray(inp['g_da_out'][0])[None, :])
    m['gret_row'] = f(np.asarray(inp['g_ret_out'][0])[None, :])
    m['w_out'] = f(inp['w_out'][0])
    m['w_up'] = f(inp['w_up'][0])
    m['w_down'] = f(inp['w_down'][0])
    return m


def kernel(**inputs):
    S = 4096
    inp = {k: np.asarray(v) for k, v in inputs.items()}
    B = inp['x'].shape[0]
    tabs = host_tables(S)
    nc = build_program(S)
    in_maps = [core_inputs(b, S, inp, tabs) for b in range(B)]
    res = run_bass_kernel_spmd(nc, in_maps, core_ids=list(range(B)))
    out = np.stack([np.asarray(res.results[b]['out']) for b in range(B)], 0)
    return out.astype(np.float32)
```

```python
import math
from contextlib import ExitStack

import numpy as np
import concourse.bass as bass
import concourse.mybir as mybir
from concourse.bass_utils import run_bass_kernel_spmd

F32 = mybir.dt.float32
BF16 = mybir.dt.bfloat16
AF = mybir.ActivationFunctionType
ALU = mybir.AluOpType
AX = mybir.AxisListType

D = 1024
DFF = 4096
EPS = 1e-6
LAMBDA_INIT = 0.2
C_QA, C_KA, C_VA, C_QR, C_KR, C_VR, C_GR, C_GA, C_GB = 0, 1024, 2048, 3072, 3584, 4096, 5120, 6144, 7168
ARENA_BYTES = 212736


class Sched:
    CE = ('pe', 'act', 'dve', 'pool')

    def __init__(self, nc, es):
        self.nc, self.es = nc, es
        self.q = {e: [] for e in ('pe', 'act', 'dve', 'pool', 'sp')}
        self.sem = {e: es.enter_context(nc.semaphore('s_' + e)) for e in self.CE}
        self.cnt = {e: 0 for e in self.CE}
        self.lastw, self.reads = {}, {}
        self.dsem = {}
        self.waited = {e: {} for e in self.q}
        self.bufs, self.bufkeys, self.inherit, self.known = [], {}, {}, set()

    @staticmethod
    def _merge(d, ev):
        k = id(ev[0])
        if k not in d or d[k][1] < ev[1]:
            d[k] = ev

    def new_buffer(self, name, lo, hi):
        inh = {}
        keep = []
        for (lo2, hi2, n2) in self.bufs:
            if lo2 < hi and lo < hi2:
                for k in self.bufkeys.get(n2, ()):
                    ev = self.lastw.get(k)
                    if ev is not None:
                        self._merge(inh, ev)
                    for ev in self.reads.get(k, {}).values():
                        self._merge(inh, ev)
                for ev in self.inherit.get(n2, {}).values():
                    self._merge(inh, ev)
                if lo <= lo2 and hi2 <= hi:
                    continue
            keep.append((lo2, hi2, n2))
        keep.append((lo, hi, name))
        self.bufs = keep
        self.inherit[name] = inh

    def _touch(self, k):
        if k in self.known:
            return
        self.known.add(k)
        name = k[0] if isinstance(k, tuple) else k
        self.bufkeys.setdefault(name, set()).add(k)
        self.reads[k] = dict(self.inherit.get(name, {}))

    def op(self, eng, fn, r=(), w=(), dma=None, ndma=1, sig=True):
        waits = {}

        def need(ev, raw):
            if ev is None:
                return
            sem, val, src = ev
            if src == eng and (not raw or eng == 'pe'):
                return
            k = id(sem)
            if self.waited[eng].get(k, 0) >= val:
                return
            if k not in waits or waits[k][1] < val:
                waits[k] = (sem, val)

        for k in r:
            self._touch(k)
            need(self.lastw.get(k), True)
        for k in w:
            self._touch(k)
            need(self.lastw.get(k), False)
            for ev in self.reads[k].values():
                need(ev, False)
        for k, (sem, val) in waits.items():
            self.waited[eng][k] = val
        if dma is not None:
            if dma not in self.dsem:
                self.dsem[dma] = [self.es.enter_context(self.nc.semaphore('d_' + dma)), 0]
            ds = self.dsem[dma]
            ds[1] += 16 * ndma
            ev = (ds[0], ds[1], 'dma')
            evh = ('dma', ds[0], ndma)
        elif sig:
            self.cnt[eng] += 1
            ev = (self.sem[eng], self.cnt[eng], eng)
            evh = ('ce', self.sem[eng])
        else:
            ev, evh = None, ('none',)
        self.q[eng].append((list(waits.values()), fn, evh))
        for k in r:
            if ev is not None:
                self._merge(self.reads[k], ev)
        for k in w:
            self.lastw[k] = ev
            self.reads[k] = {}
        return ev

    def emit(self, eng, e):
        for waits, fn, evh in self.q[eng]:
            for sem, val in waits:
                e.wait_ge(sem, val)
            ins = fn(e)
            if evh[0] == 'ce':
                ins.then_inc(evh[1], 1)
            elif evh[0] == 'dma':
                if not isinstance(ins, (list, tuple)):
                    ins = [ins]
                assert len(ins) == evh[2], (len(ins), evh[2])
                for i in ins:
                    i.then_inc(evh[1], 16)

    def run_block(self):
        with self.nc.Block() as block:
            @block.tensor
            def _(e):
                self.emit('pe', e)

            @block.scalar
            def _(e):
                self.emit('act', e)

            @block.vector
            def _(e):
                self.emit('dve', e)

            @block.gpsimd
            def _(e):
                self.emit('pool', e)

            @block.sync
            def _(e):
                self.emit('sp', e)


class Buf:
    def __init__(self, name, ap):
        self.name, self.ap = name, ap

    def k(self, *sub):
        return (self.name,) + sub


class KB:
    def __init__(self, nc, es, S, dbg=()):
        self.nc, self.es, self.S, self.dbg = nc, es, S, set(dbg)
        self.NT = S // 128
        self.sch = Sched(nc, es)
        self.arena = es.enter_context(nc.sbuf_tensor('arena', [128, ARENA_BYTES // 4], F32))
        self.top = 0
        self.nalloc = 0
        self.pbig = es.enter_context(nc.psum_tensor('pbig', [128, 8 * 512], F32))
        self.banks = [self.pbig[:, i * 512:(i + 1) * 512] for i in range(8)]
        self.outkeys = []
        self.deferred = []
        self.dr = {}

    def alloc(self, name, shape, dtype):
        esz = 4 if dtype == F32 else 2
        n = int(np.prod(shape[1:]))
        nbytes = (n * esz + 63) // 64 * 64
        lo, hi = self.top, self.top + nbytes
        assert hi <= ARENA_BYTES, ('SBUF arena overflow', name, hi)
        self.top = hi
        self.nalloc += 1
        uname = '%s#%d' % (name, self.nalloc)
        self.sch.new_buffer(uname, lo, hi)
        v = self.arena[:, lo // 4: hi // 4]
        if dtype != F32:
            v = v.bitcast(dtype)
        v = v[0:shape[0], 0:n]
        if len(shape) == 3:
            v = v.rearrange("p (a b) -> p a b", b=shape[2])
        elif len(shape) == 4:
            v = v.rearrange("p (a b c) -> p a b c", b=shape[2], c=shape[3])
        return Buf(uname, v)

    def ps(self, i):
        return ('ps', i)

    def bank_bf(self, i, lo, hi):
        return self.banks[i][:, lo:hi].bitcast(BF16)

    def op(self, *a, **k):
        return self.sch.op(*a, **k)

    def dram_in(self, name, shape, dtype=F32):
        t = self.nc.dram_tensor(name, list(shape), dtype, kind="ExternalInput").ap()
        self.dr[name] = t
        return t

    def dump(self, name, buf, keys, shape, src=None):
        if name not in self.dbg:
            return
        src = buf.ap if src is None else src
        t = self.nc.dram_tensor('dbg_' + name, list(shape), src.dtype, kind="ExternalOutput").ap()
        self.op('sp', lambda e: e.dma_start(out=t, in_=src), r=list(keys), w=[('dbgout', name)], dma='dbg_' + name)
        self.outkeys.append(('dbgout', name))

    @staticmethod
    def k2(buf, t):
        return [buf.k(t, 0), buf.k(t, 1)]

    def pump(self, n=1):
        for _ in range(n):
            if not self.deferred:
                return
            self.deferred.pop(0)()

    def flush(self):
        while self.deferred:
            self.deferred.pop(0)()

    def setup(self):
        nc, S, NT = self.nc, self.S, self.NT
        di = self.dram_in
        self.x = di('x', [S, D])
        di('c_col', [128, 8]); di('w_ada', [D, 6 * D]); di('b_ada_c', [128, 48])
        di('g1c', [128, 8]); di('g2c', [128, 8]); di('w_in', [D, 8192])
        di('gqk_row', [1, 256]); di('lam_row', [1, 256]); di('gda_row', [1, 128]); di('gret_row', [1, 256])
        di('w_out', [D, D]); di('w_up', [D, DFF]); di('w_down', [DFF, D])
        di('ropeA', [128, NT, 2, 64]); di('ropeR', [128, NT, 2, 128])
        di('retmask', [128, 128]); di('retc', [128, 8]); di('identf', [128, 128])
        self.out = nc.dram_tensor('out', [S, D], F32, kind="ExternalOutput").ap()
        self.mT_d = nc.dram_tensor('mT_d', [8, 128, S], BF16, kind="Internal").ap()
        self.wup_s = nc.dram_tensor('wup_s', [8, 128, 8, 512], BF16, kind="Internal").ap()
        self.wdn_s = nc.dram_tensor('wdn_s', [2, 8, 128, 4, 512], BF16, kind="Internal").ap()

        A = self.alloc
        dr = self.dr
        self.c_col = A('c_col', [128, 8], F32)
        self.b_ada_c = A('b_ada_c', [128, 48], F32)
        self.g1c = A('g1c', [128, 8], F32)
        self.g2c = A('g2c', [128, 8], F32)
        self.identf = A('identf', [128, 128], F32)
        self.ident = A('ident', [128, 128], BF16)
        self.onesf = A('onesf', [128, 128], F32)
        self.mhalf = A('mhalf', [128, 16], F32)
        self.modc = A('modc', [128, 48], F32)
        self.A1 = A('A1', [128, 8], F32)
        self.A2 = A('A2', [128, 8], F32)
        self.retmask = A('retmask', [128, 128], F32)
        self.retc = A('retc', [128, 8], F32)
        self.gqk_b = A('gqk_b', [128, 256], F32)
        self.lam_b = A('lam_b', [128, 256], F32)
        self.gda_b = A('gda_b', [128, 128], F32)
        self.gret_b = A('gret_b', [128, 256], F32)
        self.lamc = A('lamc', [128, 1], F32)
        self.lamt = A('lamt', [128, 4], F32)
        self.lamp = A('lamp', [128, 128], F32)

        loads = [(self.c_col, dr['c_col']), (self.b_ada_c, dr['b_ada_c']), (self.g1c, dr['g1c']),
                 (self.g2c, dr['g2c']), (self.identf, dr['identf']), (self.retmask, dr['retmask']),
                 (self.retc, dr['retc']),
                 (self.gqk_b, dr['gqk_row'][0:1, :].broadcast_to([128, 256])),
                 (self.lam_b, dr['lam_row'][0:1, :].broadcast_to([128, 256])),
                 (self.gda_b, dr['gda_row'][0:1, :].broadcast_to([128, 128])),
                 (self.gret_b, dr['gret_row'][0:1, :].broadcast_to([128, 256]))]

        def ld(e):
            return [e.dma_start(out=b.ap, in_=src) for b, src in loads]
        self.op('sp', ld, w=[b.k() for b, _ in loads], dma='const', ndma=len(loads))
        self.op('dve', lambda e: e.tensor_copy(out=self.ident.ap, in_=self.identf.ap),
                r=[self.identf.k()], w=[self.ident.k()])
        self.op('pool', lambda e: e.memset(self.onesf.ap, 1.0), w=[self.onesf.k()])
        self.op('pool', lambda e: e.memset(self.mhalf.ap, -0.5), w=[self.mhalf.k()])
        self.op('dve', lambda e: e.tensor_scalar(out=self.gda_b.ap, in0=self.gda_b.ap,
                                                 scalar1=(1.0 - LAMBDA_INIT) * 0.5, scalar2=None, op0=ALU.mult),
                r=[self.gda_b.k()], w=[self.gda_b.k()])
        self.op('dve', lambda e: e.tensor_scalar(out=self.gret_b.ap, in0=self.gret_b.ap,
                                                 scalar1=0.25, scalar2=None, op0=ALU.mult),
                r=[self.gret_b.k()], w=[self.gret_b.k()])
        lb, lp, lt = self.lam_b, self.lamp, self.lamt
        self.op('dve', lambda e: e.tensor_tensor(out=lp.ap, in0=lb.ap[:, 0:128], in1=lb.ap[:, 128:256], op=ALU.mult),
                r=[lb.k()], w=[lp.k()])
        self.op('dve', lambda e: e.tensor_reduce(out=lt.ap[:, 0:2], in_=lp.ap.rearrange("p (a b) -> p a b", b=64),
                                                 axis=AX.X, op=ALU.add), r=[lp.k()], w=[lt.k(0)])
        self.op('act', lambda e: e.activation(out=lt.ap[:, 2:4], in_=lt.ap[:, 0:2], func=AF.Exp),
                r=[lt.k(0)], w=[lt.k(1)])
        self.op('dve', lambda e: e.tensor_tensor(out=self.lamc.ap, in0=lt.ap[:, 2:3], in1=lt.ap[:, 3:4],
                                                 op=ALU.subtract), r=[lt.k(1)], w=[self.lamc.k()])
        self.op('dve', lambda e: e.tensor_scalar(out=self.lamc.ap, in0=self.lamc.ap, scalar1=LAMBDA_INIT,
                                                 scalar2=None, op0=ALU.add), r=[self.lamc.k()], w=[self.lamc.k()])
        self.persist_top = self.top

    def ffn_scratch(self):
        dr = self.dr
        for c in range(8):
            def up(c=c):
                src = dr['w_up'][:, c * 512:(c + 1) * 512].rearrange("(dc p) f -> p dc f", p=128)
                self.op('pool', lambda e: e.dma_start(out=self.wup_s[c], in_=src), w=[('wup_s',)], dma='wup_s')
            self.bg.append(up)
        for nb in range(2):
            for c in range(8):
                def dn(nb=nb, c=c):
                    src = dr['w_down'][c * 512:(c + 1) * 512, nb * 512:(nb + 1) * 512].rearrange(
                        "(fb p) n -> p fb n", p=128)
                    self.op('pool', lambda e: e.dma_start(out=self.wdn_s[nb, c], in_=src), w=[('wdn_s',)], dma='wdn_s')
                self.bg.append(dn)

    def phase_mod(self):
        A, op, dr = self.alloc, self.op, self.dr
        th = A('th', [128, 8], F32)
        scf = A('scf', [128, 8], F32)
        scb = A('scb', [128, 8], BF16)
        was = [A('wa0', [128, 8, 1024], BF16), A('wa1', [128, 8, 1024], BF16)]
        cc = self.c_col
        op('act', lambda e: e.activation(out=th.ap, in_=cc.ap, func=AF.Tanh, scale=0.5), r=[cc.k()], w=[th.k()])
        op('dve', lambda e: e.scalar_tensor_tensor(out=scf.ap, in0=th.ap, scalar=1.0, in1=cc.ap,
                                                   op0=ALU.add, op1=ALU.mult), r=[th.k(), cc.k()], w=[scf.k()])
        op('dve', lambda e: e.tensor_scalar(out=scb.ap, in0=scf.ap, scalar1=0.5, scalar2=None, op0=ALU.mult),
           r=[scf.k()], w=[scb.k()])
        bank = self.banks[7]
        for v in range(6):
            wa = was[v % 2]
            src = dr['w_ada'][:, v * 1024:(v + 1) * 1024].rearrange("(kc p) n -> p kc n", p=128)
            op('pool', lambda e, wa=wa, src=src: e.dma_start(out=wa.ap, in_=src), w=[wa.k()], dma='wa%d' % (v % 2))

            def mm(e, v=v, wa=wa):
                for j in range(8):
                    for kc in range(8):
                        last = e.matmul(bank[:, v * 8 + j: v * 8 + j + 1], lhsT=wa.ap[:, kc, j * 128:(j + 1) * 128],
                                        rhs=scb.ap[:, kc:kc + 1], start=(kc == 0), stop=(kc == 7))
                return last
            op('pe', mm, r=[wa.k(), scb.k()], w=[self.ps(7)])
        mc = self.modc
        op('dve', lambda e: e.tensor_tensor(out=mc.ap, in0=bank[:, 0:48], in1=self.b_ada_c.ap, op=ALU.add),
           r=[self.b_ada_c.k()], w=[self.ps(7), mc.k()])
        op('dve', lambda e: e.scalar_tensor_tensor(out=self.A1.ap, in0=mc.ap[:, 8:16], scalar=1.0, in1=self.g1c.ap,
                                                   op0=ALU.add, op1=ALU.mult), r=[mc.k(), self.g1c.k()], w=[self.A1.k()])
        op('dve', lambda e: e.scalar_tensor_tensor(out=self.A2.ap, in0=mc.ap[:, 32:40], scalar=1.0, in1=self.g2c.ap,
                                                   op0=ALU.add, op1=ALU.mult), r=[mc.k(), self.g2c.k()], w=[self.A2.k()])
        self.dump('modc', mc, [mc.k()], [128, 48])

    def norm_to_T(self, xt_ap, xkey, t, stats, xn, A_, Bcol, dst_fn, dst_key, banks, tag):
        op = self.op
        ss, ms, rstd, junk = stats
        op('act', lambda e: e.activation(out=junk.ap, in_=xt_ap, func=AF.Square, accum_out=ss.ap[:, t:t + 1]),
           r=[xkey], w=[junk.k(), ss.k(t)])
        op('pool', lambda e: e.tensor_scalar(out=ms.ap[:, t:t + 1], in0=ss.ap[:, t:t + 1], scalar1=1.0 / D,
                                             scalar2=EPS, op0=ALU.mult, op1=ALU.add), r=[ss.k(t)], w=[ms.k(t)])
        op('pool', lambda e: e.tensor_tensor(out=rstd.ap[:, t:t + 1], in0=ms.ap[:, t:t + 1],
                                             in1=self.mhalf.ap[:, 0:1], op=ALU.pow),
           r=[ms.k(t), self.mhalf.k()], w=[rstd.k(t)])
        op('dve', lambda e: e.tensor_scalar(out=xn.ap, in0=xt_ap, scalar1=rstd.ap[:, t:t + 1], scalar2=None,
                                            op0=ALU.mult), r=[xkey, rstd.k(t)], w=[xn.k()])
        ba, bb = banks
        pa = self.bank_bf(ba, 0, 256).rearrange("p (a b) -> p a b", b=128)
        pb = self.bank_bf(bb, 0, 256).rearrange("p (a b) -> p a b", b=128)

        def tr(e):
            for dc in range(8):
                pt = pa if dc < 4 else pb
                last = e.transpose(out=pt[:, dc % 4, :], in_=xn.ap[:, dc * 128:(dc + 1) * 128], identity=self.ident.ap)
            return last
        op('pe', tr, r=[xn.k(), self.ident.k()], w=[self.ps(ba), self.ps(bb)])
        da = dst_fn(0, 4)
        db = dst_fn(4, 8)

        def ev_act(e):
            for dc in range(4):
                last = e.activation(out=da[:, dc, :], in_=pa[:, dc, :], func=AF.Identity,
                                    scale=A_.ap[:, dc:dc + 1], bias=Bcol[:, dc:dc + 1])
            return last
        op('act', ev_act, r=[A_.k(), self.modc.k()], w=[self.ps(ba), dst_key[0]])

        def ev_dve(e):
            for dc in range(4, 8):
                last = e.tensor_scalar(out=db[:, dc - 4, :], in0=pb[:, dc - 4, :], scalar1=A_.ap[:, dc:dc + 1],
                                       scalar2=Bcol[:, dc:dc + 1], op0=ALU.mult, op1=ALU.add)
            return last
        op('dve', ev_dve, r=[A_.k(), self.modc.k()], w=[self.ps(bb), dst_key[1]])

    def phase_norm1(self):
        A, op, NT = self.alloc, self.op, self.NT
        NG = NT // 4
        xg = [A('xg%d' % i, [128, 4, D], F32) for i in range(2)]
        xns = [A('xn%d' % i, [128, D], BF16) for i in range(3)]
        junk = A('junk', [128, D], BF16)
        ss = A('ss1', [128, NT], F32)
        ms = A('ms1', [128, NT], F32)
        rstd = A('rstd1', [128, NT], F32)
        hm = self.hmixT
        A_, Bcol = self.A1, self.modc.ap[:, 0:8]

        def sa(t):
            g, k = divmod(t, 4)
            xb = xg[g % 2]
            op('sp', lambda e: e.dma_start(out=xb.ap[:, k, :], in_=self.x[t * 128:(t + 1) * 128, :]),
               w=[xb.k(k)], dma='xg%d_%d' % (g % 2, k))
            op('act', lambda e: e.activation(out=junk.ap, in_=xb.ap[:, k, :], func=AF.Square,
                                             accum_out=ss.ap[:, t:t + 1]), r=[xb.k(k)], w=[junk.k(), ss.k(t)])

        def sb(g):
            sl = slice(4 * g, 4 * g + 4)
            op('pool', lambda e: e.tensor_scalar(out=ms.ap[:, sl], in0=ss.ap[:, sl], scalar1=1.0 / D, scalar2=EPS,
                                                 op0=ALU.mult, op1=ALU.add),
               r=[ss.k(t) for t in range(4 * g, 4 * g + 4)], w=[ms.k(g)])
            op('pool', lambda e: e.tensor_tensor(out=rstd.ap[:, sl], in0=ms.ap[:, sl], in1=self.mhalf.ap[:, 0:4],
                                                 op=ALU.pow), r=[ms.k(g), self.mhalf.k()], w=[rstd.k(g)])

        def sc(t):
            g, k = divmod(t, 4)
            xb = xg[g % 2]
            xn = xns[t % 3]
            op('dve', lambda e: e.tensor_scalar(out=xn.ap, in0=xb.ap[:, k, :], scalar1=rstd.ap[:, t:t + 1], scalar2=None,
                                                op0=ALU.mult), r=[xb.k(k), rstd.k(g)], w=[xn.k()])
            ba, bb = (0, 1) if t % 2 == 0 else (2, 3)
            pa = self.bank_bf(ba, 0, 256).rearrange("p (a b) -> p a b", b=128)
            pb = self.bank_bf(bb, 0, 256).rearrange("p (a b) -> p a b", b=128)

            def tr(e):
                for dc in range(8):
                    pt = pa if dc < 4 else pb
                    last = e.transpose(out=pt[:, dc % 4, :], in_=xn.ap[:, dc * 128:(dc + 1) * 128], identity=self.ident.ap)
                return last
            op('pe', tr, r=[xn.k(), self.ident.k()], w=[self.ps(ba), self.ps(bb)])

            def ev_act(e):
                for dc in range(4):
                    last = e.activation(out=hm.ap[:, dc, t * 128:(t + 1) * 128], in_=pa[:, dc, :], func=AF.Identity,
                                        scale=A_.ap[:, dc:dc + 1], bias=Bcol[:, dc:dc + 1])
                return last
            op('act', ev_act, r=[A_.k(), self.modc.k()], w=[self.ps(ba), hm.k(t, 0)])

            def ev_dve(e):
                for dc in range(4, 8):
                    last = e.tensor_scalar(out=hm.ap[:, dc, t * 128:(t + 1) * 128], in0=pb[:, dc - 4, :],
                                           scalar1=A_.ap[:, dc:dc + 1], scalar2=Bcol[:, dc:dc + 1],
                                           op0=ALU.mult, op1=ALU.add)
                return last
            op('dve', ev_dve, r=[A_.k(), self.modc.k()], w=[self.ps(bb), hm.k(t, 1)])

        for i in range(NT + 6):
            if i < NT:
                sa(i)
                if i % 4 == 3:
                    sb(i // 4)
            if 0 <= i - 6 < NT:
                sc(i - 6)
        self.dump('hmixT', hm, sum([self.k2(hm, t) for t in range(NT)], []), [128, 8, self.S])

    def load_w(self, slot, blocks):
        dr = self.dr
        off = 0
        srcs = []
        for c0, n in blocks:
            srcs.append((off, n, dr['w_in'][:, c0:c0 + n].rearrange("(dc p) n -> p dc n", p=128)))
            off += n
        ws = self.wslots[slot]

        def ld(e):
            return [e.dma_start(out=ws.ap[:, :, o:o + n], in_=src) for o, n, src in srcs]
        self.op('pool', ld, w=[ws.k()], dma='wslot%d' % slot, ndma=len(srcs))

    def ret_head(self, r, slot):
        A, op, NT, S = self.alloc, self.op, self.NT, self.S
        mark = self.top
        ws = self.wslots[slot]
        hm = self.hmixT
        ropeR = A('ropeR', [128, NT, 2, 128], F32)
        npc = 4 if NT >= 4 else 1
        cpp = NT // npc
        for pc in range(npc):
            op('sp', lambda e, pc=pc: e.dma_start(out=ropeR.ap[:, pc * cpp:(pc + 1) * cpp],
                                                  in_=self.dr['ropeR'][:, pc * cpp:(pc + 1) * cpp]),
               w=[ropeR.k(pc)], dma='ropeR%d' % pc)
        DP = 6
        tq = [[A('tq%d_%d' % (i, j), [128, 128], F32) for j in range(2)] for i in range(2)]
        tk = [[A('tk%d_%d' % (i, j), [128, 128], F32) for j in range(2)] for i in range(2)]
        qt = [A('qt%d' % i, [128, 128], BF16) for i in range(3)]
        kt = [A('kt%d' % i, [128, 128], BF16) for i in range(3)]
        kp = [A('kp%d' % i, [128, 128], BF16) for i in range(DP)]
        vb = [A('vb%d' % i, [128, 256], BF16) for i in range(DP)]
        qkT = [A('qkT%d' % i, [128, 256], BF16) for i in range(DP)]
        scm = [A('scm%d' % i, [128, 128], BF16) for i in range(DP)]
        St = A('St', [128, 256], F32)
        Sbf = [A('Sbf%d' % i, [128, 256], BF16) for i in range(4)]
        junk = A('rjunk', [128, 256], BF16)
        ssr = A('ssr', [128, NT], F32)
        msr = A('msr', [128, NT], F32)
        rsr = A('rsr', [128, NT], F32)
        g = 1.0 - 2.0 ** (-5.0 - r)
        gC = float(np.exp(128.0 * np.log(g)))
        idec = self.retc.ap[:, 2 * r:2 * r + 1]
        kdec = self.retc.ap[:, 2 * r + 1:2 * r + 2]
        orr = self.orr
        mask = self.retmask

        def swapped(ap):
            a = [list(x) for x in ap.ap]
            return bass.AP(ap.tensor, ap.offset + 64, [a[0], [-64, 2], [1, 64]])

        def st1(c):
            pb = c % 2

            def mm(e):
                for dc in range(8):
                    last = e.matmul(self.banks[pb][:, 0:512], lhsT=hm.ap[:, dc, c * 128:(c + 1) * 128],
                                    rhs=ws.ap[:, dc, 0:512], start=(dc == 0), stop=(dc == 7))
                return last
            op('pe', mm, r=self.k2(hm, c) + [ws.k()], w=[self.ps(pb)])

        def st2a(c):
            pb = c % 2
            b = c % 2
            P = self.banks[pb]
            rk = ropeR.k(c // cpp)
            cc = ropeR.ap[:, c, 0, :]
            sn = ropeR.ap[:, c, 1, :]
            h3 = lambda ap: ap.rearrange("p (h d) -> p h d", h=2)
            for (src, dec, tt, dst) in ((P[:, 0:128], idec, tq[b], qt[c % 3]), (P[:, 128:256], kdec, tk[b], kt[c % 3])):
                op('dve', lambda e, src=src, dec=dec, tt=tt: e.scalar_tensor_tensor(
                    out=tt[0].ap, in0=src, scalar=dec, in1=cc, op0=ALU.mult, op1=ALU.mult),
                   r=[rk, self.retc.k()], w=[self.ps(pb), tt[0].k()])
                op('dve', lambda e, src=src, dec=dec, tt=tt: e.scalar_tensor_tensor(
                    out=h3(tt[1].ap), in0=swapped(src), scalar=dec, in1=h3(sn), op0=ALU.mult, op1=ALU.mult),
                   r=[rk, self.retc.k()], w=[self.ps(pb), tt[1].k()])
                op('pool', lambda e, tt=tt, dst=dst: e.tensor_tensor(out=dst.ap, in0=tt[0].ap, in1=tt[1].ap, op=ALU.add),
                   r=[tt[0].k(), tt[1].k()], w=[dst.k()])
            v_, kp_, kt_ = vb[c % DP], kp[c % DP], kt[c % 3]
            op('act', lambda e: e.activation(out=v_.ap, in_=P[:, 256:512], func=AF.Copy), w=[self.ps(pb), v_.k()])
            op('act', lambda e: e.activation(out=kp_.ap, in_=kt_.ap, func=AF.Copy, scale=gC), r=[kt_.k()], w=[kp_.k()])

        def st2b(c):
            q_, k_, T_ = qt[c % 3], kt[c % 3], qkT[c % DP]
            pT = self.bank_bf(4, 0, 128)

            def tr(e):
                e.transpose(out=pT[:, 0:128], in_=q_.ap, identity=self.ident.ap)
                return e.transpose(out=pT[:, 128:256], in_=k_.ap, identity=self.ident.ap)
            op('pe', tr, r=[q_.k(), k_.k(), self.ident.k()], w=[self.ps(4)])
            op('act', lambda e: e.activation(out=T_.ap, in_=pT, func=AF.Copy), w=[self.ps(4), T_.k()])

        def st3a(c):
            T_, sc_, v_, kp_ = qkT[c % DP], scm[c % DP], vb[c % DP], kp[c % DP]
            op('pe', lambda e: e.matmul(self.banks[5][:, 0:128], lhsT=T_.ap[:, 128:256], rhs=T_.ap[:, 0:128],
                                        start=True, stop=True), r=[T_.k()], w=[self.ps(5)])
            op('dve', lambda e: e.tensor_tensor(out=sc_.ap, in0=self.banks[5][:, 0:128], in1=mask.ap, op=ALU.mult),
               r=[mask.k()], w=[self.ps(5), sc_.k()])
            if c < NT - 1:
                kb = 3 if c % 2 == 0 else 7
                op('pe', lambda e: e.matmul(self.banks[kb][:, 0:256], lhsT=kp_.ap, rhs=v_.ap, start=True, stop=True),
                   r=[kp_.k(), v_.k()], w=[self.ps(kb)])
                if c == 0:
                    op('dve', lambda e: e.tensor_copy(out=St.ap, in_=self.banks[kb][:, 0:256]), w=[self.ps(kb), St.k()])
                else:
                    op('dve', lambda e: e.scalar_tensor_tensor(out=St.ap, in0=St.ap, scalar=gC, in1=self.banks[kb][:, 0:256],
                                                               op0=ALU.mult, op1=ALU.add),
                       r=[St.k()], w=[self.ps(kb), St.k()])
                sb_ = Sbf[(c + 1) % 4]
                op('act', lambda e: e.activation(out=sb_.ap, in_=St.ap, func=AF.Copy), r=[St.k()], w=[sb_.k()])

        def st3b(c):
            T_, sc_, v_ = qkT[c % DP], scm[c % DP], vb[c % DP]
            yb = 2 if c % 2 == 0 else 6
            sb_ = Sbf[c % 4]

            def ymm(e):
                last = e.matmul(self.banks[yb][:, 0:256], lhsT=sc_.ap, rhs=v_.ap, start=True, stop=(c == 0))
                if c > 0:
                    last = e.matmul(self.banks[yb][:, 0:256], lhsT=T_.ap[:, 0:128], rhs=sb_.ap, start=False, stop=True)
                return last
            op('pe', ymm, r=[sc_.k(), v_.k(), T_.k()] + ([sb_.k()] if c > 0 else []), w=[self.ps(yb)])
            Y = self.banks[yb][:, 0:256]
            op('act', lambda e: e.activation(out=junk.ap, in_=Y, func=AF.Square, accum_out=ssr.ap[:, c:c + 1]),
               w=[self.ps(yb), junk.k(), ssr.k(c)])
            op('pool', lambda e: e.tensor_scalar(out=msr.ap[:, c:c + 1], in0=ssr.ap[:, c:c + 1], scalar1=1.0 / 256,
                                                 scalar2=EPS, op0=ALU.mult, op1=ALU.add), r=[ssr.k(c)], w=[msr.k(c)])
            op('pool', lambda e: e.tensor_tensor(out=rsr.ap[:, c:c + 1], in0=msr.ap[:, c:c + 1],
                                                 in1=self.mhalf.ap[:, 0:1], op=ALU.pow),
               r=[msr.k(c), self.mhalf.k()], w=[rsr.k(c)])
            op('dve', lambda e: e.scalar_tensor_tensor(out=orr.ap[:, c, :], in0=Y, scalar=rsr.ap[:, c:c + 1],
                                                       in1=self.gret_b.ap, op0=ALU.mult, op1=ALU.mult),
               r=[rsr.k(c), self.gret_b.k()], w=[self.ps(yb), orr.k(c)])

        for i in range(NT + 6):
            if 0 <= i - 1 < NT:
                st2a(i - 1)
            if i < NT:
                st1(i)
            if 0 <= i - 2 < NT:
                st2b(i - 2)
            if 0 <= i - 3 < NT:
                st3a(i - 3)
            if 0 <= i - 5 < NT:
                st3b(i - 5)
        self.dump('orr%d' % r, orr, [orr.k(c) for c in range(NT)], [128, NT, 256])
        self.top = mark

    def da_head(self, h, slot):
        A, op, NT, S = self.alloc, self.op, self.NT, self.S
        NQ = S // 512
        mark = self.top
        ws = self.wslots[slot]
        hm = self.hmixT
        ropeA = self.ropeA
        QKT = A('QKT', [128, 2, S], BF16)
        V1 = A('V1', [128, NT, 129], BF16)
        exps = [A('exp%d' % i, [128, 2, 512], BF16) for i in range(4)]
        ss4 = A('ss4', [128, NT, 4], F32)
        ms4 = A('ms4', [128, NT, 4], F32)
        rs4 = A('rs4', [128, NT, 4], F32)
        markP = self.top
        sqj = [A('sqj%d' % i, [128, 256], F32) for i in range(2)]
        u = [A('u%d' % i, [128, 256], F32) for i in range(8)]
        t1 = [A('t1%d' % i, [128, 256], F32) for i in range(2)]
        t2 = [A('t2%d' % i, [128, 256], F32) for i in range(2)]
        o_ = [A('o%d' % i, [128, 256], F32) for i in range(2)]
        qk = [A('qk%d' % i, [128, 256], BF16) for i in range(4)]
        orr = self.orr
        hh = h % 2

        op('pool', lambda e: e.memset(V1.ap[:, :, 128:129], 1.0), w=[V1.k('ones')])

        g4 = lambda ap: ap.rearrange("p (g d) -> p g d", g=4)
        g42 = lambda ap: ap.rearrange("p (g h d) -> p g h d", g=4, h=2)

        def swapped(ap):
            a = [list(x) for x in ap.ap]
            return bass.AP(ap.tensor, ap.offset + 32, [a[0], [64, 4], [-32, 2], [1, 32]])

        def p1(t):
            pb = t % 2

            def mm(e):
                for dc in range(8):
                    last = e.matmul(self.banks[pb][:, 0:384], lhsT=hm.ap[:, dc, t * 128:(t + 1) * 128],
                                    rhs=ws.ap[:, dc, 0:384], start=(dc == 0), stop=(dc == 7))
                return last
            op('pe', mm, r=self.k2(hm, t) + [ws.k()], w=[self.ps(pb)])

        def p2a(t):
            pb = t % 2
            P = self.banks[pb]
            sq, uu_ = sqj[t % 2], u[t % 8]
            op('act', lambda e: e.activation(out=sq.ap, in_=P[:, 0:256], func=AF.Square), w=[self.ps(pb), sq.k()])
            op('dve', lambda e: e.tensor_tensor(out=uu_.ap, in0=P[:, 0:256], in1=self.gqk_b.ap, op=ALU.mult),
               r=[self.gqk_b.k()], w=[self.ps(pb), uu_.k()])
            op('act', lambda e: e.activation(out=V1.ap[:, t, 0:128], in_=P[:, 256:384], func=AF.Copy),
               w=[self.ps(pb), V1.k(t)])
            op('dve', lambda e: e.tensor_reduce(out=ss4.ap[:, t, :], in_=g4(sq.ap), axis=AX.X, op=ALU.add),
               r=[sq.k()], w=[ss4.k(t)])

        def p2b(g):
            sl = slice(4 * g, 4 * g + 4)
            op('pool', lambda e: e.tensor_scalar(out=ms4.ap[:, sl, :], in0=ss4.ap[:, sl, :], scalar1=1.0 / 64,
                                                 scalar2=EPS, op0=ALU.mult, op1=ALU.add),
               r=[ss4.k(t) for t in range(4 * g, 4 * g + 4)], w=[ms4.k(g)])
            op('pool', lambda e: e.tensor_tensor(out=rs4.ap[:, sl, :], in0=ms4.ap[:, sl, :],
                                                 in1=self.mhalf.ap.rearrange("p (a b) -> p a b", b=4), op=ALU.pow),
               r=[ms4.k(g), self.mhalf.k()], w=[rs4.k(g)])

        def p2c(t):
            b = t % 2
            uu_ = u[t % 8]
            cc = ropeA.ap[:, t, 0:1, :].broadcast_to([128, 4, 64])
            sn = ropeA.ap[:, t, 1, :].rearrange("p (h d) -> p h d", h=2).unsqueeze(1).broadcast_to([128, 4, 2, 32])
            op('pool', lambda e: e.tensor_tensor(out=g4(t1[b].ap), in0=g4(uu_.ap), in1=cc, op=ALU.mult),
               r=[uu_.k(), ropeA.k()], w=[t1[b].k()])
            op('dve', lambda e: e.tensor_tensor(out=g42(t2[b].ap), in0=swapped(uu_.ap), in1=sn, op=ALU.mult),
               r=[uu_.k(), ropeA.k()], w=[t2[b].k()])
            op('pool', lambda e: e.tensor_tensor(out=o_[b].ap, in0=t1[b].ap, in1=t2[b].ap, op=ALU.add),
               r=[t1[b].k(), t2[b].k()], w=[o_[b].k()])
            rb = rs4.ap[:, t, :].unsqueeze(2).broadcast_to([128, 4, 64])
            q_ = qk[t % 4]
            op('dve', lambda e: e.tensor_tensor(out=g4(q_.ap), in0=g4(o_[b].ap), in1=rb, op=ALU.mult),
               r=[o_[b].k(), rs4.k(t // 4)], w=[q_.k()])

        def p2d(t):
            q_ = qk[t % 4]
            tb = 2 + (t % 2)
            pT = self.bank_bf(tb, 0, 128).rearrange("p (a b) -> p a b", b=128)

            def tr(e):
                e.transpose(out=pT[:, 0, :], in_=q_.ap[:, 0:128], identity=self.ident.ap)
                return e.transpose(out=pT[:, 1, :], in_=q_.ap[:, 128:256], identity=self.ident.ap)
            op('pe', tr, r=[q_.k(), self.ident.k()], w=[self.ps(tb)])
            op('act', lambda e: e.activation(out=QKT.ap[:, :, t * 128:(t + 1) * 128], in_=pT, func=AF.Copy),
               w=[self.ps(tb), QKT.k(t)])

        for i in range(NT + 9):
            if 0 <= i - 1 < NT:
                p2a(i - 1)
                if (i - 1) % 4 == 3:
                    p2b((i - 1) // 4)
            if i < NT:
                p1(i)
            if 0 <= i - 6 < NT:
                p2c(i - 6)
            if 0 <= i - 8 < NT:
                p2d(i - 8)
        self.dump('QKT%d' % h, QKT, [QKT.k(t) for t in range(NT)], [128, 2, S])
        self.dump('V1_%d' % h, V1, [V1.k(t) for t in range(NT)] + [V1.k('ones')], [128, NT, 129])

        self.top = markP
        accS = [A('accS%d' % i, [128, 8, 129], F32) for i in range(4)]
        rr = A('rr', [128, NT, 4], F32)
        sso = A('sso', [128, NT], F32)
        mso = A('mso', [128, NT], F32)
        rso = A('rso', [128, NT], F32)
        tt = [A('tt%d' % i, [128, 128], F32) for i in range(2)]
        oo = [A('oo%d' % i, [128, 128], F32) for i in range(2)]
        ojk = A('ojk', [128, 128], F32)
        on = [A('on%d' % i, [128, 128], F32) for i in range(2)]
        tg = [A('tg%d' % i, [128, 384], F32) for i in range(2)]
        uu = [A('uu%d' % i, [128, 128], F32) for i in range(2)]
        u2 = [A('u2%d' % i, [128, 128], F32) for i in range(2)]
        u3 = [A('u3%d' % i, [128, 128], F32) for i in range(2)]
        m1 = [A('m1%d' % i, [128, 128], F32) for i in range(2)]
        mg = [A('mg%d' % i, [128, 128], BF16) for i in range(2)]
        mst = [A('mst%d' % i, [128, 512], BF16) for i in range(4)]
        acc_loc = [(4 + i // 3, (i % 3) * 129) for i in range(8)]

        def acc_ap(qb, m):
            bk, off = acc_loc[qb * 2 + m]
            return self.banks[bk][:, off:off + 129]

        def sS(j, kt, step):
            i = kt - 4 * j
            c0 = 128 * i if i > 0 else 0
            sa, sb_ = (0, 1) if step % 2 == 0 else (2, 3)

            def mm(e):
                e.matmul(self.banks[sa][:, c0:512], lhsT=QKT.ap[0:64, 1, kt * 128:(kt + 1) * 128],
                         rhs=QKT.ap[0:64, 0, j * 512 + c0:(j + 1) * 512], start=True, stop=True)
                return e.matmul(self.banks[sb_][:, c0:512], lhsT=QKT.ap[64:128, 1, kt * 128:(kt + 1) * 128],
                                rhs=QKT.ap[64:128, 0, j * 512 + c0:(j + 1) * 512], start=True, stop=True)
            rk = [QKT.k(kt)] + [QKT.k(4 * j + q) for q in range(4)]
            op('pe', mm, r=rk, w=[self.ps(sa), self.ps(sb_)])
            ex = exps[step % 4]
            src = self.pbig[:, sa * 512:(sa + 2) * 512].rearrange("p (m c) -> p m c", m=2)[:, :, c0:512]
            op('act', lambda e: e.activation(out=ex.ap[:, :, c0:512], in_=src, func=AF.Exp, scale=0.125),
               w=[self.ps(sa), self.ps(sb_), ex.k()])
            if i >= 0:
                op('pool', lambda e: e.memset(ex.ap[64:128, :, c0:c0 + 64], 0.0), w=[ex.k()])

        def sV(j, kt, step):
            i = kt - 4 * j
            ex = exps[step % 4]
            qb0 = max(i, 0)

            def mm(e):
                for qb in range(qb0, 4):
                    for m in range(2):
                        idx = qb * 2 + m
                        last = e.matmul(acc_ap(qb, m), lhsT=ex.ap[:, m, qb * 128:(qb + 1) * 128], rhs=V1.ap[:, kt, :],
                                        start=(kt == 0 and idx % 3 == 0), stop=(kt == 4 * j + qb),
                                        skip_group_check=True)
                return last
            op('pe', mm, r=[ex.k(), V1.k(kt), V1.k('ones')], w=[self.ps(4), self.ps(5), self.ps(6)])

        def merge_steps(j):
            ab = accS[j % 4]
            av = ab.ap
            flat = av.rearrange("p i c -> p (i c)")
            op('dve', lambda e: e.tensor_copy(out=flat[:, 0:387], in_=self.banks[4][:, 0:387]), w=[self.ps(4), ab.k(0)])
            op('dve', lambda e: e.tensor_copy(out=flat[:, 387:774], in_=self.banks[5][:, 0:387]), w=[self.ps(5), ab.k(1)])
            op('dve', lambda e: e.tensor_copy(out=flat[:, 774:1032], in_=self.banks[6][:, 0:258]), w=[self.ps(6), ab.k(2)])
            ms_ = mst[j % 4]
            steps = []
            stages = []
            for qb in range(4):
                tb = 4 * j + qb
                b = qb % 2

                def M1(qb=qb, tb=tb, b=b):
                    op('dve', lambda e: e.reciprocal(out=rr.ap[:, tb, 0:2], in_=av[:, 2 * qb:2 * qb + 2, 128]),
                       r=[ab.k(0), ab.k(1), ab.k(2)], w=[rr.k(tb, 0)])
                    op('dve', lambda e: e.tensor_tensor(out=rr.ap[:, tb, 2:3], in0=rr.ap[:, tb, 1:2], in1=self.lamc.ap,
                                                        op=ALU.mult), r=[rr.k(tb, 0), self.lamc.k()], w=[rr.k(tb, 1)])
                    op('dve', lambda e: e.tensor_scalar(out=tt[b].ap, in0=av[:, 2 * qb + 1, 0:128],
                                                        scalar1=rr.ap[:, tb, 2:3], scalar2=None, op0=ALU.mult),
                       r=[ab.k(0), ab.k(1), ab.k(2), rr.k(tb, 1)], w=[tt[b].k()])
                    op('dve', lambda e: e.scalar_tensor_tensor(out=oo[b].ap, in0=av[:, 2 * qb, 0:128],
                                                               scalar=rr.ap[:, tb, 0:1], in1=tt[b].ap,
                                                               op0=ALU.mult, op1=ALU.subtract),
                       r=[ab.k(0), ab.k(1), ab.k(2), rr.k(tb, 0), tt[b].k()], w=[oo[b].k()])
                    op('dve', lambda e: e.scalar_tensor_tensor(out=ojk.ap, in0=oo[b].ap, scalar=1.0, in1=oo[b].ap,
                                                               op0=ALU.mult, op1=ALU.mult,
                                                               accum_out=sso.ap[:, tb:tb + 1]),
                       r=[oo[b].k()], w=[ojk.k(), sso.k(tb)])
                    op('pool', lambda e: e.tensor_scalar(out=mso.ap[:, tb:tb + 1], in0=sso.ap[:, tb:tb + 1],
                                                         scalar1=1.0 / 128, scalar2=EPS, op0=ALU.mult, op1=ALU.add),
                       r=[sso.k(tb)], w=[mso.k(tb)])
                    op('pool', lambda e: e.tensor_tensor(out=rso.ap[:, tb:tb + 1], in0=mso.ap[:, tb:tb + 1],
                                                         in1=self.mhalf.ap[:, 0:1], op=ALU.pow),
                       r=[mso.k(tb), self.mhalf.k()], w=[rso.k(tb)])
                    op('dve', lambda e: e.scalar_tensor_tensor(out=on[b].ap, in0=oo[b].ap, scalar=rso.ap[:, tb:tb + 1],
                                                               in1=self.gda_b.ap, op0=ALU.mult, op1=ALU.mult),
                       r=[oo[b].k(), rso.k(tb), self.gda_b.k()], w=[on[b].k()])

                def M2(qb=qb, tb=tb, b=b):
                    def mm(e):
                        for dc in range(8):
                            last = e.matmul(self.banks[7][:, 0:384], lhsT=hm.ap[:, dc, tb * 128:(tb + 1) * 128],
                                            rhs=ws.ap[:, dc, 384:768], start=(dc == 0), stop=(dc == 7))
                        return last
                    op('pe', mm, r=self.k2(hm, tb) + [ws.k()], w=[self.ps(7)])
                    op('act', lambda e: e.activation(out=tg[b].ap, in_=self.banks[7][:, 0:384], func=AF.Tanh, scale=0.5),
                       w=[self.ps(7), tg[b].k()])
                    op('dve', lambda e: e.scalar_tensor_tensor(out=uu[b].ap, in0=tg[b].ap[:, 256:384], scalar=1.0,
                                                               in1=self.banks[7][:, 256:384], op0=ALU.add, op1=ALU.mult),
                       r=[tg[b].k()], w=[self.ps(7), uu[b].k()])

                def M3(qb=qb, tb=tb, b=b):
                    op('pool', lambda e: e.tensor_tensor(out=u2[b].ap, in0=uu[b].ap,
                                                         in1=orr.ap[:, tb, hh * 128:(hh + 1) * 128], op=ALU.mult),
                       r=[uu[b].k(), orr.k(tb)], w=[u2[b].k()])
                    op('dve', lambda e: e.scalar_tensor_tensor(out=u3[b].ap, in0=tg[b].ap[:, 128:256], scalar=1.0,
                                                               in1=u2[b].ap, op0=ALU.add, op1=ALU.mult),
                       r=[tg[b].k(), u2[b].k()], w=[u3[b].k()])
                    op('dve', lambda e: e.scalar_tensor_tensor(out=m1[b].ap, in0=tg[b].ap[:, 0:128], scalar=1.0,
                                                               in1=on[b].ap, op0=ALU.add, op1=ALU.mult),
                       r=[tg[b].k(), on[b].k()], w=[m1[b].k()])
                    op('pool', lambda e: e.tensor_tensor(out=mg[b].ap, in0=m1[b].ap, in1=u3[b].ap, op=ALU.add),
                       r=[m1[b].k(), u3[b].k()], w=[mg[b].k()])

                def M3b(qb=qb, tb=tb, b=b):
                    pT = self.bank_bf(7, 384, 448)
                    op('pe', lambda e: e.transpose(out=pT, in_=mg[b].ap, identity=self.ident.ap),
                       r=[mg[b].k(), self.ident.k()], w=[self.ps(7)])
                    op('dve', lambda e: e.tensor_copy(out=ms_.ap[:, qb * 128:(qb + 1) * 128], in_=pT),
                       w=[self.ps(7), ms_.k(qb)])
                stages.append((M1, M2, M3, M3b))
            for slot in range(4 + 3):
                for k in (3, 2, 1, 0):
                    if 0 <= slot - k < 4:
                        steps.append(stages[slot - k][k])

            def M4():
                op('sp', lambda e: e.dma_start(out=self.mT_d[h, :, j * 512:(j + 1) * 512], in_=ms_.ap),
                   r=[ms_.k(q) for q in range(4)], w=[('mT_d', h, j)], dma='mst%d' % (j % 4))
            steps.append(M4)
            return steps

        step = 0
        for j in range(NQ):
            nk = 4 * j + 4
            per = 1 if len(self.deferred) <= 2 * nk + 17 else 2
            sS(j, 0, step)
            sS(j, 1, step + 1)
            for kt in range(nk):
                if kt + 2 < nk:
                    sS(j, kt + 2, step + 2)
                sV(j, kt, step)
                step += 1
                self.pump(per)
            while len(self.deferred) > 36:
                self.pump(1)
            self.deferred = self.deferred + merge_steps(j)
            if self.bg:
                self.bg.pop(0)()
        self.flush()
        self.top = mark

    def tail(self):
        A, op, S, dr = self.alloc, self.op, self.S, self.dr
        NTT = S // 512
        self.top = self.persist_top
        G1b = A('G1b', [128, D], F32)
        G2b = A('G2b', [128, D], F32)
        wout = A('wout', [128, 8, D], BF16)
        diag = [A('diag%d' % i, [128, 128], F32) for i in range(2)]
        x1 = [A('x1_%d' % i, [128, 4, D], F32) for i in range(2)]
        mTt = [A('mTt%d' % i, [128, 8, 512], BF16) for i in range(2)]
        hffT = A('hffT', [128, 8, 512], BF16)
        hT = A('hT', [128, 32, 512], BF16)
        xn2 = [A('xn2_%d' % i, [128, D], BF16) for i in range(2)]
        rl = [A('rl%d' % i, [128, 512], F32) for i in range(2)]
        yg = [A('yg%d' % i, [128, D], F32) for i in range(2)]
        wupc = [A('wupc%d' % i, [128, 8, 512], BF16) for i in range(3)]
        wdnc = [A('wdnc%d' % i, [128, 4, 512], BF16) for i in range(4)]
        junk = A('tjunk', [128, D], BF16)
        ss = A('ss2', [128, S // 128], F32)
        ms = A('ms2', [128, S // 128], F32)
        rstd = A('rstd2', [128, S // 128], F32)

        src = dr['w_out'].rearrange("(g p) n -> p g n", p=128)

        def ldw(e):
            return [e.dma_start(out=wout.ap[:, g0:g0 + 4, :], in_=src[:, g0:g0 + 4, :]) for g0 in (0, 4)]
        op('pool', ldw, w=[wout.k()], dma='wout', ndma=2)

        for gi, (G, c0) in enumerate(((G1b, 16), (G2b, 40))):
            for half in range(2):
                for q in range(4):
                    dc = half * 4 + q
                    dg = diag[dc % 2]
                    op('dve', lambda e, dg=dg, dc=dc, c0=c0: e.tensor_scalar(
                        out=dg.ap, in0=self.identf.ap, scalar1=self.modc.ap[:, c0 + dc:c0 + dc + 1], scalar2=None,
                        op0=ALU.mult), r=[self.identf.k(), self.modc.k()], w=[dg.k()])
                    op('pe', lambda e, dg=dg, q=q, half=half: e.matmul(
                        self.banks[half][:, q * 128:(q + 1) * 128], lhsT=self.onesf.ap, rhs=dg.ap, start=True, stop=True),
                       r=[self.onesf.k(), dg.k()], w=[self.ps(half)])
                op('act', lambda e, G=G, half=half: e.activation(out=G.ap[:, half * 512:(half + 1) * 512],
                                                                 in_=self.banks[half][:, 0:512], func=AF.Copy),
                   w=[self.ps(half), G.k(half)])

        sctr = {'up': 0, 'dn': 0}

        def load_tile(i):
            xb = x1[i % 2]
            xsrc = self.x[i * 512:(i + 1) * 512, :].rearrange("(s p) d -> p s d", p=128)
            op('pool', lambda e: e.dma_start(out=xb.ap, in_=xsrc), w=[xb.k(s) for s in range(4)], dma='x1_%d' % (i % 2))
            mt = mTt[i % 2]
            msrc = self.mT_d[:, :, i * 512:(i + 1) * 512].rearrange("g p t -> p g t")
            op('pool', lambda e: e.dma_start(out=mt.ap, in_=msrc), r=[('mT_d', g, i) for g in range(8)],
               w=[mt.k()], dma='mTt%d' % (i % 2))

        def O_steps(i):
            xb = x1[i % 2]
            mt = mTt[i % 2]
            steps = []
            for s in range(4):
                def Oa(s=s):
                    def mm(e):
                        for nb in range(2):
                            for g in range(8):
                                last = e.matmul(self.banks[6 + nb][:, 0:512], lhsT=mt.ap[:, g, s * 128:(s + 1) * 128],
                                                rhs=wout.ap[:, g, nb * 512:(nb + 1) * 512], start=(g == 0), stop=(g == 7))
                        return last
                    op('pe', mm, r=[mt.k(), wout.k()], w=[self.ps(6), self.ps(7)])
                    y = yg[s % 2]
                    for nb in range(2):
                        op('dve', lambda e, nb=nb: e.tensor_tensor(out=y.ap[:, nb * 512:(nb + 1) * 512],
                                                                    in0=self.banks[6 + nb][:, 0:512],
                                                                    in1=G1b.ap[:, nb * 512:(nb + 1) * 512], op=ALU.mult),
                           r=[G1b.k(nb)], w=[self.ps(6 + nb), y.k(nb)])
                    op('pool', lambda e: e.tensor_tensor(out=xb.ap[:, s, :], in0=y.ap, in1=xb.ap[:, s, :], op=ALU.add),
                       r=[y.k(0), y.k(1), xb.k(s)], w=[xb.k(s)])

                def Ob(s=s):
                    t = i * 4 + s
                    self.norm_to_T(xb.ap[:, s, :], xb.k(s), t, (ss, ms, rstd, junk), xn2[s % 2], self.A2,
                                   self.modc.ap[:, 24:32],
                                   lambda lo, hi: hffT.ap[:, lo:hi, s * 128:(s + 1) * 128], self.k2(hffT, s), (6, 7), 'n2')
                steps += [Oa, Ob]
            return steps

        def U(i):
            for c in range(8):
                wb = wupc[sctr['up'] % 3]
                sctr['up'] += 1
                op('sp', lambda e, wb=wb, c=c: e.dma_start(out=wb.ap, in_=self.wup_s[c]), r=[('wup_s',)], w=[wb.k()],
                   dma=wb.name)
                for fl in range(4):
                    fb = 4 * c + fl
                    bk = 4 + fb % 2

                    def mm(e, fl=fl, bk=bk, wb=wb):
                        for dc in range(8):
                            last = e.matmul(self.banks[bk][:, 0:512], lhsT=wb.ap[:, dc, fl * 128:(fl + 1) * 128],
                                            rhs=hffT.ap[:, dc, :], start=(dc == 0), stop=(dc == 7))
                        return last
                    op('pe', mm, r=[wb.k()] + sum([self.k2(hffT, s) for s in range(4)], []), w=[self.ps(bk)])
                    r_ = rl[fb % 2]
                    op('act', lambda e, bk=bk, r_=r_: e.activation(out=r_.ap, in_=self.banks[bk][:, 0:512], func=AF.Relu),
                       w=[self.ps(bk), r_.k()])
                    eng = 'dve' if fb % 2 == 0 else 'pool'
                    op(eng, lambda e, r_=r_, fb=fb: e.tensor_tensor(out=hT.ap[:, fb, :], in0=r_.ap, in1=r_.ap, op=ALU.mult),
                       r=[r_.k()], w=[hT.k(fb)])

        def Dn(i):
            xb = x1[i % 2]
            for nb in range(2):
                for c in range(8):
                    wb = wdnc[sctr['dn'] % 4]
                    sctr['dn'] += 1
                    op('sp', lambda e, wb=wb, c=c, nb=nb: e.dma_start(out=wb.ap, in_=self.wdn_s[nb, c]),
                       r=[('wdn_s',)], w=[wb.k()], dma=wb.name)

                    def mm(e, wb=wb, c=c):
                        for s in range(4):
                            for fl in range(4):
                                last = e.matmul(self.banks[s][:, 0:512], lhsT=hT.ap[:, 4 * c + fl, s * 128:(s + 1) * 128],
                                                rhs=wb.ap[:, fl, :], start=(c == 0 and fl == 0), stop=(c == 7 and fl == 3))
                        return last
                    op('pe', mm, r=[wb.k()] + [hT.k(4 * c + fl) for fl in range(4)], w=[self.ps(s) for s in range(4)])
                    self.pump(1)
                for s in range(4):
                    y = yg[s % 2]
                    op('dve', lambda e, s=s, y=y, nb=nb: e.tensor_tensor(out=y.ap[:, 0:512], in0=self.banks[s][:, 0:512],
                                                                         in1=G2b.ap[:, nb * 512:(nb + 1) * 512], op=ALU.mult),
                       r=[G2b.k(nb)], w=[self.ps(s), y.k(0)])
                    op('pool', lambda e, s=s, y=y, nb=nb: e.tensor_tensor(
                        out=xb.ap[:, s, nb * 512:(nb + 1) * 512], in0=y.ap[:, 0:512],
                        in1=xb.ap[:, s, nb * 512:(nb + 1) * 512], op=ALU.add), r=[y.k(0), xb.k(s)], w=[xb.k(s)])
            dst = self.out[i * 512:(i + 1) * 512, :].rearrange("(s p) d -> p s d", p=128)
            op('pool', lambda e: e.dma_start(out=dst, in_=xb.ap), r=[xb.k(s) for s in range(4)], w=[('out', i)],
               dma='ost%d' % (i % 2))
            self.outkeys.append(('out', i))

        load_tile(0)
        for f in O_steps(0):
            f()
        for i in range(NTT):
            if i + 1 < NTT:
                load_tile(i + 1)
                self.deferred = O_steps(i + 1)
            U(i)
            Dn(i)
            self.flush()

    def build(self, upto='all'):
        self.setup()
        A = self.alloc
        self.hmixT = A('hmixT', [128, 8, self.S], BF16)
        self.orr = A('orr', [128, self.NT, 256], BF16)
        self.ropeA = A('ropeA', [128, self.NT, 2, 64], F32)
        self.wslots = [A('wslot0', [128, 8, 768], BF16), A('wslot1', [128, 8, 768], BF16)]
        self.mixer_top = self.top
        self.bg = []
        phases = []
        for r in range(4):
            phases += [('ret', r), ('da', 2 * r), ('da', 2 * r + 1)]
        if upto == 'ret0':
            phases = phases[:1]
        elif upto == 'da0':
            phases = phases[:2]

        def wblocks(ph):
            kind, i = ph
            if kind == 'ret':
                return [(C_QR + i * 128, 128), (C_KR + i * 128, 128), (C_VR + i * 256, 256)]
            return [(C_QA + i * 128, 128), (C_KA + i * 128, 128), (C_VA + i * 128, 128),
                    (C_GA + i * 128, 128), (C_GB + i * 128, 128), (C_GR + i * 128, 128)]
        self.phase_mod()
        self.load_w(0, wblocks(phases[0]))
        self.op('sp', lambda e: e.dma_start(out=self.ropeA.ap, in_=self.dr['ropeA']), w=[self.ropeA.k()], dma='ropeA')
        self.phase_norm1()
        self.top = self.mixer_top
        if upto == 'norm1':
            return self.finish()
        self.ffn_scratch()
        for n, ph in enumerate(phases):
            if n + 1 < len(phases):
                self.load_w((n + 1) % 2, wblocks(phases[n + 1]))
            if ph[0] == 'ret':
                self.ret_head(ph[1], n % 2)
            else:
                self.da_head(ph[1], n % 2)
        while self.bg:
            self.bg.pop(0)()
        if upto != 'all':
            return self.finish()
        self.tail()
        return self.finish()

    def finish(self):
        self.flush()
        keys = list(self.outkeys)
        self.op('sp', lambda e: e.nop(), r=keys + [('wup_s',), ('wdn_s',)], sig=False)
        self.sch.run_block()


def build_program(S=4096, dbg=(), upto='all'):
    nc = bass.Bass("TRN2", target_bir_lowering=False)
    with ExitStack() as es:
        kb = KB(nc, es, S, dbg)
        kb.build(upto)
    return nc


def host_tables(S):
    NT = S // 128
    pos = np.arange(S, dtype=np.float32)
    f32 = np.float32
    invA = (10000.0 ** (-np.arange(0, 64, 2, dtype=f32) / f32(64))).astype(f32)
    angA = (pos[:, None] * invA[None, :]).astype(f32).astype(np.float64)
    cA, sA = np.cos(angA).astype(f32), np.sin(angA).astype(f32)
    ropeA = np.stack([np.concatenate([cA, cA], 1), np.concatenate([-sA, sA], 1)], 1)
    ropeA = np.ascontiguousarray(ropeA.reshape(NT, 128, 2, 64).transpose(1, 0, 2, 3))
    invR = (1.0 / (f32(10000.0) ** np.linspace(0.0, 1.0, 64, dtype=f32))).astype(f32)
    angR = (pos[:, None] * invR[None, :]).astype(f32).astype(np.float64)
    cR, sR = np.cos(angR).astype(f32), np.sin(angR).astype(f32)
    ropeR = np.stack([np.concatenate([cR, cR], 1), np.concatenate([-sR, sR], 1)], 1)
    ropeR = np.ascontiguousarray(ropeR.reshape(NT, 128, 2, 128).transpose(1, 0, 2, 3))
    n = np.arange(128, dtype=np.float64)
    retc = np.zeros((128, 8), f32)
    for r in range(4):
        lg = np.log(1.0 - 2.0 ** (-5.0 - r))
        retc[:, 2 * r] = np.exp((n + 1.0) * lg)
        retc[:, 2 * r + 1] = (128.0 ** -0.5) * np.exp(-(n + 1.0) * lg)
    jj, ii = np.meshgrid(np.arange(128), np.arange(128), indexing='ij')
    retmask = (ii >= jj).astype(f32)
    return dict(ropeA=ropeA, ropeR=ropeR, retc=retc, retmask=retmask, identf=np.eye(128, dtype=f32))


def core_inputs(b, S, inp, tabs):
    f = lambda a: np.ascontiguousarray(np.asarray(a, dtype=np.float32))
    col = lambda v, n: f(np.asarray(v, np.float32).reshape(n, 128).T)
    m = dict(tabs)
    m['x'] = f(inp['x'][b, :S])
    m['c_col'] = col(inp['c'][b], 8)
    m['w_ada'] = f(inp['w_ada'][0])
    m['b_ada_c'] = col(inp['b_ada'][0], 48)
    m['g1c'] = col(inp['g_norm1'][0], 8)
    m['g2c'] = col(inp['g_norm2'][0], 8)
    m['w_in'] = f(inp['w_in'][0])
    gq, gk = np.asarray(inp['g_q'][0], np.float32), np.asarray(inp['g_k'][0], np.float32)
    m['gqk_row'] = f(np.concatenate([gq, gq, gk, gk])[None, :])
    m['lam_row'] = f(np.concatenate([inp['lambda_q1'][0], inp['lambda_q2'][0],
                                     inp['lambda_k1'][0], inp['lambda_k2'][0]])[None, :])
    m['gda_row'] = f(np.asarray(inp['g_da_out'][0])[None, :])
    m['gret_row'] = f(np.asarray(inp['g_ret_out'][0])[None, :])
    m['w_out'] = f(inp['w_out'][0])
    m['w_up'] = f(inp['w_up'][0])
    m['w_down'] = f(inp['w_down'][0])
    return m


def kernel(**inputs):
    S = 4096
    inp = {k: np.asarray(v) for k, v in inputs.items()}
    B = inp['x'].shape[0]
    tabs = host_tables(S)
    nc = build_program(S)
    in_maps = [core_inputs(b, S, inp, tabs) for b in range(B)]
    res = run_bass_kernel_spmd(nc, in_maps, core_ids=list(range(B)))
    out = np.stack([np.asarray(res.results[b]['out']) for b in range(B)], 0)
    return out.astype(np.float32)
```

```python
import math
from contextlib import ExitStack

import numpy as np
import concourse.bass as bass
import concourse.mybir as mybir
from concourse.bass_utils import run_bass_kernel_spmd

F32 = mybir.dt.float32
BF16 = mybir.dt.bfloat16
AF = mybir.ActivationFunctionType
ALU = mybir.AluOpType
AX = mybir.AxisListType

D = 1024
DFF = 4096
EPS = 1e-6
LAMBDA_INIT = 0.2
C_QA, C_KA, C_VA, C_QR, C_KR, C_VR, C_GR, C_GA, C_GB = 0, 1024, 2048, 3072, 3584, 4096, 5120, 6144, 7168
ARENA_BYTES = 212736


class Sched:
    CE = ('pe', 'act', 'dve', 'pool')

    def __init__(self, nc, es):
        self.nc, self.es = nc, es
        self.q = {e: [] for e in ('pe', 'act', 'dve', 'pool', 'sp')}
        self.sem = {e: es.enter_context(nc.semaphore('s_' + e)) for e in self.CE}
        self.cnt = {e: 0 for e in self.CE}
        self.lastw, self.reads = {}, {}
        self.dsem = {}
        self.waited = {e: {} for e in self.q}
        self.bufs, self.bufkeys, self.inherit, self.known = [], {}, {}, set()

    @staticmethod
    def _merge(d, ev):
        k = id(ev[0])
        if k not in d or d[k][1] < ev[1]:
            d[k] = ev

    def new_buffer(self, name, lo, hi):
        inh = {}
        keep = []
        for (lo2, hi2, n2) in self.bufs:
            if lo2 < hi and lo < hi2:
                for k in self.bufkeys.get(n2, ()):
                    ev = self.lastw.get(k)
                    if ev is not None:
                        self._merge(inh, ev)
                    for ev in self.reads.get(k, {}).values():
                        self._merge(inh, ev)
                for ev in self.inherit.get(n2, {}).values():
                    self._merge(inh, ev)
                if lo <= lo2 and hi2 <= hi:
                    continue
            keep.append((lo2, hi2, n2))
        keep.append((lo, hi, name))
        self.bufs = keep
        self.inherit[name] = inh

    def _touch(self, k):
        if k in self.known:
            return
        self.known.add(k)
        name = k[0] if isinstance(k, tuple) else k
        self.bufkeys.setdefault(name, set()).add(k)
        self.reads[k] = dict(self.inherit.get(name, {}))

    def op(self, eng, fn, r=(), w=(), dma=None, ndma=1, sig=True):
        waits = {}

        def need(ev, raw):
            if ev is None:
                return
            sem, val, src = ev
            if src == eng and (not raw or eng == 'pe'):
                return
            k = id(sem)
            if self.waited[eng].get(k, 0) >= val:
                return
            if k not in waits or waits[k][1] < val:
                waits[k] = (sem, val)

        for k in r:
            self._touch(k)
            need(self.lastw.get(k), True)
        for k in w:
            self._touch(k)
            need(self.lastw.get(k), False)
            for ev in self.reads[k].values():
                need(ev, False)
        for k, (sem, val) in waits.items():
            self.waited[eng][k] = val
        if dma is not None:
            if dma not in self.dsem:
                self.dsem[dma] = [self.es.enter_context(self.nc.semaphore('d_' + dma)), 0]
            ds = self.dsem[dma]
            ds[1] += 16 * ndma
            ev = (ds[0], ds[1], 'dma')
            evh = ('dma', ds[0], ndma)
        elif sig:
            self.cnt[eng] += 1
            ev = (self.sem[eng], self.cnt[eng], eng)
            evh = ('ce', self.sem[eng])
        else:
            ev, evh = None, ('none',)
        self.q[eng].append((list(waits.values()), fn, evh))
        for k in r:
            if ev is not None:
                self._merge(self.reads[k], ev)
        for k in w:
            self.lastw[k] = ev
            self.reads[k] = {}
        return ev

    def emit(self, eng, e):
        for waits, fn, evh in self.q[eng]:
            for sem, val in waits:
                e.wait_ge(sem, val)
            ins = fn(e)
            if evh[0] == 'ce':
                ins.then_inc(evh[1], 1)
            elif evh[0] == 'dma':
                if not isinstance(ins, (list, tuple)):
                    ins = [ins]
                assert len(ins) == evh[2], (len(ins), evh[2])
                for i in ins:
                    i.then_inc(evh[1], 16)

    def run_block(self):
        with self.nc.Block() as block:
            @block.tensor
            def _(e):
                self.emit('pe', e)

            @block.scalar
            def _(e):
                self.emit('act', e)

            @block.vector
            def _(e):
                self.emit('dve', e)

            @block.gpsimd
            def _(e):
                self.emit('pool', e)

            @block.sync
            def _(e):
                self.emit('sp', e)


class Buf:
    def __init__(self, name, ap):
        self.name, self.ap = name, ap

    def k(self, *sub):
        return (self.name,) + sub


class KB:
    def __init__(self, nc, es, S, dbg=()):
        self.nc, self.es, self.S, self.dbg = nc, es, S, set(dbg)
        self.NT = S // 128
        self.sch = Sched(nc, es)
        self.arena = es.enter_context(nc.sbuf_tensor('arena', [128, ARENA_BYTES // 4], F32))
        self.top = 0
        self.nalloc = 0
        self.pbig = es.enter_context(nc.psum_tensor('pbig', [128, 8 * 512], F32))
        self.banks = [self.pbig[:, i * 512:(i + 1) * 512] for i in range(8)]
        self.outkeys = []
        self.deferred = []
        self.dr = {}

    def alloc(self, name, shape, dtype):
        esz = 4 if dtype == F32 else 2
        n = int(np.prod(shape[1:]))
        nbytes = (n * esz + 63) // 64 * 64
        lo, hi = self.top, self.top + nbytes
        assert hi <= ARENA_BYTES, ('SBUF arena overflow', name, hi)
        self.top = hi
        self.nalloc += 1
        uname = '%s#%d' % (name, self.nalloc)
        self.sch.new_buffer(uname, lo, hi)
        v = self.arena[:, lo // 4: hi // 4]
        if dtype != F32:
            v = v.bitcast(dtype)
        v = v[0:shape[0], 0:n]
        if len(shape) == 3:
            v = v.rearrange("p (a b) -> p a b", b=shape[2])
        elif len(shape) == 4:
            v = v.rearrange("p (a b c) -> p a b c", b=shape[2], c=shape[3])
        return Buf(uname, v)

    def ps(self, i):
        return ('ps', i)

    def bank_bf(self, i, lo, hi):
        return self.banks[i][:, lo:hi].bitcast(BF16)

    def op(self, *a, **k):
        return self.sch.op(*a, **k)

    def dram_in(self, name, shape, dtype=F32):
        t = self.nc.dram_tensor(name, list(shape), dtype, kind="ExternalInput").ap()
        self.dr[name] = t
        return t

    def dump(self, name, buf, keys, shape, src=None):
        if name not in self.dbg:
            return
        src = buf.ap if src is None else src
        t = self.nc.dram_tensor('dbg_' + name, list(shape), src.dtype, kind="ExternalOutput").ap()
        self.op('sp', lambda e: e.dma_start(out=t, in_=src), r=list(keys), w=[('dbgout', name)], dma='dbg_' + name)
        self.outkeys.append(('dbgout', name))

    @staticmethod
    def k2(buf, t):
        return [buf.k(t, 0), buf.k(t, 1)]

    def pump(self, n=1):
        for _ in range(n):
            if not self.deferred:
                return
            self.deferred.pop(0)()

    def flush(self):
        while self.deferred:
            self.deferred.pop(0)()

    def setup(self):
        nc, S, NT = self.nc, self.S, self.NT
        di = self.dram_in
        self.x = di('x', [S, D])
        di('c_col', [128, 8]); di('w_ada', [D, 6 * D]); di('b_ada_c', [128, 48])
        di('g1c', [128, 8]); di('g2c', [128, 8]); di('w_in', [D, 8192])
        di('gqk_row', [1, 256]); di('lam_row', [1, 256]); di('gda_row', [1, 128]); di('gret_row', [1, 256])
        di('w_out', [D, D]); di('w_up', [D, DFF]); di('w_down', [DFF, D])
        di('ropeA', [128, NT, 2, 64]); di('ropeR', [128, NT, 2, 128])
        di('retmask', [128, 128]); di('retc', [128, 8]); di('identf', [128, 128])
        self.out = nc.dram_tensor('out', [S, D], F32, kind="ExternalOutput").ap()
        self.mT_d = nc.dram_tensor('mT_d', [8, 128, S], BF16, kind="Internal").ap()
        self.wup_s = nc.dram_tensor('wup_s', [8, 128, 8, 512], BF16, kind="Internal").ap()
        self.wdn_s = nc.dram_tensor('wdn_s', [2, 8, 128, 4, 512], BF16, kind="Internal").ap()

        A = self.alloc
        dr = self.dr
        self.c_col = A('c_col', [128, 8], F32)
        self.b_ada_c = A('b_ada_c', [128, 48], F32)
        self.g1c = A('g1c', [128, 8], F32)
        self.g2c = A('g2c', [128, 8], F32)
        self.identf = A('identf', [128, 128], F32)
        self.ident = A('ident', [128, 128], BF16)
        self.onesf = A('onesf', [128, 128], F32)
        self.mhalf = A('mhalf', [128, 16], F32)
        self.epsc = A('epsc', [128, 1], F32)
        self.modc = A('modc', [128, 48], F32)
        self.A1 = A('A1', [128, 8], F32)
        self.A2 = A('A2', [128, 8], F32)
        self.retmask = A('retmask', [128, 128], F32)
        self.retc = A('retc', [128, 8], F32)
        self.gqk_b = A('gqk_b', [128, 256], F32)
        self.lam_b = A('lam_b', [128, 256], F32)
        self.gda_b = A('gda_b', [128, 128], F32)
        self.gret_b = A('gret_b', [128, 256], F32)
        self.lamc = A('lamc', [128, 1], F32)
        self.lamt = A('lamt', [128, 4], F32)
        self.lamp = A('lamp', [128, 128], F32)

        loads = [(self.c_col, dr['c_col']), (self.b_ada_c, dr['b_ada_c']), (self.g1c, dr['g1c']),
                 (self.g2c, dr['g2c']), (self.identf, dr['identf']), (self.retmask, dr['retmask']),
                 (self.retc, dr['retc']),
                 (self.gqk_b, dr['gqk_row'][0:1, :].broadcast_to([128, 256])),
                 (self.lam_b, dr['lam_row'][0:1, :].broadcast_to([128, 256])),
                 (self.gda_b, dr['gda_row'][0:1, :].broadcast_to([128, 128])),
                 (self.gret_b, dr['gret_row'][0:1, :].broadcast_to([128, 256]))]

        def ld(e):
            return [e.dma_start(out=b.ap, in_=src) for b, src in loads]
        self.op('sp', ld, w=[b.k() for b, _ in loads], dma='const', ndma=len(loads))
        self.op('dve', lambda e: e.tensor_copy(out=self.ident.ap, in_=self.identf.ap),
                r=[self.identf.k()], w=[self.ident.k()])
        self.op('pool', lambda e: e.memset(self.onesf.ap, 1.0), w=[self.onesf.k()])
        self.op('pool', lambda e: e.memset(self.mhalf.ap, -0.5), w=[self.mhalf.k()])
        self.op('pool', lambda e: e.memset(self.epsc.ap, EPS), w=[self.epsc.k()])
        self.op('dve', lambda e: e.tensor_scalar(out=self.gda_b.ap, in0=self.gda_b.ap,
                                                 scalar1=(1.0 - LAMBDA_INIT) * 0.5, scalar2=None, op0=ALU.mult),
                r=[self.gda_b.k()], w=[self.gda_b.k()])
        self.op('dve', lambda e: e.tensor_scalar(out=self.gret_b.ap, in0=self.gret_b.ap,
                                                 scalar1=0.25, scalar2=None, op0=ALU.mult),
                r=[self.gret_b.k()], w=[self.gret_b.k()])
        lb, lp, lt = self.lam_b, self.lamp, self.lamt
        self.op('dve', lambda e: e.tensor_tensor(out=lp.ap, in0=lb.ap[:, 0:128], in1=lb.ap[:, 128:256], op=ALU.mult),
                r=[lb.k()], w=[lp.k()])
        self.op('dve', lambda e: e.tensor_reduce(out=lt.ap[:, 0:2], in_=lp.ap.rearrange("p (a b) -> p a b", b=64),
                                                 axis=AX.X, op=ALU.add), r=[lp.k()], w=[lt.k(0)])
        self.op('act', lambda e: e.activation(out=lt.ap[:, 2:4], in_=lt.ap[:, 0:2], func=AF.Exp),
                r=[lt.k(0)], w=[lt.k(1)])
        self.op('dve', lambda e: e.tensor_tensor(out=self.lamc.ap, in0=lt.ap[:, 2:3], in1=lt.ap[:, 3:4],
                                                 op=ALU.subtract), r=[lt.k(1)], w=[self.lamc.k()])
        self.op('dve', lambda e: e.tensor_scalar(out=self.lamc.ap, in0=self.lamc.ap, scalar1=LAMBDA_INIT,
                                                 scalar2=None, op0=ALU.add), r=[self.lamc.k()], w=[self.lamc.k()])
        self.persist_top = self.top

    def ffn_scratch(self):
        dr = self.dr
        for c in range(8):
            def up(c=c):
                src = dr['w_up'][:, c * 512:(c + 1) * 512].rearrange("(dc p) f -> p dc f", p=128)
                self.op('pool', lambda e: e.dma_start(out=self.wup_s[c], in_=src), w=[('wup_s',)], dma='wup_s')
            self.bg.append(up)
        for nb in range(2):
            for c in range(8):
                def dn(nb=nb, c=c):
                    src = dr['w_down'][c * 512:(c + 1) * 512, nb * 512:(nb + 1) * 512].rearrange(
                        "(fb p) n -> p fb n", p=128)
                    self.op('pool', lambda e: e.dma_start(out=self.wdn_s[nb, c], in_=src), w=[('wdn_s',)], dma='wdn_s')
                self.bg.append(dn)

    def phase_mod(self):
        A, op, dr = self.alloc, self.op, self.dr
        th = A('th', [128, 8], F32)
        scf = A('scf', [128, 8], F32)
        scb = A('scb', [128, 8], BF16)
        was = [A('wa0', [128, 8, 1024], BF16), A('wa1', [128, 8, 1024], BF16)]
        cc = self.c_col
        op('act', lambda e: e.activation(out=th.ap, in_=cc.ap, func=AF.Tanh, scale=0.5), r=[cc.k()], w=[th.k()])
        op('dve', lambda e: e.scalar_tensor_tensor(out=scf.ap, in0=th.ap, scalar=1.0, in1=cc.ap,
                                                   op0=ALU.add, op1=ALU.mult), r=[th.k(), cc.k()], w=[scf.k()])
        op('dve', lambda e: e.tensor_scalar(out=scb.ap, in0=scf.ap, scalar1=0.5, scalar2=None, op0=ALU.mult),
           r=[scf.k()], w=[scb.k()])
        bank = self.banks[7]
        mc = self.modc

        def dma(v):
            wa = was[v % 2]
            src = dr['w_ada'][:, v * 1024:(v + 1) * 1024].rearrange("(kc p) n -> p kc n", p=128)
            op('pool', lambda e: e.dma_start(out=wa.ap, in_=src), w=[wa.k()], dma='wa%d' % (v % 2))

        def mmv(v):
            wa = was[v % 2]

            def mm(e):
                for j in range(8):
                    for kc in range(8):
                        last = e.matmul(bank[:, v * 8 + j: v * 8 + j + 1], lhsT=wa.ap[:, kc, j * 128:(j + 1) * 128],
                                        rhs=scb.ap[:, kc:kc + 1], start=(kc == 0), stop=(kc == 7))
                return last
            op('pe', mm, r=[wa.k(), scb.k()], w=[self.ps(7)])

        def fin():
            op('dve', lambda e: e.tensor_tensor(out=mc.ap[:, 16:48], in0=bank[:, 16:48], in1=self.b_ada_c.ap[:, 16:48],
                                                op=ALU.add), r=[self.b_ada_c.k()], w=[self.ps(7), mc.k(1)])
            op('dve', lambda e: e.scalar_tensor_tensor(out=self.A2.ap, in0=mc.ap[:, 32:40], scalar=1.0, in1=self.g2c.ap,
                                                       op0=ALU.add, op1=ALU.mult), r=[mc.k(1), self.g2c.k()], w=[self.A2.k()])
            self.dump('modc', mc, [mc.k(0), mc.k(1)], [128, 48])

        dma(0); dma(1); mmv(0); mmv(1); dma(2); dma(3)
        op('dve', lambda e: e.tensor_tensor(out=mc.ap[:, 0:16], in0=bank[:, 0:16], in1=self.b_ada_c.ap[:, 0:16], op=ALU.add),
           r=[self.b_ada_c.k()], w=[self.ps(7), mc.k(0)])
        op('dve', lambda e: e.scalar_tensor_tensor(out=self.A1.ap, in0=mc.ap[:, 8:16], scalar=1.0, in1=self.g1c.ap,
                                                   op0=ALU.add, op1=ALU.mult), r=[mc.k(0), self.g1c.k()], w=[self.A1.k()])
        self.mod_rest = {6: [lambda: mmv(2), lambda: dma(4)], 12: [lambda: mmv(3), lambda: dma(5)],
                         18: [lambda: mmv(4)], 24: [lambda: mmv(5), fin]}

    def norm_to_T(self, xt_ap, xkey, t, stats, xn, A_, Bcol, dst_fn, dst_key, banks, tag):
        op = self.op
        ss, ms, rstd, junk = stats
        op('act', lambda e: e.activation(out=junk.ap, in_=xt_ap, func=AF.Square, accum_out=ss.ap[:, t:t + 1]),
           r=[xkey], w=[junk.k(), ss.k(t)])
        op('pool', lambda e: e.tensor_scalar(out=ms.ap[:, t:t + 1], in0=ss.ap[:, t:t + 1], scalar1=1.0 / D,
                                             scalar2=EPS, op0=ALU.mult, op1=ALU.add), r=[ss.k(t)], w=[ms.k(t)])
        op('pool', lambda e: e.tensor_tensor(out=rstd.ap[:, t:t + 1], in0=ms.ap[:, t:t + 1],
                                             in1=self.mhalf.ap[:, 0:1], op=ALU.pow),
           r=[ms.k(t), self.mhalf.k()], w=[rstd.k(t)])
        op('dve', lambda e: e.tensor_scalar(out=xn.ap, in0=xt_ap, scalar1=rstd.ap[:, t:t + 1], scalar2=None,
                                            op0=ALU.mult), r=[xkey, rstd.k(t)], w=[xn.k()])
        ba, bb = banks
        pa = self.bank_bf(ba, 0, 256).rearrange("p (a b) -> p a b", b=128)
        pb = self.bank_bf(bb, 0, 256).rearrange("p (a b) -> p a b", b=128)

        def tr(e):
            for dc in range(8):
                pt = pa if dc < 4 else pb
                last = e.transpose(out=pt[:, dc % 4, :], in_=xn.ap[:, dc * 128:(dc + 1) * 128], identity=self.ident.ap)
            return last
        op('pe', tr, r=[xn.k(), self.ident.k()], w=[self.ps(ba), self.ps(bb)])
        da = dst_fn(0, 4)
        db = dst_fn(4, 8)

        def ev_act(e):
            for dc in range(4):
                last = e.activation(out=da[:, dc, :], in_=pa[:, dc, :], func=AF.Identity,
                                    scale=A_.ap[:, dc:dc + 1], bias=Bcol[:, dc:dc + 1])
            return last
        op('act', ev_act, r=[A_.k(), self.modc.k(1)], w=[self.ps(ba), dst_key[0]])

        def ev_dve(e):
            for dc in range(4, 8):
                last = e.tensor_scalar(out=db[:, dc - 4, :], in0=pb[:, dc - 4, :], scalar1=A_.ap[:, dc:dc + 1],
                                       scalar2=Bcol[:, dc:dc + 1], op0=ALU.mult, op1=ALU.add)
            return last
        op('dve', ev_dve, r=[A_.k(), self.modc.k(1)], w=[self.ps(bb), dst_key[1]])

    def phase_norm1(self):
        A, op, NT = self.alloc, self.op, self.NT
        NG = NT // 4
        xg = [A('xg%d' % i, [128, 4, D], F32) for i in range(2)]
        xns = [A('xn%d' % i, [128, D], BF16) for i in range(3)]
        junk = A('junk', [128, D], BF16)
        ss = A('ss1', [128, NT], F32)
        ms = A('ms1', [128, NT], F32)
        rstd = A('rstd1', [128, NT], F32)
        hm = self.hmixT
        A_, Bcol = self.A1, self.modc.ap[:, 0:8]

        def sa(t):
            g, k = divmod(t, 4)
            xb = xg[g % 2]
            op('sp', lambda e: e.dma_start(out=xb.ap[:, k, :], in_=self.x[t * 128:(t + 1) * 128, :]),
               w=[xb.k(k)], dma='xg%d_%d' % (g % 2, k))
            op('act', lambda e: e.activation(out=junk.ap, in_=xb.ap[:, k, :], func=AF.Square,
                                             accum_out=ss.ap[:, t:t + 1]), r=[xb.k(k)], w=[junk.k(), ss.k(t)])

        def sb(g):
            sl = slice(4 * g, 4 * g + 4)
            op('act', lambda e: e.activation(out=ms.ap[:, sl], in_=ss.ap[:, sl], func=AF.Sqrt, scale=1.0 / D, bias=self.epsc.ap),
               r=[ss.k(t) for t in range(4 * g, 4 * g + 4)] + [self.epsc.k()], w=[ms.k(g)])
            op('dve', lambda e: e.reciprocal(out=rstd.ap[:, sl], in_=ms.ap[:, sl]), r=[ms.k(g)], w=[rstd.k(g)])

        def sc(t):
            g, k = divmod(t, 4)
            xb = xg[g % 2]
            xn = xns[t % 3]
            op('dve', lambda e: e.tensor_scalar(out=xn.ap, in0=xb.ap[:, k, :], scalar1=rstd.ap[:, t:t + 1], scalar2=None,
                                                op0=ALU.mult), r=[xb.k(k), rstd.k(g)], w=[xn.k()])
            ba, bb = (0, 1) if t % 2 == 0 else (2, 3)
            pa = self.bank_bf(ba, 0, 256).rearrange("p (a b) -> p a b", b=128)
            pb = self.bank_bf(bb, 0, 256).rearrange("p (a b) -> p a b", b=128)

            def tr(e):
                for dc in range(8):
                    pt = pa if dc < 4 else pb
                    last = e.transpose(out=pt[:, dc % 4, :], in_=xn.ap[:, dc * 128:(dc + 1) * 128], identity=self.ident.ap)
                return last
            op('pe', tr, r=[xn.k(), self.ident.k()], w=[self.ps(ba), self.ps(bb)])

            def ev_act(e):
                for dc in range(4):
                    last = e.activation(out=hm.ap[:, dc, t * 128:(t + 1) * 128], in_=pa[:, dc, :], func=AF.Identity,
                                        scale=A_.ap[:, dc:dc + 1], bias=Bcol[:, dc:dc + 1])
                return last
            op('act', ev_act, r=[A_.k(), self.modc.k(0)], w=[self.ps(ba), hm.k(t, 0)])

            def ev_dve(e):
                for dc in range(4, 8):
                    last = e.tensor_scalar(out=hm.ap[:, dc, t * 128:(t + 1) * 128], in0=pb[:, dc - 4, :],
                                           scalar1=A_.ap[:, dc:dc + 1], scalar2=Bcol[:, dc:dc + 1],
                                           op0=ALU.mult, op1=ALU.add)
                return last
            op('dve', ev_dve, r=[A_.k(), self.modc.k(0)], w=[self.ps(bb), hm.k(t, 1)])

        for i in range(NT + 6):
            if i < NT:
                sa(i)
                if i % 4 == 3:
                    sb(i // 4)
            if 0 <= i - 6 < NT:
                sc(i - 6)
            for f in self.mod_rest.pop(i, []):
                f()
        for i in sorted(self.mod_rest):
            for f in self.mod_rest[i]:
                f()
        self.mod_rest = {}
        self.dump('hmixT', hm, sum([self.k2(hm, t) for t in range(NT)], []), [128, 8, self.S])

    def load_w(self, slot, blocks):
        dr = self.dr
        off = 0
        srcs = []
        for c0, n in blocks:
            srcs.append((off, n, dr['w_in'][:, c0:c0 + n].rearrange("(dc p) n -> p dc n", p=128)))
            off += n
        ws = self.wslots[slot]

        def ld(e):
            return [e.dma_start(out=ws.ap[:, :, o:o + n], in_=src) for o, n, src in srcs]
        self.op('pool', ld, w=[ws.k()], dma='wslot%d' % slot, ndma=len(srcs))

    def ret_head(self, r, slot):
        A, op, NT, S = self.alloc, self.op, self.NT, self.S
        mark = self.top
        ws = self.wslots[slot]
        hm = self.hmixT
        ropeR = A('ropeR', [128, NT, 2, 128], F32)
        npc = 4 if NT >= 4 else 1
        cpp = NT // npc
        for pc in range(npc):
            op('sp', lambda e, pc=pc: e.dma_start(out=ropeR.ap[:, pc * cpp:(pc + 1) * cpp],
                                                  in_=self.dr['ropeR'][:, pc * cpp:(pc + 1) * cpp]),
               w=[ropeR.k(pc)], dma='ropeR%d' % pc)
        DP = 6
        tq = [[A('tq%d_%d' % (i, j), [128, 128], F32) for j in range(2)] for i in range(2)]
        tk = [[A('tk%d_%d' % (i, j), [128, 128], F32) for j in range(2)] for i in range(2)]
        qt = [A('qt%d' % i, [128, 128], BF16) for i in range(3)]
        kt = [A('kt%d' % i, [128, 128], BF16) for i in range(3)]
        kp = [A('kp%d' % i, [128, 128], BF16) for i in range(DP)]
        vb = [A('vb%d' % i, [128, 256], BF16) for i in range(DP)]
        qkT = [A('qkT%d' % i, [128, 256], BF16) for i in range(DP)]
        scm = [A('scm%d' % i, [128, 128], BF16) for i in range(DP)]
        St = A('St', [128, 256], F32)
        Sbf = [A('Sbf%d' % i, [128, 256], BF16) for i in range(4)]
        junk = A('rjunk', [128, 256], BF16)
        ssr = A('ssr', [128, NT], F32)
        msr = A('msr', [128, NT], F32)
        rsr = A('rsr', [128, NT], F32)
        g = 1.0 - 2.0 ** (-5.0 - r)
        gC = float(np.exp(128.0 * np.log(g)))
        idec = self.retc.ap[:, 2 * r:2 * r + 1]
        kdec = self.retc.ap[:, 2 * r + 1:2 * r + 2]
        orr = self.orr
        mask = self.retmask

        def swapped(ap):
            a = [list(x) for x in ap.ap]
            return bass.AP(ap.tensor, ap.offset + 64, [a[0], [-64, 2], [1, 64]])

        def st1(c):
            pb = c % 2

            def mm(e):
                for dc in range(8):
                    last = e.matmul(self.banks[pb][:, 0:512], lhsT=hm.ap[:, dc, c * 128:(c + 1) * 128],
                                    rhs=ws.ap[:, dc, 0:512], start=(dc == 0), stop=(dc == 7))
                return last
            op('pe', mm, r=self.k2(hm, c) + [ws.k()], w=[self.ps(pb)])

        def st2a(c):
            pb = c % 2
            b = c % 2
            P = self.banks[pb]
            rk = ropeR.k(c // cpp)
            cc = ropeR.ap[:, c, 0, :]
            sn = ropeR.ap[:, c, 1, :]
            h3 = lambda ap: ap.rearrange("p (h d) -> p h d", h=2)
            for (src, dec, tt, dst) in ((P[:, 0:128], idec, tq[b], qt[c % 3]), (P[:, 128:256], kdec, tk[b], kt[c % 3])):
                op('dve', lambda e, src=src, dec=dec, tt=tt: e.scalar_tensor_tensor(
                    out=tt[0].ap, in0=src, scalar=dec, in1=cc, op0=ALU.mult, op1=ALU.mult),
                   r=[rk, self.retc.k()], w=[self.ps(pb), tt[0].k()])
                op('dve', lambda e, src=src, dec=dec, tt=tt: e.scalar_tensor_tensor(
                    out=h3(tt[1].ap), in0=swapped(src), scalar=dec, in1=h3(sn), op0=ALU.mult, op1=ALU.mult),
                   r=[rk, self.retc.k()], w=[self.ps(pb), tt[1].k()])
                op('pool', lambda e, tt=tt, dst=dst: e.tensor_tensor(out=dst.ap, in0=tt[0].ap, in1=tt[1].ap, op=ALU.add),
                   r=[tt[0].k(), tt[1].k()], w=[dst.k()])
            v_, kp_, kt_ = vb[c % DP], kp[c % DP], kt[c % 3]
            op('act', lambda e: e.activation(out=v_.ap, in_=P[:, 256:512], func=AF.Copy), w=[self.ps(pb), v_.k()])
            op('act', lambda e: e.activation(out=kp_.ap, in_=kt_.ap, func=AF.Copy, scale=gC), r=[kt_.k()], w=[kp_.k()])

        def st2b(c):
            q_, k_, T_ = qt[c % 3], kt[c % 3], qkT[c % DP]
            pT = self.bank_bf(4, 0, 128)

            def tr(e):
                e.transpose(out=pT[:, 0:128], in_=q_.ap, identity=self.ident.ap)
                return e.transpose(out=pT[:, 128:256], in_=k_.ap, identity=self.ident.ap)
            op('pe', tr, r=[q_.k(), k_.k(), self.ident.k()], w=[self.ps(4)])
            op('act', lambda e: e.activation(out=T_.ap, in_=pT, func=AF.Copy), w=[self.ps(4), T_.k()])

        def st3a(c):
            T_, sc_, v_, kp_ = qkT[c % DP], scm[c % DP], vb[c % DP], kp[c % DP]
            op('pe', lambda e: e.matmul(self.banks[5][:, 0:128], lhsT=T_.ap[:, 128:256], rhs=T_.ap[:, 0:128],
                                        start=True, stop=True), r=[T_.k()], w=[self.ps(5)])
            op('dve', lambda e: e.tensor_tensor(out=sc_.ap, in0=self.banks[5][:, 0:128], in1=mask.ap, op=ALU.mult),
               r=[mask.k()], w=[self.ps(5), sc_.k()])
            if c < NT - 1:
                kb = 3 if c % 2 == 0 else 7
                op('pe', lambda e: e.matmul(self.banks[kb][:, 0:256], lhsT=kp_.ap, rhs=v_.ap, start=True, stop=True),
                   r=[kp_.k(), v_.k()], w=[self.ps(kb)])
                if c == 0:
                    op('dve', lambda e: e.tensor_copy(out=St.ap, in_=self.banks[kb][:, 0:256]), w=[self.ps(kb), St.k()])
                else:
                    op('dve', lambda e: e.scalar_tensor_tensor(out=St.ap, in0=St.ap, scalar=gC, in1=self.banks[kb][:, 0:256],
                                                               op0=ALU.mult, op1=ALU.add),
                       r=[St.k()], w=[self.ps(kb), St.k()])
                sb_ = Sbf[(c + 1) % 4]
                op('act', lambda e: e.activation(out=sb_.ap, in_=St.ap, func=AF.Copy), r=[St.k()], w=[sb_.k()])

        def st3b(c):
            T_, sc_, v_ = qkT[c % DP], scm[c % DP], vb[c % DP]
            yb = 2 if c % 2 == 0 else 6
            sb_ = Sbf[c % 4]

            def ymm(e):
                last = e.matmul(self.banks[yb][:, 0:256], lhsT=sc_.ap, rhs=v_.ap, start=True, stop=(c == 0))
                if c > 0:
                    last = e.matmul(self.banks[yb][:, 0:256], lhsT=T_.ap[:, 0:128], rhs=sb_.ap, start=False, stop=True)
                return last
            op('pe', ymm, r=[sc_.k(), v_.k(), T_.k()] + ([sb_.k()] if c > 0 else []), w=[self.ps(yb)])
            Y = self.banks[yb][:, 0:256]
            op('act', lambda e: e.activation(out=junk.ap, in_=Y, func=AF.Square, accum_out=ssr.ap[:, c:c + 1]),
               w=[self.ps(yb), junk.k(), ssr.k(c)])
            op('act', lambda e: e.activation(out=msr.ap[:, c:c + 1], in_=ssr.ap[:, c:c + 1], func=AF.Sqrt,
                                             scale=1.0 / 256, bias=self.epsc.ap), r=[ssr.k(c), self.epsc.k()], w=[msr.k(c)])
            op('dve', lambda e: e.reciprocal(out=rsr.ap[:, c:c + 1], in_=msr.ap[:, c:c + 1]), r=[msr.k(c)], w=[rsr.k(c)])
            op('dve', lambda e: e.scalar_tensor_tensor(out=orr.ap[:, c, :], in0=Y, scalar=rsr.ap[:, c:c + 1],
                                                       in1=self.gret_b.ap, op0=ALU.mult, op1=ALU.mult),
               r=[rsr.k(c), self.gret_b.k()], w=[self.ps(yb), orr.k(c)])

        for i in range(NT + 6):
            if 0 <= i - 1 < NT:
                st2a(i - 1)
            if i < NT:
                st1(i)
            if 0 <= i - 2 < NT:
                st2b(i - 2)
            if 0 <= i - 3 < NT:
                st3a(i - 3)
            if 0 <= i - 5 < NT:
                st3b(i - 5)
        self.dump('orr%d' % r, orr, [orr.k(c) for c in range(NT)], [128, NT, 256])
        self.top = mark

    def da_head(self, h, slot):
        A, op, NT, S = self.alloc, self.op, self.NT, self.S
        NQ = S // 512
        mark = self.top
        ws = self.wslots[slot]
        hm = self.hmixT
        ropeA = self.ropeA
        QKT = A('QKT', [128, 2, S], BF16)
        V1 = A('V1', [128, NT, 129], BF16)
        exps = [A('exp%d' % i, [128, 2, 512], BF16) for i in range(4)]
        ss4 = A('ss4', [128, NT, 4], F32)
        ms4 = A('ms4', [128, NT, 4], F32)
        rs4 = A('rs4', [128, NT, 4], F32)
        markP = self.top
        sqj = [A('sqj%d' % i, [128, 256], F32) for i in range(2)]
        u = [A('u%d' % i, [128, 256], F32) for i in range(8)]
        t1 = [A('t1%d' % i, [128, 256], F32) for i in range(2)]
        t2 = [A('t2%d' % i, [128, 256], F32) for i in range(2)]
        o_ = [A('o%d' % i, [128, 256], F32) for i in range(2)]
        qk = [A('qk%d' % i, [128, 256], BF16) for i in range(4)]
        orr = self.orr
        hh = h % 2

        op('pool', lambda e: e.memset(V1.ap[:, :, 128:129], 1.0), w=[V1.k('ones')])

        g4 = lambda ap: ap.rearrange("p (g d) -> p g d", g=4)
        g42 = lambda ap: ap.rearrange("p (g h d) -> p g h d", g=4, h=2)

        def swapped(ap):
            a = [list(x) for x in ap.ap]
            return bass.AP(ap.tensor, ap.offset + 32, [a[0], [64, 4], [-32, 2], [1, 32]])

        def p1(t):
            pb = t % 2

            def mm(e):
                for dc in range(8):
                    last = e.matmul(self.banks[pb][:, 0:384], lhsT=hm.ap[:, dc, t * 128:(t + 1) * 128],
                                    rhs=ws.ap[:, dc, 0:384], start=(dc == 0), stop=(dc == 7))
                return last
            op('pe', mm, r=self.k2(hm, t) + [ws.k()], w=[self.ps(pb)])

        def p2a(t):
            pb = t % 2
            P = self.banks[pb]
            sq, uu_ = sqj[t % 2], u[t % 8]
            op('act', lambda e: e.activation(out=sq.ap, in_=P[:, 0:256], func=AF.Square), w=[self.ps(pb), sq.k()])
            op('dve', lambda e: e.tensor_tensor(out=uu_.ap, in0=P[:, 0:256], in1=self.gqk_b.ap, op=ALU.mult),
               r=[self.gqk_b.k()], w=[self.ps(pb), uu_.k()])
            op('act', lambda e: e.activation(out=V1.ap[:, t, 0:128], in_=P[:, 256:384], func=AF.Copy),
               w=[self.ps(pb), V1.k(t)])
            op('dve', lambda e: e.tensor_reduce(out=ss4.ap[:, t, :], in_=g4(sq.ap), axis=AX.X, op=ALU.add),
               r=[sq.k()], w=[ss4.k(t)])
            op('act', lambda e: e.activation(out=ms4.ap[:, t, :], in_=ss4.ap[:, t, :], func=AF.Sqrt, scale=1.0 / 64,
                                             bias=self.epsc.ap), r=[ss4.k(t), self.epsc.k()], w=[ms4.k(t)])
            op('dve', lambda e: e.reciprocal(out=rs4.ap[:, t, :], in_=ms4.ap[:, t, :]), r=[ms4.k(t)], w=[rs4.k(t)])

        def p2c(t):
            b = t % 2
            uu_ = u[t % 8]
            cc = ropeA.ap[:, t, 0:1, :].broadcast_to([128, 4, 64])
            sn = ropeA.ap[:, t, 1, :].rearrange("p (h d) -> p h d", h=2).unsqueeze(1).broadcast_to([128, 4, 2, 32])
            op('pool', lambda e: e.tensor_tensor(out=g4(t1[b].ap), in0=g4(uu_.ap), in1=cc, op=ALU.mult),
               r=[uu_.k(), ropeA.k()], w=[t1[b].k()])
            op('dve', lambda e: e.tensor_tensor(out=g42(t2[b].ap), in0=swapped(uu_.ap), in1=sn, op=ALU.mult),
               r=[uu_.k(), ropeA.k()], w=[t2[b].k()])
            op('pool', lambda e: e.tensor_tensor(out=o_[b].ap, in0=t1[b].ap, in1=t2[b].ap, op=ALU.add),
               r=[t1[b].k(), t2[b].k()], w=[o_[b].k()])
            rb = rs4.ap[:, t, :].unsqueeze(2).broadcast_to([128, 4, 64])
            q_ = qk[t % 4]
            op('dve', lambda e: e.tensor_tensor(out=g4(q_.ap), in0=g4(o_[b].ap), in1=rb, op=ALU.mult),
               r=[o_[b].k(), rs4.k(t)], w=[q_.k()])

        def p2d(t):
            q_ = qk[t % 4]
            tb = 2 + (t % 2)
            pT = self.bank_bf(tb, 0, 128).rearrange("p (a b) -> p a b", b=128)

            def tr(e):
                e.transpose(out=pT[:, 0, :], in_=q_.ap[:, 0:128], identity=self.ident.ap)
                return e.transpose(out=pT[:, 1, :], in_=q_.ap[:, 128:256], identity=self.ident.ap)
            op('pe', tr, r=[q_.k(), self.ident.k()], w=[self.ps(tb)])
            op('act', lambda e: e.activation(out=QKT.ap[:, :, t * 128:(t + 1) * 128], in_=pT, func=AF.Copy),
               w=[self.ps(tb), QKT.k(t)])

        for i in range(NT + 9):
            if 0 <= i - 1 < NT:
                p2a(i - 1)
            if i < NT:
                p1(i)
            if 0 <= i - 6 < NT:
                p2c(i - 6)
            if 0 <= i - 8 < NT:
                p2d(i - 8)
        self.dump('QKT%d' % h, QKT, [QKT.k(t) for t in range(NT)], [128, 2, S])
        self.dump('V1_%d' % h, V1, [V1.k(t) for t in range(NT)] + [V1.k('ones')], [128, NT, 129])

        self.top = markP
        accS = [A('accS%d' % i, [128, 8, 129], F32) for i in range(4)]
        rr = A('rr', [128, NT, 4], F32)
        sso = A('sso', [128, NT], F32)
        mso = A('mso', [128, NT], F32)
        rso = A('rso', [128, NT], F32)
        tt = [A('tt%d' % i, [128, 128], F32) for i in range(2)]
        oo = [A('oo%d' % i, [128, 128], F32) for i in range(2)]
        ojk = A('ojk', [128, 128], F32)
        on = [A('on%d' % i, [128, 128], F32) for i in range(2)]
        tg = [A('tg%d' % i, [128, 384], F32) for i in range(2)]
        uu = [A('uu%d' % i, [128, 128], F32) for i in range(2)]
        u2 = [A('u2%d' % i, [128, 128], F32) for i in range(2)]
        u3 = [A('u3%d' % i, [128, 128], F32) for i in range(2)]
        m1 = [A('m1%d' % i, [128, 128], F32) for i in range(2)]
        mg = [A('mg%d' % i, [128, 128], BF16) for i in range(2)]
        mst = [A('mst%d' % i, [128, 512], BF16) for i in range(4)]
        acc_loc = [(4 + i // 3, (i % 3) * 129) for i in range(8)]

        def acc_ap(qb, m):
            bk, off = acc_loc[qb * 2 + m]
            return self.banks[bk][:, off:off + 129]

        def sS(j, kt, step):
            i = kt - 4 * j
            c0 = 128 * i if i > 0 else 0
            sa, sb_ = (0, 1) if step % 2 == 0 else (2, 3)

            def mm(e):
                e.matmul(self.banks[sa][:, c0:512], lhsT=QKT.ap[0:64, 1, kt * 128:(kt + 1) * 128],
                         rhs=QKT.ap[0:64, 0, j * 512 + c0:(j + 1) * 512], start=True, stop=True)
                return e.matmul(self.banks[sb_][:, c0:512], lhsT=QKT.ap[64:128, 1, kt * 128:(kt + 1) * 128],
                                rhs=QKT.ap[64:128, 0, j * 512 + c0:(j + 1) * 512], start=True, stop=True)
            rk = [QKT.k(kt)] + [QKT.k(4 * j + q) for q in range(4)]
            op('pe', mm, r=rk, w=[self.ps(sa), self.ps(sb_)])
            ex = exps[step % 4]
            src = self.pbig[:, sa * 512:(sa + 2) * 512].rearrange("p (m c) -> p m c", m=2)[:, :, c0:512]
            op('act', lambda e: e.activation(out=ex.ap[:, :, c0:512], in_=src, func=AF.Exp, scale=0.125),
               w=[self.ps(sa), self.ps(sb_), ex.k()])
            if i >= 0:
                op('pool', lambda e: e.memset(ex.ap[64:128, :, c0:c0 + 64], 0.0), w=[ex.k()])

        def sV(j, kt, step):
            i = kt - 4 * j
            ex = exps[step % 4]
            qb0 = max(i, 0)

            def mm(e):
                for qb in range(qb0, 4):
                    for m in range(2):
                        idx = qb * 2 + m
                        last = e.matmul(acc_ap(qb, m), lhsT=ex.ap[:, m, qb * 128:(qb + 1) * 128], rhs=V1.ap[:, kt, :],
                                        start=(kt == 0 and idx % 3 == 0), stop=(kt == 4 * j + qb),
                                        skip_group_check=True)
                return last
            op('pe', mm, r=[ex.k(), V1.k(kt), V1.k('ones')], w=[self.ps(4), self.ps(5), self.ps(6)])

        def merge_steps(j):
            ab = accS[j % 4]
            av = ab.ap
            flat = av.rearrange("p i c -> p (i c)")
            op('dve', lambda e: e.tensor_copy(out=flat[:, 0:387], in_=self.banks[4][:, 0:387]), w=[self.ps(4), ab.k(0)])
            op('dve', lambda e: e.tensor_copy(out=flat[:, 387:774], in_=self.banks[5][:, 0:387]), w=[self.ps(5), ab.k(1)])
            op('dve', lambda e: e.tensor_copy(out=flat[:, 774:1032], in_=self.banks[6][:, 0:258]), w=[self.ps(6), ab.k(2)])
            ms_ = mst[j % 4]
            steps = []
            stages = []
            for qb in range(4):
                tb = 4 * j + qb
                b = qb % 2

                def M1(qb=qb, tb=tb, b=b):
                    op('dve', lambda e: e.reciprocal(out=rr.ap[:, tb, 0:2], in_=av[:, 2 * qb:2 * qb + 2, 128]),
                       r=[ab.k(0), ab.k(1), ab.k(2)], w=[rr.k(tb, 0)])
                    op('dve', lambda e: e.tensor_tensor(out=rr.ap[:, tb, 2:3], in0=rr.ap[:, tb, 1:2], in1=self.lamc.ap,
                                                        op=ALU.mult), r=[rr.k(tb, 0), self.lamc.k()], w=[rr.k(tb, 1)])
                    op('dve', lambda e: e.tensor_scalar(out=tt[b].ap, in0=av[:, 2 * qb + 1, 0:128],
                                                        scalar1=rr.ap[:, tb, 2:3], scalar2=None, op0=ALU.mult),
                       r=[ab.k(0), ab.k(1), ab.k(2), rr.k(tb, 1)], w=[tt[b].k()])
                    op('dve', lambda e: e.scalar_tensor_tensor(out=oo[b].ap, in0=av[:, 2 * qb, 0:128],
                                                               scalar=rr.ap[:, tb, 0:1], in1=tt[b].ap,
                                                               op0=ALU.mult, op1=ALU.subtract),
                       r=[ab.k(0), ab.k(1), ab.k(2), rr.k(tb, 0), tt[b].k()], w=[oo[b].k()])
                    op('dve', lambda e: e.scalar_tensor_tensor(out=ojk.ap, in0=oo[b].ap, scalar=1.0, in1=oo[b].ap,
                                                               op0=ALU.mult, op1=ALU.mult,
                                                               accum_out=sso.ap[:, tb:tb + 1]),
                       r=[oo[b].k()], w=[ojk.k(), sso.k(tb)])
                    op('pool', lambda e: e.tensor_scalar(out=mso.ap[:, tb:tb + 1], in0=sso.ap[:, tb:tb + 1],
                                                         scalar1=1.0 / 128, scalar2=EPS, op0=ALU.mult, op1=ALU.add),
                       r=[sso.k(tb)], w=[mso.k(tb)])
                    op('pool', lambda e: e.tensor_tensor(out=rso.ap[:, tb:tb + 1], in0=mso.ap[:, tb:tb + 1],
                                                         in1=self.mhalf.ap[:, 0:1], op=ALU.pow),
                       r=[mso.k(tb), self.mhalf.k()], w=[rso.k(tb)])
                    op('dve', lambda e: e.scalar_tensor_tensor(out=on[b].ap, in0=oo[b].ap, scalar=rso.ap[:, tb:tb + 1],
                                                               in1=self.gda_b.ap, op0=ALU.mult, op1=ALU.mult),
                       r=[oo[b].k(), rso.k(tb), self.gda_b.k()], w=[on[b].k()])

                def M2(qb=qb, tb=tb, b=b):
                    def mm(e):
                        for dc in range(8):
                            last = e.matmul(self.banks[7][:, 0:384], lhsT=hm.ap[:, dc, tb * 128:(tb + 1) * 128],
                                            rhs=ws.ap[:, dc, 384:768], start=(dc == 0), stop=(dc == 7))
                        return last
                    op('pe', mm, r=self.k2(hm, tb) + [ws.k()], w=[self.ps(7)])
                    op('act', lambda e: e.activation(out=tg[b].ap, in_=self.banks[7][:, 0:384], func=AF.Tanh, scale=0.5),
                       w=[self.ps(7), tg[b].k()])
                    op('dve', lambda e: e.scalar_tensor_tensor(out=uu[b].ap, in0=tg[b].ap[:, 256:384], scalar=1.0,
                                                               in1=self.banks[7][:, 256:384], op0=ALU.add, op1=ALU.mult),
                       r=[tg[b].k()], w=[self.ps(7), uu[b].k()])

                def M3(qb=qb, tb=tb, b=b):
                    op('pool', lambda e: e.tensor_tensor(out=u2[b].ap, in0=uu[b].ap,
                                                         in1=orr.ap[:, tb, hh * 128:(hh + 1) * 128], op=ALU.mult),
                       r=[uu[b].k(), orr.k(tb)], w=[u2[b].k()])
                    op('dve', lambda e: e.scalar_tensor_tensor(out=u3[b].ap, in0=tg[b].ap[:, 128:256], scalar=1.0,
                                                               in1=u2[b].ap, op0=ALU.add, op1=ALU.mult),
                       r=[tg[b].k(), u2[b].k()], w=[u3[b].k()])
                    op('dve', lambda e: e.scalar_tensor_tensor(out=m1[b].ap, in0=tg[b].ap[:, 0:128], scalar=1.0,
                                                               in1=on[b].ap, op0=ALU.add, op1=ALU.mult),
                       r=[tg[b].k(), on[b].k()], w=[m1[b].k()])
                    op('pool', lambda e: e.tensor_tensor(out=mg[b].ap, in0=m1[b].ap, in1=u3[b].ap, op=ALU.add),
                       r=[m1[b].k(), u3[b].k()], w=[mg[b].k()])

                def M3b(qb=qb, tb=tb, b=b):
                    pT = self.bank_bf(7, 384, 448)
                    op('pe', lambda e: e.transpose(out=pT, in_=mg[b].ap, identity=self.ident.ap),
                       r=[mg[b].k(), self.ident.k()], w=[self.ps(7)])
                    op('dve', lambda e: e.tensor_copy(out=ms_.ap[:, qb * 128:(qb + 1) * 128], in_=pT),
                       w=[self.ps(7), ms_.k(qb)])
                stages.append((M1, M2, M3, M3b))
            for slot in range(4 + 3):
                for k in (3, 2, 1, 0):
                    if 0 <= slot - k < 4:
                        steps.append(stages[slot - k][k])

            def M4():
                op('sp', lambda e: e.dma_start(out=self.mT_d[h, :, j * 512:(j + 1) * 512], in_=ms_.ap),
                   r=[ms_.k(q) for q in range(4)], w=[('mT_d', h, j)], dma='mst%d' % (j % 4))
            steps.append(M4)
            return steps

        step = 0
        for j in range(NQ):
            nk = 4 * j + 4
            per = 1 if len(self.deferred) <= 2 * nk + 17 else 2
            sS(j, 0, step)
            sS(j, 1, step + 1)
            for kt in range(nk):
                if kt + 2 < nk:
                    sS(j, kt + 2, step + 2)
                sV(j, kt, step)
                step += 1
                self.pump(per)
            while len(self.deferred) > 36:
                self.pump(1)
            self.deferred = self.deferred + merge_steps(j)
            if self.bg:
                self.bg.pop(0)()
        self.flush()
        self.top = mark

    def tail(self):
        A, op, S, dr = self.alloc, self.op, self.S, self.dr
        NTT = S // 512
        self.top = self.persist_top
        G1b = A('G1b', [128, D], F32)
        G2b = A('G2b', [128, D], F32)
        wout = A('wout', [128, 8, D], BF16)
        diag = [A('diag%d' % i, [128, 128], F32) for i in range(2)]
        x1 = [A('x1_%d' % i, [128, 4, D], F32) for i in range(2)]
        mTt = [A('mTt%d' % i, [128, 8, 512], BF16) for i in range(2)]
        hffT = A('hffT', [128, 8, 512], BF16)
        hT = A('hT', [128, 32, 512], BF16)
        xn2 = [A('xn2_%d' % i, [128, D], BF16) for i in range(2)]
        rl = [A('rl%d' % i, [128, 512], F32) for i in range(2)]
        yg = [A('yg%d' % i, [128, D], F32) for i in range(2)]
        wupc = [A('wupc%d' % i, [128, 8, 512], BF16) for i in range(3)]
        wdnc = [A('wdnc%d' % i, [128, 4, 512], BF16) for i in range(4)]
        junk = A('tjunk', [128, D], BF16)
        ss = A('ss2', [128, S // 128], F32)
        ms = A('ms2', [128, S // 128], F32)
        rstd = A('rstd2', [128, S // 128], F32)

        src = dr['w_out'].rearrange("(g p) n -> p g n", p=128)

        def ldw(e):
            return [e.dma_start(out=wout.ap[:, g0:g0 + 4, :], in_=src[:, g0:g0 + 4, :]) for g0 in (0, 4)]
        op('pool', ldw, w=[wout.k()], dma='wout', ndma=2)

        for gi, (G, c0) in enumerate(((G1b, 16), (G2b, 40))):
            for half in range(2):
                for q in range(4):
                    dc = half * 4 + q
                    dg = diag[dc % 2]
                    op('dve', lambda e, dg=dg, dc=dc, c0=c0: e.tensor_scalar(
                        out=dg.ap, in0=self.identf.ap, scalar1=self.modc.ap[:, c0 + dc:c0 + dc + 1], scalar2=None,
                        op0=ALU.mult), r=[self.identf.k(), self.modc.k(1)], w=[dg.k()])
                    op('pe', lambda e, dg=dg, q=q, half=half: e.matmul(
                        self.banks[half][:, q * 128:(q + 1) * 128], lhsT=self.onesf.ap, rhs=dg.ap, start=True, stop=True),
                       r=[self.onesf.k(), dg.k()], w=[self.ps(half)])
                op('act', lambda e, G=G, half=half: e.activation(out=G.ap[:, half * 512:(half + 1) * 512],
                                                                 in_=self.banks[half][:, 0:512], func=AF.Copy),
                   w=[self.ps(half), G.k(half)])

        sctr = {'up': 0, 'dn': 0}

        def load_tile(i):
            xb = x1[i % 2]
            xsrc = self.x[i * 512:(i + 1) * 512, :].rearrange("(s p) d -> p s d", p=128)
            op('pool', lambda e: e.dma_start(out=xb.ap, in_=xsrc), w=[xb.k(s) for s in range(4)], dma='x1_%d' % (i % 2))
            mt = mTt[i % 2]
            msrc = self.mT_d[:, :, i * 512:(i + 1) * 512].rearrange("g p t -> p g t")
            op('pool', lambda e: e.dma_start(out=mt.ap, in_=msrc), r=[('mT_d', g, i) for g in range(8)],
               w=[mt.k()], dma='mTt%d' % (i % 2))

        def O_steps(i):
            xb = x1[i % 2]
            mt = mTt[i % 2]
            steps = []
            for s in range(4):
                def Oa(s=s):
                    def mm(e):
                        for nb in range(2):
                            for g in range(8):
                                last = e.matmul(self.banks[6 + nb][:, 0:512], lhsT=mt.ap[:, g, s * 128:(s + 1) * 128],
                                                rhs=wout.ap[:, g, nb * 512:(nb + 1) * 512], start=(g == 0), stop=(g == 7))
                        return last
                    op('pe', mm, r=[mt.k(), wout.k()], w=[self.ps(6), self.ps(7)])
                    y = yg[s % 2]
                    for nb in range(2):
                        op('dve', lambda e, nb=nb: e.tensor_tensor(out=y.ap[:, nb * 512:(nb + 1) * 512],
                                                                    in0=self.banks[6 + nb][:, 0:512],
                                                                    in1=G1b.ap[:, nb * 512:(nb + 1) * 512], op=ALU.mult),
                           r=[G1b.k(nb)], w=[self.ps(6 + nb), y.k(nb)])
                    op('pool', lambda e: e.tensor_tensor(out=xb.ap[:, s, :], in0=y.ap, in1=xb.ap[:, s, :], op=ALU.add),
                       r=[y.k(0), y.k(1), xb.k(s)], w=[xb.k(s)])

                def Ob(s=s):
                    t = i * 4 + s
                    self.norm_to_T(xb.ap[:, s, :], xb.k(s), t, (ss, ms, rstd, junk), xn2[s % 2], self.A2,
                                   self.modc.ap[:, 24:32],
                                   lambda lo, hi: hffT.ap[:, lo:hi, s * 128:(s + 1) * 128], self.k2(hffT, s), (6, 7), 'n2')
                steps += [Oa, Ob]
            return steps

        def U(i):
            for c in range(8):
                wb = wupc[sctr['up'] % 3]
                sctr['up'] += 1
                op('sp', lambda e, wb=wb, c=c: e.dma_start(out=wb.ap, in_=self.wup_s[c]), r=[('wup_s',)], w=[wb.k()],
                   dma=wb.name)
                for fl in range(4):
                    fb = 4 * c + fl
                    bk = 4 + fb % 2

                    def mm(e, fl=fl, bk=bk, wb=wb):
                        for dc in range(8):
                            last = e.matmul(self.banks[bk][:, 0:512], lhsT=wb.ap[:, dc, fl * 128:(fl + 1) * 128],
                                            rhs=hffT.ap[:, dc, :], start=(dc == 0), stop=(dc == 7))
                        return last
                    op('pe', mm, r=[wb.k()] + sum([self.k2(hffT, s) for s in range(4)], []), w=[self.ps(bk)])
                    r_ = rl[fb % 2]
                    op('act', lambda e, bk=bk, r_=r_: e.activation(out=r_.ap, in_=self.banks[bk][:, 0:512], func=AF.Relu),
                       w=[self.ps(bk), r_.k()])
                    eng = 'dve' if fb % 2 == 0 else 'pool'
                    op(eng, lambda e, r_=r_, fb=fb: e.tensor_tensor(out=hT.ap[:, fb, :], in0=r_.ap, in1=r_.ap, op=ALU.mult),
                       r=[r_.k()], w=[hT.k(fb)])

        def Dn(i):
            xb = x1[i % 2]
            for nb in range(2):
                for c in range(8):
                    wb = wdnc[sctr['dn'] % 4]
                    sctr['dn'] += 1
                    op('sp', lambda e, wb=wb, c=c, nb=nb: e.dma_start(out=wb.ap, in_=self.wdn_s[nb, c]),
                       r=[('wdn_s',)], w=[wb.k()], dma=wb.name)

                    def mm(e, wb=wb, c=c):
                        for s in range(4):
                            for fl in range(4):
                                last = e.matmul(self.banks[s][:, 0:512], lhsT=hT.ap[:, 4 * c + fl, s * 128:(s + 1) * 128],
                                                rhs=wb.ap[:, fl, :], start=(c == 0 and fl == 0), stop=(c == 7 and fl == 3))
                        return last
                    op('pe', mm, r=[wb.k()] + [hT.k(4 * c + fl) for fl in range(4)], w=[self.ps(s) for s in range(4)])
                    self.pump(1)
                for s in range(4):
                    y = yg[s % 2]
                    op('dve', lambda e, s=s, y=y, nb=nb: e.tensor_tensor(out=y.ap[:, 0:512], in0=self.banks[s][:, 0:512],
                                                                         in1=G2b.ap[:, nb * 512:(nb + 1) * 512], op=ALU.mult),
                       r=[G2b.k(nb)], w=[self.ps(s), y.k(0)])
                    op('pool', lambda e, s=s, y=y, nb=nb: e.tensor_tensor(
                        out=xb.ap[:, s, nb * 512:(nb + 1) * 512], in0=y.ap[:, 0:512],
                        in1=xb.ap[:, s, nb * 512:(nb + 1) * 512], op=ALU.add), r=[y.k(0), xb.k(s)], w=[xb.k(s)])
            dst = self.out[i * 512:(i + 1) * 512, :].rearrange("(s p) d -> p s d", p=128)
            op('pool', lambda e: e.dma_start(out=dst, in_=xb.ap), r=[xb.k(s) for s in range(4)], w=[('out', i)],
               dma='ost%d' % (i % 2))
            self.outkeys.append(('out', i))

        load_tile(0)
        for f in O_steps(0):
            f()
        for i in range(NTT):
            if i + 1 < NTT:
                load_tile(i + 1)
                self.deferred = O_steps(i + 1)
            U(i)
            Dn(i)
            self.flush()

    def build(self, upto='all'):
        self.setup()
        A = self.alloc
        self.hmixT = A('hmixT', [128, 8, self.S], BF16)
        self.orr = A('orr', [128, self.NT, 256], BF16)
        self.ropeA = A('ropeA', [128, self.NT, 2, 64], F32)
        self.wslots = [A('wslot0', [128, 8, 768], BF16), A('wslot1', [128, 8, 768], BF16)]
        self.mixer_top = self.top
        self.bg = []
        phases = []
        for r in range(4):
            phases += [('ret', r), ('da', 2 * r), ('da', 2 * r + 1)]
        if upto == 'ret0':
            phases = phases[:1]
        elif upto == 'da0':
            phases = phases[:2]

        def wblocks(ph):
            kind, i = ph
            if kind == 'ret':
                return [(C_QR + i * 128, 128), (C_KR + i * 128, 128), (C_VR + i * 256, 256)]
            return [(C_QA + i * 128, 128), (C_KA + i * 128, 128), (C_VA + i * 128, 128),
                    (C_GA + i * 128, 128), (C_GB + i * 128, 128), (C_GR + i * 128, 128)]
        self.phase_mod()
        self.load_w(0, wblocks(phases[0]))
        self.op('sp', lambda e: e.dma_start(out=self.ropeA.ap, in_=self.dr['ropeA']), w=[self.ropeA.k()], dma='ropeA')
        self.phase_norm1()
        self.top = self.mixer_top
        if upto == 'norm1':
            return self.finish()
        self.ffn_scratch()
        for n, ph in enumerate(phases):
            if n + 1 < len(phases):
                self.load_w((n + 1) % 2, wblocks(phases[n + 1]))
            if ph[0] == 'ret':
                self.ret_head(ph[1], n % 2)
            else:
                self.da_head(ph[1], n % 2)
        while self.bg:
            self.bg.pop(0)()
        if upto != 'all':
            return self.finish()
        self.tail()
        return self.finish()

    def finish(self):
        self.flush()
        keys = list(self.outkeys)
        self.op('sp', lambda e: e.nop(), r=keys + [('wup_s',), ('wdn_s',)], sig=False)
        self.sch.run_block()


def build_program(S=4096, dbg=(), upto='all'):
    nc = bass.Bass("TRN2", target_bir_lowering=False)
    with ExitStack() as es:
        kb = KB(nc, es, S, dbg)
        kb.build(upto)
    return nc


def host_tables(S):
    NT = S // 128
    pos = np.arange(S, dtype=np.float32)
    f32 = np.float32
    invA = (10000.0 ** (-np.arange(0, 64, 2, dtype=f32) / f32(64))).astype(f32)
    angA = (pos[:, None] * invA[None, :]).astype(f32).astype(np.float64)
    cA, sA = np.cos(angA).astype(f32), np.sin(angA).astype(f32)
    ropeA = np.stack([np.concatenate([cA, cA], 1), np.concatenate([-sA, sA], 1)], 1)
    ropeA = np.ascontiguousarray(ropeA.reshape(NT, 128, 2, 64).transpose(1, 0, 2, 3))
    invR = (1.0 / (f32(10000.0) ** np.linspace(0.0, 1.0, 64, dtype=f32))).astype(f32)
    angR = (pos[:, None] * invR[None, :]).astype(f32).astype(np.float64)
    cR, sR = np.cos(angR).astype(f32), np.sin(angR).astype(f32)
    ropeR = np.stack([np.concatenate([cR, cR], 1), np.concatenate([-sR, sR], 1)], 1)
    ropeR = np.ascontiguousarray(ropeR.reshape(NT, 128, 2, 128).transpose(1, 0, 2, 3))
    n = np.arange(128, dtype=np.float64)
    retc = np.zeros((128, 8), f32)
    for r in range(4):
        lg = np.log(1.0 - 2.0 ** (-5.0 - r))
        retc[:, 2 * r] = np.exp((n + 1.0) * lg)
        retc[:, 2 * r + 1] = (128.0 ** -0.5) * np.exp(-(n + 1.0) * lg)
    jj, ii = np.meshgrid(np.arange(128), np.arange(128), indexing='ij')
    retmask = (ii >= jj).astype(f32)
    return dict(ropeA=ropeA, ropeR=ropeR, retc=retc, retmask=retmask, identf=np.eye(128, dtype=f32))


def core_inputs(b, S, inp, tabs):
    f = lambda a: np.ascontiguousarray(np.asarray(a, dtype=np.float32))
    col = lambda v, n: f(np.asarray(v, np.float32).reshape(n, 128).T)
    m = dict(tabs)
    m['x'] = f(inp['x'][b, :S])
    m['c_col'] = col(inp['c'][b], 8)
    m['w_ada'] = f(inp['w_ada'][0])
    m['b_ada_c'] = col(inp['b_ada'][0], 48)
    m['g1c'] = col(inp['g_norm1'][0], 8)
    m['g2c'] = col(inp['g_norm2'][0], 8)
    m['w_in'] = f(inp['w_in'][0])
    gq, gk = np.asarray(inp['g_q'][0], np.float32), np.asarray(inp['g_k'][0], np.float32)
    m['gqk_row'] = f(np.concatenate([gq, gq, gk, gk])[None, :])
    m['lam_row'] = f(np.concatenate([inp['lambda_q1'][0], inp['lambda_q2'][0],
                                     inp['lambda_k1'][0], inp['lambda_k2'][0]])[None, :])
    m['gda_row'] = f(np.asarray(inp['g_da_out'][0])[None, :])
    m['gret_row'] = f(np.asarray(inp['g_ret_out'][0])[None, :])
    m['w_out'] = f(inp['w_out'][0])
    m['w_up'] = f(inp['w_up'][0])
    m['w_down'] = f(inp['w_down'][0])
    return m


def kernel(**inputs):
    S = 4096
    inp = {k: np.asarray(v) for k, v in inputs.items()}
    B = inp['x'].shape[0]
    tabs = host_tables(S)
    nc = build_program(S)
    in_maps = [core_inputs(b, S, inp, tabs) for b in range(B)]
    res = run_bass_kernel_spmd(nc, in_maps, core_ids=list(range(B)))
    out = np.stack([np.asarray(res.results[b]['out']) for b in range(B)], 0)
    return out.astype(np.float32)
```

```python
import math
from contextlib import ExitStack

import numpy as np
import concourse.bass as bass
import concourse.mybir as mybir
from concourse.bass_utils import run_bass_kernel_spmd

F32 = mybir.dt.float32
BF16 = mybir.dt.bfloat16
AF = mybir.ActivationFunctionType
ALU = mybir.AluOpType
AX = mybir.AxisListType

D = 1024
DFF = 4096
EPS = 1e-6
LAMBDA_INIT = 0.2
C_QA, C_KA, C_VA, C_QR, C_KR, C_VR, C_GR, C_GA, C_GB = 0, 1024, 2048, 3072, 3584, 4096, 5120, 6144, 7168
ARENA_BYTES = 212736


class Sched:
    CE = ('pe', 'act', 'dve', 'pool')

    def __init__(self, nc, es):
        self.nc, self.es = nc, es
        self.q = {e: [] for e in ('pe', 'act', 'dve', 'pool', 'sp')}
        self.sem = {e: es.enter_context(nc.semaphore('s_' + e)) for e in self.CE}
        self.cnt = {e: 0 for e in self.CE}
        self.lastw, self.reads = {}, {}
        self.dsem = {}
        self.waited = {e: {} for e in self.q}
        self.bufs, self.bufkeys, self.inherit, self.known = [], {}, {}, set()

    @staticmethod
    def _merge(d, ev):
        k = id(ev[0])
        if k not in d or d[k][1] < ev[1]:
            d[k] = ev

    def new_buffer(self, name, lo, hi):
        inh = {}
        keep = []
        for (lo2, hi2, n2) in self.bufs:
            if lo2 < hi and lo < hi2:
                for k in self.bufkeys.get(n2, ()):
                    ev = self.lastw.get(k)
                    if ev is not None:
                        self._merge(inh, ev)
                    for ev in self.reads.get(k, {}).values():
                        self._merge(inh, ev)
                for ev in self.inherit.get(n2, {}).values():
                    self._merge(inh, ev)
                if lo <= lo2 and hi2 <= hi:
                    continue
            keep.append((lo2, hi2, n2))
        keep.append((lo, hi, name))
        self.bufs = keep
        self.inherit[name] = inh

    def _touch(self, k):
        if k in self.known:
            return
        self.known.add(k)
        name = k[0] if isinstance(k, tuple) else k
        self.bufkeys.setdefault(name, set()).add(k)
        self.reads[k] = dict(self.inherit.get(name, {}))

    def op(self, eng, fn, r=(), w=(), dma=None, ndma=1, sig=True):
        waits = {}

        def need(ev, raw):
            if ev is None:
                return
            sem, val, src = ev
            if src == eng and (not raw or eng == 'pe'):
                return
            k = id(sem)
            if self.waited[eng].get(k, 0) >= val:
                return
            if k not in waits or waits[k][1] < val:
                waits[k] = (sem, val)

        for k in r:
            self._touch(k)
            need(self.lastw.get(k), True)
        for k in w:
            self._touch(k)
            need(self.lastw.get(k), False)
            for ev in self.reads[k].values():
                need(ev, False)
        for k, (sem, val) in waits.items():
            self.waited[eng][k] = val
        if dma is not None:
            if dma not in self.dsem:
                self.dsem[dma] = [self.es.enter_context(self.nc.semaphore('d_' + dma)), 0]
            ds = self.dsem[dma]
            ds[1] += 16 * ndma
            ev = (ds[0], ds[1], 'dma')
            evh = ('dma', ds[0], ndma)
        elif sig:
            self.cnt[eng] += 1
            ev = (self.sem[eng], self.cnt[eng], eng)
            evh = ('ce', self.sem[eng])
        else:
            ev, evh = None, ('none',)
        self.q[eng].append((list(waits.values()), fn, evh))
        for k in r:
            if ev is not None:
                self._merge(self.reads[k], ev)
        for k in w:
            self.lastw[k] = ev
            self.reads[k] = {}
        return ev

    def emit(self, eng, e):
        for waits, fn, evh in self.q[eng]:
            for sem, val in waits:
                e.wait_ge(sem, val)
            ins = fn(e)
            if evh[0] == 'ce':
                ins.then_inc(evh[1], 1)
            elif evh[0] == 'dma':
                if not isinstance(ins, (list, tuple)):
                    ins = [ins]
                assert len(ins) == evh[2], (len(ins), evh[2])
                for i in ins:
                    i.then_inc(evh[1], 16)

    def run_block(self):
        with self.nc.Block() as block:
            @block.tensor
            def _(e):
                self.emit('pe', e)

            @block.scalar
            def _(e):
                self.emit('act', e)

            @block.vector
            def _(e):
                self.emit('dve', e)

            @block.gpsimd
            def _(e):
                self.emit('pool', e)

            @block.sync
            def _(e):
                self.emit('sp', e)


class Buf:
    def __init__(self, name, ap):
        self.name, self.ap = name, ap

    def k(self, *sub):
        return (self.name,) + sub


class KB:
    def __init__(self, nc, es, S, dbg=()):
        self.nc, self.es, self.S, self.dbg = nc, es, S, set(dbg)
        self.NT = S // 128
        self.sch = Sched(nc, es)
        self.arena = es.enter_context(nc.sbuf_tensor('arena', [128, ARENA_BYTES // 4], F32))
        self.top = 0
        self.nalloc = 0
        self.pbig = es.enter_context(nc.psum_tensor('pbig', [128, 8 * 512], F32))
        self.banks = [self.pbig[:, i * 512:(i + 1) * 512] for i in range(8)]
        self.outkeys = []
        self.deferred = []
        self.dr = {}

    def alloc(self, name, shape, dtype):
        esz = 4 if dtype == F32 else 2
        n = int(np.prod(shape[1:]))
        nbytes = (n * esz + 63) // 64 * 64
        lo, hi = self.top, self.top + nbytes
        assert hi <= ARENA_BYTES, ('SBUF arena overflow', name, hi)
        self.top = hi
        self.nalloc += 1
        uname = '%s#%d' % (name, self.nalloc)
        self.sch.new_buffer(uname, lo, hi)
        v = self.arena[:, lo // 4: hi // 4]
        if dtype != F32:
            v = v.bitcast(dtype)
        v = v[0:shape[0], 0:n]
        if len(shape) == 3:
            v = v.rearrange("p (a b) -> p a b", b=shape[2])
        elif len(shape) == 4:
            v = v.rearrange("p (a b c) -> p a b c", b=shape[2], c=shape[3])
        return Buf(uname, v)

    def ps(self, i):
        return ('ps', i)

    def bank_bf(self, i, lo, hi):
        return self.banks[i][:, lo:hi].bitcast(BF16)

    def op(self, *a, **k):
        return self.sch.op(*a, **k)

    def dram_in(self, name, shape, dtype=F32):
        t = self.nc.dram_tensor(name, list(shape), dtype, kind="ExternalInput").ap()
        self.dr[name] = t
        return t

    def dump(self, name, buf, keys, shape, src=None):
        if name not in self.dbg:
            return
        src = buf.ap if src is None else src
        t = self.nc.dram_tensor('dbg_' + name, list(shape), src.dtype, kind="ExternalOutput").ap()
        self.op('sp', lambda e: e.dma_start(out=t, in_=src), r=list(keys), w=[('dbgout', name)], dma='dbg_' + name)
        self.outkeys.append(('dbgout', name))

    @staticmethod
    def k2(buf, t):
        return [buf.k(t, 0), buf.k(t, 1)]

    def pump(self, n=1):
        for _ in range(n):
            if not self.deferred:
                return
            self.deferred.pop(0)()

    def flush(self):
        while self.deferred:
            self.deferred.pop(0)()

    def setup(self):
        nc, S, NT = self.nc, self.S, self.NT
        di = self.dram_in
        self.x = di('x', [S, D])
        di('c_col', [128, 8]); di('w_ada', [D, 6 * D]); di('b_ada_c', [128, 48])
        di('g1c', [128, 8]); di('g2c', [128, 8]); di('w_in', [D, 8192])
        di('gqk_row', [1, 256]); di('lam_row', [1, 256]); di('gda_row', [1, 128]); di('gret_row', [1, 256])
        di('w_out', [D, D]); di('w_up', [D, DFF]); di('w_down', [DFF, D])
        di('ropeA', [128, NT, 2, 64]); di('ropeR', [128, NT, 2, 128])
        di('retmask', [128, 128]); di('retc', [128, 8]); di('identf', [128, 128])
        self.out = nc.dram_tensor('out', [S, D], F32, kind="ExternalOutput").ap()
        self.mT_d = nc.dram_tensor('mT_d', [8, 128, S], BF16, kind="Internal").ap()
        self.wup_s = nc.dram_tensor('wup_s', [8, 128, 8, 512], BF16, kind="Internal").ap()
        self.wdn_s = nc.dram_tensor('wdn_s', [2, 8, 128, 4, 512], BF16, kind="Internal").ap()

        A = self.alloc
        dr = self.dr
        self.c_col = A('c_col', [128, 8], F32)
        self.b_ada_c = A('b_ada_c', [128, 48], F32)
        self.g1c = A('g1c', [128, 8], F32)
        self.g2c = A('g2c', [128, 8], F32)
        self.identf = A('identf', [128, 128], F32)
        self.ident = A('ident', [128, 128], BF16)
        self.onesf = A('onesf', [128, 128], F32)
        self.mhalf = A('mhalf', [128, 16], F32)
        self.epsc = A('epsc', [128, 1], F32)
        self.modc = A('modc', [128, 48], F32)
        self.A1 = A('A1', [128, 8], F32)
        self.A2 = A('A2', [128, 8], F32)
        self.retmask = A('retmask', [128, 128], F32)
        self.retc = A('retc', [128, 8], F32)
        self.gqk_b = A('gqk_b', [128, 256], F32)
        self.lam_b = A('lam_b', [128, 256], F32)
        self.gda_b = A('gda_b', [128, 128], F32)
        self.gret_b = A('gret_b', [128, 256], F32)
        self.lamc = A('lamc', [128, 1], F32)
        self.lamt = A('lamt', [128, 4], F32)
        self.lamp = A('lamp', [128, 128], F32)

        loads = [(self.c_col, dr['c_col']), (self.b_ada_c, dr['b_ada_c']), (self.g1c, dr['g1c']),
                 (self.g2c, dr['g2c']), (self.identf, dr['identf']), (self.retmask, dr['retmask']),
                 (self.retc, dr['retc']),
                 (self.gqk_b, dr['gqk_row'][0:1, :].broadcast_to([128, 256])),
                 (self.lam_b, dr['lam_row'][0:1, :].broadcast_to([128, 256])),
                 (self.gda_b, dr['gda_row'][0:1, :].broadcast_to([128, 128])),
                 (self.gret_b, dr['gret_row'][0:1, :].broadcast_to([128, 256]))]

        def ld(e):
            return [e.dma_start(out=b.ap, in_=src) for b, src in loads]
        self.op('sp', ld, w=[b.k() for b, _ in loads], dma='const', ndma=len(loads))
        self.op('dve', lambda e: e.tensor_copy(out=self.ident.ap, in_=self.identf.ap),
                r=[self.identf.k()], w=[self.ident.k()])
        self.op('pool', lambda e: e.memset(self.onesf.ap, 1.0), w=[self.onesf.k()])
        self.op('pool', lambda e: e.memset(self.mhalf.ap, -0.5), w=[self.mhalf.k()])
        self.op('pool', lambda e: e.memset(self.epsc.ap, EPS), w=[self.epsc.k()])
        self.op('dve', lambda e: e.tensor_scalar(out=self.gda_b.ap, in0=self.gda_b.ap,
                                                 scalar1=(1.0 - LAMBDA_INIT) * 0.5, scalar2=None, op0=ALU.mult),
                r=[self.gda_b.k()], w=[self.gda_b.k()])
        self.op('dve', lambda e: e.tensor_scalar(out=self.gret_b.ap, in0=self.gret_b.ap,
                                                 scalar1=0.25, scalar2=None, op0=ALU.mult),
                r=[self.gret_b.k()], w=[self.gret_b.k()])
        lb, lp, lt = self.lam_b, self.lamp, self.lamt
        self.op('dve', lambda e: e.tensor_tensor(out=lp.ap, in0=lb.ap[:, 0:128], in1=lb.ap[:, 128:256], op=ALU.mult),
                r=[lb.k()], w=[lp.k()])
        self.op('dve', lambda e: e.tensor_reduce(out=lt.ap[:, 0:2], in_=lp.ap.rearrange("p (a b) -> p a b", b=64),
                                                 axis=AX.X, op=ALU.add), r=[lp.k()], w=[lt.k(0)])
        self.op('act', lambda e: e.activation(out=lt.ap[:, 2:4], in_=lt.ap[:, 0:2], func=AF.Exp),
                r=[lt.k(0)], w=[lt.k(1)])
        self.op('dve', lambda e: e.tensor_tensor(out=self.lamc.ap, in0=lt.ap[:, 2:3], in1=lt.ap[:, 3:4],
                                                 op=ALU.subtract), r=[lt.k(1)], w=[self.lamc.k()])
        self.op('dve', lambda e: e.tensor_scalar(out=self.lamc.ap, in0=self.lamc.ap, scalar1=LAMBDA_INIT,
                                                 scalar2=None, op0=ALU.add), r=[self.lamc.k()], w=[self.lamc.k()])
        self.persist_top = self.top

    def ffn_scratch(self):
        dr = self.dr
        for c in range(8):
            def up(c=c):
                src = dr['w_up'][:, c * 512:(c + 1) * 512].rearrange("(dc p) f -> p dc f", p=128)
                self.op('pool', lambda e: e.dma_start(out=self.wup_s[c], in_=src), w=[('wup_s',)], dma='wup_s')
            self.bg.append(up)
        for nb in range(2):
            for c in range(8):
                def dn(nb=nb, c=c):
                    src = dr['w_down'][c * 512:(c + 1) * 512, nb * 512:(nb + 1) * 512].rearrange(
                        "(fb p) n -> p fb n", p=128)
                    self.op('pool', lambda e: e.dma_start(out=self.wdn_s[nb, c], in_=src), w=[('wdn_s',)], dma='wdn_s')
                self.bg.append(dn)

    def phase_mod(self):
        A, op, dr = self.alloc, self.op, self.dr
        th = A('th', [128, 8], F32)
        scf = A('scf', [128, 8], F32)
        scb = A('scb', [128, 8], BF16)
        was = [A('wa0', [128, 8, 1024], BF16), A('wa1', [128, 8, 1024], BF16)]
        cc = self.c_col
        op('act', lambda e: e.activation(out=th.ap, in_=cc.ap, func=AF.Tanh, scale=0.5), r=[cc.k()], w=[th.k()])
        op('dve', lambda e: e.scalar_tensor_tensor(out=scf.ap, in0=th.ap, scalar=1.0, in1=cc.ap,
                                                   op0=ALU.add, op1=ALU.mult), r=[th.k(), cc.k()], w=[scf.k()])
        op('dve', lambda e: e.tensor_scalar(out=scb.ap, in0=scf.ap, scalar1=0.5, scalar2=None, op0=ALU.mult),
           r=[scf.k()], w=[scb.k()])
        bank = self.banks[7]
        mc = self.modc

        def dma(v):
            wa = was[v % 2]
            src = dr['w_ada'][:, v * 1024:(v + 1) * 1024].rearrange("(kc p) n -> p kc n", p=128)
            op('pool', lambda e: e.dma_start(out=wa.ap, in_=src), w=[wa.k()], dma='wa%d' % (v % 2))

        def mmv(v):
            wa = was[v % 2]

            def mm(e):
                for j in range(8):
                    for kc in range(8):
                        last = e.matmul(bank[:, v * 8 + j: v * 8 + j + 1], lhsT=wa.ap[:, kc, j * 128:(j + 1) * 128],
                                        rhs=scb.ap[:, kc:kc + 1], start=(kc == 0), stop=(kc == 7))
                return last
            op('pe', mm, r=[wa.k(), scb.k()], w=[self.ps(7)])

        def fin():
            op('dve', lambda e: e.tensor_tensor(out=mc.ap[:, 16:48], in0=bank[:, 16:48], in1=self.b_ada_c.ap[:, 16:48],
                                                op=ALU.add), r=[self.b_ada_c.k()], w=[self.ps(7), mc.k(1)])
            op('dve', lambda e: e.scalar_tensor_tensor(out=self.A2.ap, in0=mc.ap[:, 32:40], scalar=1.0, in1=self.g2c.ap,
                                                       op0=ALU.add, op1=ALU.mult), r=[mc.k(1), self.g2c.k()], w=[self.A2.k()])
            self.dump('modc', mc, [mc.k(0), mc.k(1)], [128, 48])

        dma(0); dma(1); mmv(0); mmv(1); dma(2); dma(3)
        op('dve', lambda e: e.tensor_tensor(out=mc.ap[:, 0:16], in0=bank[:, 0:16], in1=self.b_ada_c.ap[:, 0:16], op=ALU.add),
           r=[self.b_ada_c.k()], w=[self.ps(7), mc.k(0)])
        op('dve', lambda e: e.scalar_tensor_tensor(out=self.A1.ap, in0=mc.ap[:, 8:16], scalar=1.0, in1=self.g1c.ap,
                                                   op0=ALU.add, op1=ALU.mult), r=[mc.k(0), self.g1c.k()], w=[self.A1.k()])
        self.mod_rest = {6: [lambda: mmv(2), lambda: dma(4)], 12: [lambda: mmv(3), lambda: dma(5)],
                         18: [lambda: mmv(4)], 24: [lambda: mmv(5), fin]}

    def norm_to_T(self, xt_ap, xkey, t, stats, xn, A_, Bcol, dst_fn, dst_key, banks, tag):
        op = self.op
        ss, ms, rstd, junk = stats
        op('act', lambda e: e.activation(out=junk.ap, in_=xt_ap, func=AF.Square, accum_out=ss.ap[:, t:t + 1]),
           r=[xkey], w=[junk.k(), ss.k(t)])
        op('pool', lambda e: e.tensor_scalar(out=ms.ap[:, t:t + 1], in0=ss.ap[:, t:t + 1], scalar1=1.0 / D,
                                             scalar2=EPS, op0=ALU.mult, op1=ALU.add), r=[ss.k(t)], w=[ms.k(t)])
        op('pool', lambda e: e.tensor_tensor(out=rstd.ap[:, t:t + 1], in0=ms.ap[:, t:t + 1],
                                             in1=self.mhalf.ap[:, 0:1], op=ALU.pow),
           r=[ms.k(t), self.mhalf.k()], w=[rstd.k(t)])
        op('dve', lambda e: e.tensor_scalar(out=xn.ap, in0=xt_ap, scalar1=rstd.ap[:, t:t + 1], scalar2=None,
                                            op0=ALU.mult), r=[xkey, rstd.k(t)], w=[xn.k()])
        ba, bb = banks
        pa = self.bank_bf(ba, 0, 256).rearrange("p (a b) -> p a b", b=128)
        pb = self.bank_bf(bb, 0, 256).rearrange("p (a b) -> p a b", b=128)

        def tr(e):
            for dc in range(8):
                pt = pa if dc < 4 else pb
                last = e.transpose(out=pt[:, dc % 4, :], in_=xn.ap[:, dc * 128:(dc + 1) * 128], identity=self.ident.ap)
            return last
        op('pe', tr, r=[xn.k(), self.ident.k()], w=[self.ps(ba), self.ps(bb)])
        da = dst_fn(0, 4)
        db = dst_fn(4, 8)

        def ev_act(e):
            for dc in range(4):
                last = e.activation(out=da[:, dc, :], in_=pa[:, dc, :], func=AF.Identity,
                                    scale=A_.ap[:, dc:dc + 1], bias=Bcol[:, dc:dc + 1])
            return last
        op('act', ev_act, r=[A_.k(), self.modc.k(1)], w=[self.ps(ba), dst_key[0]])

        def ev_dve(e):
            for dc in range(4, 8):
                last = e.tensor_scalar(out=db[:, dc - 4, :], in0=pb[:, dc - 4, :], scalar1=A_.ap[:, dc:dc + 1],
                                       scalar2=Bcol[:, dc:dc + 1], op0=ALU.mult, op1=ALU.add)
            return last
        op('dve', ev_dve, r=[A_.k(), self.modc.k(1)], w=[self.ps(bb), dst_key[1]])

    def phase_norm1(self):
        A, op, NT = self.alloc, self.op, self.NT
        NG = NT // 4
        xg = [A('xg%d' % i, [128, 4, D], F32) for i in range(2)]
        xns = [A('xn%d' % i, [128, D], BF16) for i in range(3)]
        junk = A('junk', [128, D], BF16)
        ss = A('ss1', [128, NT], F32)
        ms = A('ms1', [128, NT], F32)
        rstd = A('rstd1', [128, NT], F32)
        hm = self.hmixT
        A_, Bcol = self.A1, self.modc.ap[:, 0:8]

        def sa(t):
            g, k = divmod(t, 4)
            xb = xg[g % 2]
            op('sp', lambda e: e.dma_start(out=xb.ap[:, k, :], in_=self.x[t * 128:(t + 1) * 128, :]),
               w=[xb.k(k)], dma='xg%d_%d' % (g % 2, k))
            op('act', lambda e: e.activation(out=junk.ap, in_=xb.ap[:, k, :], func=AF.Square,
                                             accum_out=ss.ap[:, t:t + 1]), r=[xb.k(k)], w=[junk.k(), ss.k(t)])

        def sb(g):
            sl = slice(4 * g, 4 * g + 4)
            op('act', lambda e: e.activation(out=ms.ap[:, sl], in_=ss.ap[:, sl], func=AF.Sqrt, scale=1.0 / D, bias=self.epsc.ap),
               r=[ss.k(t) for t in range(4 * g, 4 * g + 4)] + [self.epsc.k()], w=[ms.k(g)])
            op('dve', lambda e: e.reciprocal(out=rstd.ap[:, sl], in_=ms.ap[:, sl]), r=[ms.k(g)], w=[rstd.k(g)])

        def sc(t):
            g, k = divmod(t, 4)
            xb = xg[g % 2]
            xn = xns[t % 3]
            op('dve', lambda e: e.tensor_scalar(out=xn.ap, in0=xb.ap[:, k, :], scalar1=rstd.ap[:, t:t + 1], scalar2=None,
                                                op0=ALU.mult), r=[xb.k(k), rstd.k(g)], w=[xn.k()])
            ba, bb = (0, 1) if t % 2 == 0 else (2, 3)
            pa = self.bank_bf(ba, 0, 256).rearrange("p (a b) -> p a b", b=128)
            pb = self.bank_bf(bb, 0, 256).rearrange("p (a b) -> p a b", b=128)

            def tr(e):
                for dc in range(8):
                    pt = pa if dc < 4 else pb
                    last = e.transpose(out=pt[:, dc % 4, :], in_=xn.ap[:, dc * 128:(dc + 1) * 128], identity=self.ident.ap)
                return last
            op('pe', tr, r=[xn.k(), self.ident.k()], w=[self.ps(ba), self.ps(bb)])

            def ev_act(e):
                for dc in range(4):
                    last = e.activation(out=hm.ap[:, dc, t * 128:(t + 1) * 128], in_=pa[:, dc, :], func=AF.Identity,
                                        scale=A_.ap[:, dc:dc + 1], bias=Bcol[:, dc:dc + 1])
                return last
            op('act', ev_act, r=[A_.k(), self.modc.k(0)], w=[self.ps(ba), hm.k(t, 0)])

            def ev_dve(e):
                for dc in range(4, 8):
                    last = e.tensor_scalar(out=hm.ap[:, dc, t * 128:(t + 1) * 128], in0=pb[:, dc - 4, :],
                                           scalar1=A_.ap[:, dc:dc + 1], scalar2=Bcol[:, dc:dc + 1],
                                           op0=ALU.mult, op1=ALU.add)
                return last
            op('dve', ev_dve, r=[A_.k(), self.modc.k(0)], w=[self.ps(bb), hm.k(t, 1)])

        for i in range(NT + 6):
            if i < NT:
                sa(i)
                if i % 4 == 3:
                    sb(i // 4)
            if 0 <= i - 6 < NT:
                sc(i - 6)
            for f in self.mod_rest.pop(i, []):
                f()
        for i in sorted(self.mod_rest):
            for f in self.mod_rest[i]:
                f()
        self.mod_rest = {}
        self.dump('hmixT', hm, sum([self.k2(hm, t) for t in range(NT)], []), [128, 8, self.S])

    def load_w(self, slot, blocks):
        dr = self.dr
        off = 0
        srcs = []
        for c0, n in blocks:
            srcs.append((off, n, dr['w_in'][:, c0:c0 + n].rearrange("(dc p) n -> p dc n", p=128)))
            off += n
        ws = self.wslots[slot]

        def ld(e):
            return [e.dma_start(out=ws.ap[:, :, o:o + n], in_=src) for o, n, src in srcs]
        self.op('pool', ld, w=[ws.k()], dma='wslot%d' % slot, ndma=len(srcs))

    def ret_head(self, r, slot):
        A, op, NT, S = self.alloc, self.op, self.NT, self.S
        mark = self.top
        ws = self.wslots[slot]
        hm = self.hmixT
        ropeR = A('ropeR', [128, NT, 2, 128], F32)
        npc = 4 if NT >= 4 else 1
        cpp = NT // npc
        for pc in range(npc):
            op('sp', lambda e, pc=pc: e.dma_start(out=ropeR.ap[:, pc * cpp:(pc + 1) * cpp],
                                                  in_=self.dr['ropeR'][:, pc * cpp:(pc + 1) * cpp]),
               w=[ropeR.k(pc)], dma='ropeR%d' % pc)
        qkf = [A('qkf%d' % i, [128, 256], F32) for i in range(3)]
        tq = [[A('tq%d_%d' % (i, j), [128, 128], F32) for j in range(2)] for i in range(2)]
        tk = [[A('tk%d_%d' % (i, j), [128, 128], F32) for j in range(2)] for i in range(2)]
        qt = [A('qt%d' % i, [128, 128], BF16) for i in range(3)]
        kt = [A('kt%d' % i, [128, 128], BF16) for i in range(3)]
        kp = [A('kp%d' % i, [128, 128], BF16) for i in range(4)]
        vb = [A('vb%d' % i, [128, 256], BF16) for i in range(10)]
        qkT = [A('qkT%d' % i, [128, 256], BF16) for i in range(6)]
        scm = [A('scm%d' % i, [128, 128], BF16) for i in range(4)]
        ysb = [A('ysb%d' % i, [128, 256], F32) for i in range(4)]
        St = A('St', [128, 256], F32)
        Sbf = [A('Sbf%d' % i, [128, 256], BF16) for i in range(4)]
        junk = A('rjunk', [128, 256], BF16)
        ssr = A('ssr', [128, NT], F32)
        msr = A('msr', [128, NT], F32)
        rsr = A('rsr', [128, NT], F32)
        g = 1.0 - 2.0 ** (-5.0 - r)
        gC = float(np.exp(128.0 * np.log(g)))
        idec = self.retc.ap[:, 2 * r:2 * r + 1]
        kdec = self.retc.ap[:, 2 * r + 1:2 * r + 2]
        orr = self.orr
        mask = self.retmask
        h3 = lambda ap: ap.rearrange("p (h d) -> p h d", h=2)

        def swapped(ap):
            a = [list(x) for x in ap.ap]
            return bass.AP(ap.tensor, ap.offset + 64, [a[0], [-64, 2], [1, 64]])

        def s_proj(c):
            pb = c % 2

            def mm(e):
                for dc in range(8):
                    last = e.matmul(self.banks[pb][:, 0:512], lhsT=hm.ap[:, dc, c * 128:(c + 1) * 128],
                                    rhs=ws.ap[:, dc, 0:512], start=(dc == 0), stop=(dc == 7))
                return last
            op('pe', mm, r=self.k2(hm, c) + [ws.k()], w=[self.ps(pb)])

        def s_evac(c):
            pb = c % 2
            P = self.banks[pb]
            f_, v_ = qkf[c % 3], vb[c % 10]
            op('act', lambda e: e.activation(out=f_.ap, in_=P[:, 0:256], func=AF.Copy), w=[self.ps(pb), f_.k()])
            op('act', lambda e: e.activation(out=v_.ap, in_=P[:, 256:512], func=AF.Copy), w=[self.ps(pb), v_.k()])

        def s_rope(c):
            b = c % 2
            f_ = qkf[c % 3]
            rk = ropeR.k(c // cpp)
            cc = ropeR.ap[:, c, 0, :]
            sn = ropeR.ap[:, c, 1, :]
            for (src, dec, tt) in ((f_.ap[:, 0:128], idec, tq[b]), (f_.ap[:, 128:256], kdec, tk[b])):
                op('dve', lambda e, src=src, dec=dec, tt=tt: e.scalar_tensor_tensor(
                    out=tt[0].ap, in0=src, scalar=dec, in1=cc, op0=ALU.mult, op1=ALU.mult),
                   r=[rk, self.retc.k(), f_.k()], w=[tt[0].k()])
                op('dve', lambda e, src=src, dec=dec, tt=tt: e.scalar_tensor_tensor(
                    out=h3(tt[1].ap), in0=swapped(src), scalar=dec, in1=h3(sn), op0=ALU.mult, op1=ALU.mult),
                   r=[rk, self.retc.k(), f_.k()], w=[tt[1].k()])

        def s_add(c):
            b = c % 2
            for tt, dst in ((tq[b], qt[c % 3]), (tk[b], kt[c % 3])):
                op('pool', lambda e, tt=tt, dst=dst: e.tensor_tensor(out=dst.ap, in0=tt[0].ap, in1=tt[1].ap, op=ALU.add),
                   r=[tt[0].k(), tt[1].k()], w=[dst.k()])

        def s_tr(c):
            q_, k_, kp_ = qt[c % 3], kt[c % 3], kp[c % 4]
            op('act', lambda e: e.activation(out=kp_.ap, in_=k_.ap, func=AF.Copy, scale=gC), r=[k_.k()], w=[kp_.k()])
            pT = self.bank_bf(4, 0, 128)

            def tr(e):
                e.transpose(out=pT[:, 0:128], in_=q_.ap, identity=self.ident.ap)
                return e.transpose(out=pT[:, 128:256], in_=k_.ap, identity=self.ident.ap)
            op('pe', tr, r=[q_.k(), k_.k(), self.ident.k()], w=[self.ps(4)])

        def s_trev(c):
            T_ = qkT[c % 6]
            pT = self.bank_bf(4, 0, 128)
            op('act', lambda e: e.activation(out=T_.ap, in_=pT, func=AF.Copy), w=[self.ps(4), T_.k()])

        def s_sc(c):
            T_, v_, kp_ = qkT[c % 6], vb[c % 10], kp[c % 4]
            op('pe', lambda e: e.matmul(self.banks[5][:, 0:128], lhsT=T_.ap[:, 128:256], rhs=T_.ap[:, 0:128],
                                        start=True, stop=True), r=[T_.k()], w=[self.ps(5)])
            if c < NT - 1:
                kb = 3 if c % 2 == 0 else 7
                op('pe', lambda e: e.matmul(self.banks[kb][:, 0:256], lhsT=kp_.ap, rhs=v_.ap, start=True, stop=True),
                   r=[kp_.k(), v_.k()], w=[self.ps(kb)])

        def s_mask(c):
            sc_ = scm[c % 4]
            op('dve', lambda e: e.tensor_tensor(out=sc_.ap, in0=self.banks[5][:, 0:128], in1=mask.ap, op=ALU.mult),
               r=[mask.k()], w=[self.ps(5), sc_.k()])
            if c < NT - 1:
                kb = 3 if c % 2 == 0 else 7
                if c == 0:
                    op('dve', lambda e: e.tensor_copy(out=St.ap, in_=self.banks[kb][:, 0:256]), w=[self.ps(kb), St.k()])
                else:
                    op('dve', lambda e: e.scalar_tensor_tensor(out=St.ap, in0=St.ap, scalar=gC, in1=self.banks[kb][:, 0:256],
                                                               op0=ALU.mult, op1=ALU.add),
                       r=[St.k()], w=[self.ps(kb), St.k()])

        def s_sbf(c):
            if c < NT - 1:
                sb_ = Sbf[(c + 1) % 4]
                op('act', lambda e: e.activation(out=sb_.ap, in_=St.ap, func=AF.Copy), r=[St.k()], w=[sb_.k()])

        def s_y(c):
            T_, sc_, v_ = qkT[c % 6], scm[c % 4], vb[c % 10]
            yb = 2 if c % 2 == 0 else 6
            sb_ = Sbf[c % 4]

            def ymm(e):
                last = e.matmul(self.banks[yb][:, 0:256], lhsT=sc_.ap, rhs=v_.ap, start=True, stop=(c == 0))
                if c > 0:
                    last = e.matmul(self.banks[yb][:, 0:256], lhsT=T_.ap[:, 0:128], rhs=sb_.ap, start=False, stop=True)
                return last
            op('pe', ymm, r=[sc_.k(), v_.k(), T_.k()] + ([sb_.k()] if c > 0 else []), w=[self.ps(yb)])

        def s_yev(c):
            yb = 2 if c % 2 == 0 else 6
            y_ = ysb[c % 4]
            op('act', lambda e: e.activation(out=y_.ap, in_=self.banks[yb][:, 0:256], func=AF.Copy), w=[self.ps(yb), y_.k()])

        def s_ysq(c):
            y_ = ysb[c % 4]
            op('act', lambda e: e.activation(out=junk.ap, in_=y_.ap, func=AF.Square, accum_out=ssr.ap[:, c:c + 1]),
               r=[y_.k()], w=[junk.k(), ssr.k(c)])

        def s_ysqrt(c):
            op('act', lambda e: e.activation(out=msr.ap[:, c:c + 1], in_=ssr.ap[:, c:c + 1], func=AF.Sqrt,
                                             scale=1.0 / 256, bias=self.epsc.ap), r=[ssr.k(c), self.epsc.k()], w=[msr.k(c)])

        def s_orr(c):
            y_ = ysb[c % 4]
            op('dve', lambda e: e.reciprocal(out=rsr.ap[:, c:c + 1], in_=msr.ap[:, c:c + 1]), r=[msr.k(c)], w=[rsr.k(c)])
            op('dve', lambda e: e.scalar_tensor_tensor(out=orr.ap[:, c, :], in0=y_.ap, scalar=rsr.ap[:, c:c + 1],
                                                       in1=self.gret_b.ap, op0=ALU.mult, op1=ALU.mult),
               r=[rsr.k(c), self.gret_b.k(), y_.k()], w=[orr.k(c)])

        stages = [(1, s_evac), (2, s_rope), (3, s_add), (5, s_trev), (4, s_tr), (8, s_sbf), (7, s_mask), (6, s_sc),
                  (10, s_yev), (9, s_y), (11, s_ysq), (12, s_ysqrt), (13, s_orr)]
        for i in range(NT + 14):
            for d, f in stages:
                if 0 <= i - d < NT:
                    f(i - d)
            if i < NT:
                s_proj(i)
        self.dump('orr%d' % r, orr, [orr.k(c) for c in range(NT)], [128, NT, 256])
        self.top = mark

    def da_head(self, h, slot):
        A, op, NT, S = self.alloc, self.op, self.NT, self.S
        NQ = S // 512
        mark = self.top
        ws = self.wslots[slot]
        hm = self.hmixT
        ropeA = self.ropeA
        QKT = A('QKT', [128, 2, S], BF16)
        V1 = A('V1', [128, NT, 129], BF16)
        exps = [A('exp%d' % i, [128, 2, 512], BF16) for i in range(4)]
        ss4 = A('ss4', [128, NT, 4], F32)
        ms4 = A('ms4', [128, NT, 4], F32)
        rs4 = A('rs4', [128, NT, 4], F32)
        markP = self.top
        sqj = [A('sqj%d' % i, [128, 256], F32) for i in range(2)]
        u = [A('u%d' % i, [128, 256], F32) for i in range(8)]
        t1 = [A('t1%d' % i, [128, 256], F32) for i in range(2)]
        t2 = [A('t2%d' % i, [128, 256], F32) for i in range(2)]
        o_ = [A('o%d' % i, [128, 256], F32) for i in range(2)]
        qk = [A('qk%d' % i, [128, 256], BF16) for i in range(4)]
        orr = self.orr
        hh = h % 2

        op('pool', lambda e: e.memset(V1.ap[:, :, 128:129], 1.0), w=[V1.k('ones')])

        g4 = lambda ap: ap.rearrange("p (g d) -> p g d", g=4)
        g42 = lambda ap: ap.rearrange("p (g h d) -> p g h d", g=4, h=2)

        def swapped(ap):
            a = [list(x) for x in ap.ap]
            return bass.AP(ap.tensor, ap.offset + 32, [a[0], [64, 4], [-32, 2], [1, 32]])

        def p1(t):
            pb = t % 2

            def mm(e):
                for dc in range(8):
                    last = e.matmul(self.banks[pb][:, 0:384], lhsT=hm.ap[:, dc, t * 128:(t + 1) * 128],
                                    rhs=ws.ap[:, dc, 0:384], start=(dc == 0), stop=(dc == 7))
                return last
            op('pe', mm, r=self.k2(hm, t) + [ws.k()], w=[self.ps(pb)])

        def pA(t):
            pb = t % 2
            P = self.banks[pb]
            sq, uu_ = sqj[t % 2], u[t % 8]
            op('act', lambda e: e.activation(out=uu_.ap, in_=P[:, 0:256], func=AF.Copy), w=[self.ps(pb), uu_.k()])
            op('act', lambda e: e.activation(out=V1.ap[:, t, 0:128], in_=P[:, 256:384], func=AF.Copy),
               w=[self.ps(pb), V1.k(t)])
            op('act', lambda e: e.activation(out=sq.ap, in_=P[:, 0:256], func=AF.Square), w=[self.ps(pb), sq.k()])

        def pB(t):
            sq, uu_ = sqj[t % 2], u[t % 8]
            op('dve', lambda e: e.tensor_reduce(out=ss4.ap[:, t, :], in_=g4(sq.ap), axis=AX.X, op=ALU.add),
               r=[sq.k()], w=[ss4.k(t)])
            op('dve', lambda e: e.tensor_tensor(out=uu_.ap, in0=uu_.ap, in1=self.gqk_b.ap, op=ALU.mult),
               r=[self.gqk_b.k(), uu_.k()], w=[uu_.k()])

        def pC(t):
            op('act', lambda e: e.activation(out=ms4.ap[:, t, :], in_=ss4.ap[:, t, :], func=AF.Sqrt, scale=1.0 / 64,
                                             bias=self.epsc.ap), r=[ss4.k(t), self.epsc.k()], w=[ms4.k(t)])

        def pD(t):
            b = t % 2
            uu_ = u[t % 8]
            op('dve', lambda e: e.reciprocal(out=rs4.ap[:, t, :], in_=ms4.ap[:, t, :]), r=[ms4.k(t)], w=[rs4.k(t)])
            cc = ropeA.ap[:, t, 0:1, :].broadcast_to([128, 4, 64])
            sn = ropeA.ap[:, t, 1, :].rearrange("p (h d) -> p h d", h=2).unsqueeze(1).broadcast_to([128, 4, 2, 32])
            op('pool', lambda e: e.tensor_tensor(out=g4(t1[b].ap), in0=g4(uu_.ap), in1=cc, op=ALU.mult),
               r=[uu_.k(), ropeA.k()], w=[t1[b].k()])
            op('dve', lambda e: e.tensor_tensor(out=g42(t2[b].ap), in0=swapped(uu_.ap), in1=sn, op=ALU.mult),
               r=[uu_.k(), ropeA.k()], w=[t2[b].k()])

        def pE(t):
            b = t % 2
            op('pool', lambda e: e.tensor_tensor(out=o_[b].ap, in0=t1[b].ap, in1=t2[b].ap, op=ALU.add),
               r=[t1[b].k(), t2[b].k()], w=[o_[b].k()])

        def pF(t):
            b = t % 2
            rb = rs4.ap[:, t, :].unsqueeze(2).broadcast_to([128, 4, 64])
            q_ = qk[t % 4]
            op('dve', lambda e: e.tensor_tensor(out=g4(q_.ap), in0=g4(o_[b].ap), in1=rb, op=ALU.mult),
               r=[o_[b].k(), rs4.k(t)], w=[q_.k()])

        def pG(t):
            q_ = qk[t % 4]
            tb = 2 + (t % 2)
            pT = self.bank_bf(tb, 0, 128).rearrange("p (a b) -> p a b", b=128)

            def tr(e):
                e.transpose(out=pT[:, 0, :], in_=q_.ap[:, 0:128], identity=self.ident.ap)
                return e.transpose(out=pT[:, 1, :], in_=q_.ap[:, 128:256], identity=self.ident.ap)
            op('pe', tr, r=[q_.k(), self.ident.k()], w=[self.ps(tb)])
            op('act', lambda e: e.activation(out=QKT.ap[:, :, t * 128:(t + 1) * 128], in_=pT, func=AF.Copy),
               w=[self.ps(tb), QKT.k(t)])

        stages = [(1, pA), (2, pB), (3, pC), (4, pD), (5, pE), (6, pF), (8, pG)]
        for i in range(NT + 9):
            for d, f in stages:
                if 0 <= i - d < NT:
                    f(i - d)
            if i < NT:
                p1(i)
        self.dump('QKT%d' % h, QKT, [QKT.k(t) for t in range(NT)], [128, 2, S])
        self.dump('V1_%d' % h, V1, [V1.k(t) for t in range(NT)] + [V1.k('ones')], [128, NT, 129])

        self.top = markP
        accS = [A('accS%d' % i, [128, 8, 129], F32) for i in range(4)]
        rr = A('rr', [128, NT, 4], F32)
        sso = A('sso', [128, NT], F32)
        mso = A('mso', [128, NT], F32)
        rso = A('rso', [128, NT], F32)
        tt = [A('tt%d' % i, [128, 128], F32) for i in range(2)]
        oo = [A('oo%d' % i, [128, 128], F32) for i in range(2)]
        ojk = A('ojk', [128, 128], F32)
        on = [A('on%d' % i, [128, 128], F32) for i in range(2)]
        tg = [A('tg%d' % i, [128, 384], F32) for i in range(2)]
        uu = [A('uu%d' % i, [128, 128], F32) for i in range(2)]
        u2 = [A('u2%d' % i, [128, 128], F32) for i in range(2)]
        u3 = [A('u3%d' % i, [128, 128], F32) for i in range(2)]
        m1 = [A('m1%d' % i, [128, 128], F32) for i in range(2)]
        mg = [A('mg%d' % i, [128, 128], BF16) for i in range(2)]
        mst = [A('mst%d' % i, [128, 512], BF16) for i in range(4)]
        acc_loc = [(4 + i // 3, (i % 3) * 129) for i in range(8)]

        def acc_ap(qb, m):
            bk, off = acc_loc[qb * 2 + m]
            return self.banks[bk][:, off:off + 129]

        def sS(j, kt, step):
            i = kt - 4 * j
            c0 = 128 * i if i > 0 else 0
            sa, sb_ = (0, 1) if step % 2 == 0 else (2, 3)

            def mm(e):
                e.matmul(self.banks[sa][:, c0:512], lhsT=QKT.ap[0:64, 1, kt * 128:(kt + 1) * 128],
                         rhs=QKT.ap[0:64, 0, j * 512 + c0:(j + 1) * 512], start=True, stop=True)
                return e.matmul(self.banks[sb_][:, c0:512], lhsT=QKT.ap[64:128, 1, kt * 128:(kt + 1) * 128],
                                rhs=QKT.ap[64:128, 0, j * 512 + c0:(j + 1) * 512], start=True, stop=True)
            rk = [QKT.k(kt)] + [QKT.k(4 * j + q) for q in range(4)]
            op('pe', mm, r=rk, w=[self.ps(sa), self.ps(sb_)])
            ex = exps[step % 4]
            src = self.pbig[:, sa * 512:(sa + 2) * 512].rearrange("p (m c) -> p m c", m=2)[:, :, c0:512]
            op('act', lambda e: e.activation(out=ex.ap[:, :, c0:512], in_=src, func=AF.Exp, scale=0.125),
               w=[self.ps(sa), self.ps(sb_), ex.k()])
            if i >= 0:
                op('pool', lambda e: e.memset(ex.ap[64:128, :, c0:c0 + 64], 0.0), w=[ex.k()])

        def sV(j, kt, step):
            i = kt - 4 * j
            ex = exps[step % 4]
            qb0 = max(i, 0)

            def mm(e):
                for qb in range(qb0, 4):
                    for m in range(2):
                        idx = qb * 2 + m
                        last = e.matmul(acc_ap(qb, m), lhsT=ex.ap[:, m, qb * 128:(qb + 1) * 128], rhs=V1.ap[:, kt, :],
                                        start=(kt == 0 and idx % 3 == 0), stop=(kt == 4 * j + qb),
                                        skip_group_check=True)
                return last
            op('pe', mm, r=[ex.k(), V1.k(kt), V1.k('ones')], w=[self.ps(4), self.ps(5), self.ps(6)])

        def merge_steps(j):
            ab = accS[j % 4]
            av = ab.ap
            flat = av.rearrange("p i c -> p (i c)")
            op('dve', lambda e: e.tensor_copy(out=flat[:, 0:387], in_=self.banks[4][:, 0:387]), w=[self.ps(4), ab.k(0)])
            op('dve', lambda e: e.tensor_copy(out=flat[:, 387:774], in_=self.banks[5][:, 0:387]), w=[self.ps(5), ab.k(1)])
            op('dve', lambda e: e.tensor_copy(out=flat[:, 774:1032], in_=self.banks[6][:, 0:258]), w=[self.ps(6), ab.k(2)])
            ms_ = mst[j % 4]
            steps = []
            stages = []
            for qb in range(4):
                tb = 4 * j + qb
                b = qb % 2

                def M1(qb=qb, tb=tb, b=b):
                    op('dve', lambda e: e.reciprocal(out=rr.ap[:, tb, 0:2], in_=av[:, 2 * qb:2 * qb + 2, 128]),
                       r=[ab.k(0), ab.k(1), ab.k(2)], w=[rr.k(tb, 0)])
                    op('dve', lambda e: e.tensor_tensor(out=rr.ap[:, tb, 2:3], in0=rr.ap[:, tb, 1:2], in1=self.lamc.ap,
                                                        op=ALU.mult), r=[rr.k(tb, 0), self.lamc.k()], w=[rr.k(tb, 1)])
                    op('dve', lambda e: e.tensor_scalar(out=tt[b].ap, in0=av[:, 2 * qb + 1, 0:128],
                                                        scalar1=rr.ap[:, tb, 2:3], scalar2=None, op0=ALU.mult),
                       r=[ab.k(0), ab.k(1), ab.k(2), rr.k(tb, 1)], w=[tt[b].k()])
                    op('dve', lambda e: e.scalar_tensor_tensor(out=oo[b].ap, in0=av[:, 2 * qb, 0:128],
                                                               scalar=rr.ap[:, tb, 0:1], in1=tt[b].ap,
                                                               op0=ALU.mult, op1=ALU.subtract),
                       r=[ab.k(0), ab.k(1), ab.k(2), rr.k(tb, 0), tt[b].k()], w=[oo[b].k()])
                    op('dve', lambda e: e.scalar_tensor_tensor(out=ojk.ap, in0=oo[b].ap, scalar=1.0, in1=oo[b].ap,
                                                               op0=ALU.mult, op1=ALU.mult,
                                                               accum_out=sso.ap[:, tb:tb + 1]),
                       r=[oo[b].k()], w=[ojk.k(), sso.k(tb)])
                    op('pool', lambda e: e.tensor_scalar(out=mso.ap[:, tb:tb + 1], in0=sso.ap[:, tb:tb + 1],
                                                         scalar1=1.0 / 128, scalar2=EPS, op0=ALU.mult, op1=ALU.add),
                       r=[sso.k(tb)], w=[mso.k(tb)])
                    op('pool', lambda e: e.tensor_tensor(out=rso.ap[:, tb:tb + 1], in0=mso.ap[:, tb:tb + 1],
                                                         in1=self.mhalf.ap[:, 0:1], op=ALU.pow),
                       r=[mso.k(tb), self.mhalf.k()], w=[rso.k(tb)])
                    op('dve', lambda e: e.scalar_tensor_tensor(out=on[b].ap, in0=oo[b].ap, scalar=rso.ap[:, tb:tb + 1],
                                                               in1=self.gda_b.ap, op0=ALU.mult, op1=ALU.mult),
                       r=[oo[b].k(), rso.k(tb), self.gda_b.k()], w=[on[b].k()])

                def M2(qb=qb, tb=tb, b=b):
                    def mm(e):
                        for dc in range(8):
                            last = e.matmul(self.banks[7][:, 0:384], lhsT=hm.ap[:, dc, tb * 128:(tb + 1) * 128],
                                            rhs=ws.ap[:, dc, 384:768], start=(dc == 0), stop=(dc == 7))
                        return last
                    op('pe', mm, r=self.k2(hm, tb) + [ws.k()], w=[self.ps(7)])
                    op('act', lambda e: e.activation(out=tg[b].ap, in_=self.banks[7][:, 0:384], func=AF.Tanh, scale=0.5),
                       w=[self.ps(7), tg[b].k()])
                    op('dve', lambda e: e.scalar_tensor_tensor(out=uu[b].ap, in0=tg[b].ap[:, 256:384], scalar=1.0,
                                                               in1=self.banks[7][:, 256:384], op0=ALU.add, op1=ALU.mult),
                       r=[tg[b].k()], w=[self.ps(7), uu[b].k()])

                def M3(qb=qb, tb=tb, b=b):
                    op('pool', lambda e: e.tensor_tensor(out=u2[b].ap, in0=uu[b].ap,
                                                         in1=orr.ap[:, tb, hh * 128:(hh + 1) * 128], op=ALU.mult),
                       r=[uu[b].k(), orr.k(tb)], w=[u2[b].k()])
                    op('dve', lambda e: e.scalar_tensor_tensor(out=u3[b].ap, in0=tg[b].ap[:, 128:256], scalar=1.0,
                                                               in1=u2[b].ap, op0=ALU.add, op1=ALU.mult),
                       r=[tg[b].k(), u2[b].k()], w=[u3[b].k()])
                    op('dve', lambda e: e.scalar_tensor_tensor(out=m1[b].ap, in0=tg[b].ap[:, 0:128], scalar=1.0,
                                                               in1=on[b].ap, op0=ALU.add, op1=ALU.mult),
                       r=[tg[b].k(), on[b].k()], w=[m1[b].k()])
                    op('pool', lambda e: e.tensor_tensor(out=mg[b].ap, in0=m1[b].ap, in1=u3[b].ap, op=ALU.add),
                       r=[m1[b].k(), u3[b].k()], w=[mg[b].k()])

                def M3b(qb=qb, tb=tb, b=b):
                    pT = self.bank_bf(7, 384, 448)
                    op('pe', lambda e: e.transpose(out=pT, in_=mg[b].ap, identity=self.ident.ap),
                       r=[mg[b].k(), self.ident.k()], w=[self.ps(7)])
                    op('dve', lambda e: e.tensor_copy(out=ms_.ap[:, qb * 128:(qb + 1) * 128], in_=pT),
                       w=[self.ps(7), ms_.k(qb)])
                stages.append((M1, M2, M3, M3b))
            for slot in range(4 + 3):
                for k in (3, 2, 1, 0):
                    if 0 <= slot - k < 4:
                        steps.append(stages[slot - k][k])

            def M4():
                op('sp', lambda e: e.dma_start(out=self.mT_d[h, :, j * 512:(j + 1) * 512], in_=ms_.ap),
                   r=[ms_.k(q) for q in range(4)], w=[('mT_d', h, j)], dma='mst%d' % (j % 4))
            steps.append(M4)
            return steps

        step = 0
        for j in range(NQ):
            nk = 4 * j + 4
            per = 1 if len(self.deferred) <= 2 * nk + 17 else 2
            sS(j, 0, step)
            sS(j, 1, step + 1)
            for kt in range(nk):
                if kt + 2 < nk:
                    sS(j, kt + 2, step + 2)
                sV(j, kt, step)
                step += 1
                self.pump(per)
            while len(self.deferred) > 36:
                self.pump(1)
            self.deferred = self.deferred + merge_steps(j)
            if self.bg:
                self.bg.pop(0)()
        self.flush()
        self.top = mark

    def tail(self):
        A, op, S, dr = self.alloc, self.op, self.S, self.dr
        NTT = S // 512
        self.top = self.persist_top
        G1b = A('G1b', [128, D], F32)
        G2b = A('G2b', [128, D], F32)
        wout = A('wout', [128, 8, D], BF16)
        diag = [A('diag%d' % i, [128, 128], F32) for i in range(2)]
        x1 = [A('x1_%d' % i, [128, 4, D], F32) for i in range(3)]
        mTt = [A('mTt%d' % i, [128, 8, 512], BF16) for i in range(2)]
        hffT = A('hffT', [128, 8, 512], BF16)
        hT = A('hT', [128, 32, 512], BF16)
        xn2 = [A('xn2_%d' % i, [128, D], BF16) for i in range(2)]
        rl = [A('rl%d' % i, [128, 512], F32) for i in range(2)]
        yg = [A('yg%d' % i, [128, D], F32) for i in range(2)]
        wupc = [A('wupc%d' % i, [128, 8, 512], BF16) for i in range(3)]
        wdnc = [A('wdnc%d' % i, [128, 4, 512], BF16) for i in range(4)]
        junk = A('tjunk', [128, D], BF16)
        ss = A('ss2', [128, S // 128], F32)
        ms = A('ms2', [128, S // 128], F32)
        rstd = A('rstd2', [128, S // 128], F32)

        src = dr['w_out'].rearrange("(g p) n -> p g n", p=128)

        def ldw(e):
            return [e.dma_start(out=wout.ap[:, g0:g0 + 4, :], in_=src[:, g0:g0 + 4, :]) for g0 in (0, 4)]
        op('pool', ldw, w=[wout.k()], dma='wout', ndma=2)

        for gi, (G, c0) in enumerate(((G1b, 16), (G2b, 40))):
            for half in range(2):
                for q in range(4):
                    dc = half * 4 + q
                    dg = diag[dc % 2]
                    op('dve', lambda e, dg=dg, dc=dc, c0=c0: e.tensor_scalar(
                        out=dg.ap, in0=self.identf.ap, scalar1=self.modc.ap[:, c0 + dc:c0 + dc + 1], scalar2=None,
                        op0=ALU.mult), r=[self.identf.k(), self.modc.k(1)], w=[dg.k()])
                    op('pe', lambda e, dg=dg, q=q, half=half: e.matmul(
                        self.banks[half][:, q * 128:(q + 1) * 128], lhsT=self.onesf.ap, rhs=dg.ap, start=True, stop=True),
                       r=[self.onesf.k(), dg.k()], w=[self.ps(half)])
                op('act', lambda e, G=G, half=half: e.activation(out=G.ap[:, half * 512:(half + 1) * 512],
                                                                 in_=self.banks[half][:, 0:512], func=AF.Copy),
                   w=[self.ps(half), G.k(half)])

        sctr = {'up': 0, 'dn': 0}

        def load_tile(i):
            xb = x1[i % 3]
            xsrc = self.x[i * 512:(i + 1) * 512, :].rearrange("(s p) d -> p s d", p=128)
            op('sp', lambda e: e.dma_start(out=xb.ap, in_=xsrc), w=[xb.k(s) for s in range(4)], dma='x1_%d' % (i % 3))
            mt = mTt[i % 2]
            msrc = self.mT_d[:, :, i * 512:(i + 1) * 512].rearrange("g p t -> p g t")
            op('sp', lambda e: e.dma_start(out=mt.ap, in_=msrc), r=[('mT_d', g, i) for g in range(8)],
               w=[mt.k()], dma='mTt%d' % (i % 2))

        def O_steps(i):
            xb = x1[i % 3]
            mt = mTt[i % 2]
            Oa, Ob1, Ob2 = [], [], []
            for s in range(4):
                t = i * 4 + s
                xn = xn2[s % 2]

                def fa(s=s, t=t):
                    def mm(e):
                        for nb in range(2):
                            for g in range(8):
                                last = e.matmul(self.banks[6 + nb][:, 0:512], lhsT=mt.ap[:, g, s * 128:(s + 1) * 128],
                                                rhs=wout.ap[:, g, nb * 512:(nb + 1) * 512], start=(g == 0), stop=(g == 7))
                        return last
                    op('pe', mm, r=[mt.k(), wout.k()], w=[self.ps(6), self.ps(7)])
                    y = yg[s % 2]
                    for nb in range(2):
                        op('dve', lambda e, nb=nb: e.tensor_tensor(out=y.ap[:, nb * 512:(nb + 1) * 512],
                                                                    in0=self.banks[6 + nb][:, 0:512],
                                                                    in1=G1b.ap[:, nb * 512:(nb + 1) * 512], op=ALU.mult),
                           r=[G1b.k(nb)], w=[self.ps(6 + nb), y.k(nb)])
                    op('pool', lambda e: e.tensor_tensor(out=xb.ap[:, s, :], in0=y.ap, in1=xb.ap[:, s, :], op=ALU.add),
                       r=[y.k(0), y.k(1), xb.k(s)], w=[xb.k(s)])
                    op('act', lambda e: e.activation(out=junk.ap, in_=xb.ap[:, s, :], func=AF.Square,
                                                     accum_out=ss.ap[:, t:t + 1]), r=[xb.k(s)], w=[junk.k(), ss.k(t)])

                def fb1(s=s, t=t, xn=xn):
                    op('act', lambda e: e.activation(out=ms.ap[:, t:t + 1], in_=ss.ap[:, t:t + 1], func=AF.Sqrt,
                                                     scale=1.0 / D, bias=self.epsc.ap), r=[ss.k(t), self.epsc.k()], w=[ms.k(t)])
                    op('dve', lambda e: e.reciprocal(out=rstd.ap[:, t:t + 1], in_=ms.ap[:, t:t + 1]), r=[ms.k(t)], w=[rstd.k(t)])
                    op('dve', lambda e: e.tensor_scalar(out=xn.ap, in0=xb.ap[:, s, :], scalar1=rstd.ap[:, t:t + 1],
                                                        scalar2=None, op0=ALU.mult), r=[xb.k(s), rstd.k(t)], w=[xn.k()])

                def fb2(s=s, xn=xn):
                    pa = self.bank_bf(6, 0, 256).rearrange("p (a b) -> p a b", b=128)
                    pb = self.bank_bf(7, 0, 256).rearrange("p (a b) -> p a b", b=128)

                    def tr(e):
                        for dc in range(8):
                            pt = pa if dc < 4 else pb
                            last = e.transpose(out=pt[:, dc % 4, :], in_=xn.ap[:, dc * 128:(dc + 1) * 128],
                                               identity=self.ident.ap)
                        return last
                    op('pe', tr, r=[xn.k(), self.ident.k()], w=[self.ps(6), self.ps(7)])
                    A_, Bc = self.A2, self.modc.ap[:, 24:32]

                    def ev_act(e):
                        for dc in range(4):
                            last = e.activation(out=hffT.ap[:, dc, s * 128:(s + 1) * 128], in_=pa[:, dc, :],
                                                func=AF.Identity, scale=A_.ap[:, dc:dc + 1], bias=Bc[:, dc:dc + 1])
                        return last
                    op('act', ev_act, r=[A_.k(), self.modc.k(1)], w=[self.ps(6), hffT.k(s, 0)])

                    def ev_dve(e):
                        for dc in range(4, 8):
                            last = e.tensor_scalar(out=hffT.ap[:, dc, s * 128:(s + 1) * 128], in0=pb[:, dc - 4, :],
                                                   scalar1=A_.ap[:, dc:dc + 1], scalar2=Bc[:, dc:dc + 1],
                                                   op0=ALU.mult, op1=ALU.add)
                        return last
                    op('dve', ev_dve, r=[A_.k(), self.modc.k(1)], w=[self.ps(7), hffT.k(s, 1)])
                Oa.append(fa); Ob1.append(fb1); Ob2.append(fb2)
            return [Oa[0], Ob1[0], Oa[1], Ob2[0], Ob1[1], Oa[2], Ob2[1], Ob1[2], Oa[3], Ob2[2], Ob1[3], (lambda: None), Ob2[3]]

        def U(i):
            for c in range(8):
                wb = wupc[sctr['up'] % 3]
                sctr['up'] += 1
                op('sp', lambda e, wb=wb, c=c: e.dma_start(out=wb.ap, in_=self.wup_s[c]), r=[('wup_s',)], w=[wb.k()],
                   dma=wb.name)
                for fl in range(4):
                    fb = 4 * c + fl
                    bk = 4 + fb % 2

                    def mm(e, fl=fl, bk=bk, wb=wb):
                        for dc in range(8):
                            last = e.matmul(self.banks[bk][:, 0:512], lhsT=wb.ap[:, dc, fl * 128:(fl + 1) * 128],
                                            rhs=hffT.ap[:, dc, :], start=(dc == 0), stop=(dc == 7))
                        return last
                    op('pe', mm, r=[wb.k()] + sum([self.k2(hffT, s) for s in range(4)], []), w=[self.ps(bk)])
                    r_ = rl[fb % 2]
                    op('act', lambda e, bk=bk, r_=r_: e.activation(out=r_.ap, in_=self.banks[bk][:, 0:512], func=AF.Relu),
                       w=[self.ps(bk), r_.k()])
                    op('dve', lambda e, r_=r_, fb=fb: e.tensor_tensor(out=hT.ap[:, fb, :], in0=r_.ap, in1=r_.ap, op=ALU.mult),
                       r=[r_.k()], w=[hT.k(fb)])

        def Dn(i):
            xb = x1[i % 3]
            for nb in range(2):
                for c in range(8):
                    wb = wdnc[sctr['dn'] % 4]
                    sctr['dn'] += 1
                    op('sp', lambda e, wb=wb, c=c, nb=nb: e.dma_start(out=wb.ap, in_=self.wdn_s[nb, c]),
                       r=[('wdn_s',)], w=[wb.k()], dma=wb.name)

                    def mm(e, wb=wb, c=c):
                        for s in range(4):
                            for fl in range(4):
                                last = e.matmul(self.banks[s][:, 0:512], lhsT=hT.ap[:, 4 * c + fl, s * 128:(s + 1) * 128],
                                                rhs=wb.ap[:, fl, :], start=(c == 0 and fl == 0), stop=(c == 7 and fl == 3))
                        return last
                    op('pe', mm, r=[wb.k()] + [hT.k(4 * c + fl) for fl in range(4)], w=[self.ps(s) for s in range(4)])
                    self.pump(1)
                for s in range(4):
                    y = yg[s % 2]
                    op('dve', lambda e, s=s, y=y, nb=nb: e.tensor_tensor(out=y.ap[:, 0:512], in0=self.banks[s][:, 0:512],
                                                                         in1=G2b.ap[:, nb * 512:(nb + 1) * 512], op=ALU.mult),
                       r=[G2b.k(nb)], w=[self.ps(s), y.k(0)])
                    op('pool', lambda e, s=s, y=y, nb=nb: e.tensor_tensor(
                        out=xb.ap[:, s, nb * 512:(nb + 1) * 512], in0=y.ap[:, 0:512],
                        in1=xb.ap[:, s, nb * 512:(nb + 1) * 512], op=ALU.add), r=[y.k(0), xb.k(s)], w=[xb.k(s)])
            dst = self.out[i * 512:(i + 1) * 512, :].rearrange("(s p) d -> p s d", p=128)
            op('sp', lambda e: e.dma_start(out=dst, in_=xb.ap), r=[xb.k(s) for s in range(4)], w=[('out', i)],
               dma='ost%d' % (i % 3))
            self.outkeys.append(('out', i))

        load_tile(0)
        for f in O_steps(0):
            f()
        for i in range(NTT):
            if i + 1 < NTT:
                load_tile(i + 1)
                self.deferred = O_steps(i + 1)
            U(i)
            self.pump(1)
            Dn(i)
            self.flush()

    def build(self, upto='all'):
        self.setup()
        A = self.alloc
        self.hmixT = A('hmixT', [128, 8, self.S], BF16)
        self.orr = A('orr', [128, self.NT, 256], BF16)
        self.ropeA = A('ropeA', [128, self.NT, 2, 64], F32)
        self.wslots = [A('wslot0', [128, 8, 768], BF16), A('wslot1', [128, 8, 768], BF16)]
        self.mixer_top = self.top
        self.bg = []
        phases = []
        for r in range(4):
            phases += [('ret', r), ('da', 2 * r), ('da', 2 * r + 1)]
        if upto == 'ret0':
            phases = phases[:1]
        elif upto == 'da0':
            phases = phases[:2]

        def wblocks(ph):
            kind, i = ph
            if kind == 'ret':
                return [(C_QR + i * 128, 128), (C_KR + i * 128, 128), (C_VR + i * 256, 256)]
            return [(C_QA + i * 128, 128), (C_KA + i * 128, 128), (C_VA + i * 128, 128),
                    (C_GA + i * 128, 128), (C_GB + i * 128, 128), (C_GR + i * 128, 128)]
        self.phase_mod()
        self.load_w(0, wblocks(phases[0]))
        self.op('sp', lambda e: e.dma_start(out=self.ropeA.ap, in_=self.dr['ropeA']), w=[self.ropeA.k()], dma='ropeA')
        self.phase_norm1()
        self.top = self.mixer_top
        if upto == 'norm1':
            return self.finish()
        self.ffn_scratch()
        for n, ph in enumerate(phases):
            if n + 1 < len(phases):
                self.load_w((n + 1) % 2, wblocks(phases[n + 1]))
            if ph[0] == 'ret':
                self.ret_head(ph[1], n % 2)
            else:
                self.da_head(ph[1], n % 2)
        while self.bg:
            self.bg.pop(0)()
        if upto != 'all':
            return self.finish()
        self.tail()
        return self.finish()

    def finish(self):
        self.flush()
        keys = list(self.outkeys)
        self.op('sp', lambda e: e.nop(), r=keys + [('wup_s',), ('wdn_s',)], sig=False)
        self.sch.run_block()


def build_program(S=4096, dbg=(), upto='all'):
    nc = bass.Bass("TRN2", target_bir_lowering=False)
    with ExitStack() as es:
        kb = KB(nc, es, S, dbg)
        kb.build(upto)
    return nc


def host_tables(S):
    NT = S // 128
    pos = np.arange(S, dtype=np.float32)
    f32 = np.float32
    invA = (10000.0 ** (-np.arange(0, 64, 2, dtype=f32) / f32(64))).astype(f32)
    angA = (pos[:, None] * invA[None, :]).astype(f32).astype(np.float64)
    cA, sA = np.cos(angA).astype(f32), np.sin(angA).astype(f32)
    ropeA = np.stack([np.concatenate([cA, cA], 1), np.concatenate([-sA, sA], 1)], 1)
    ropeA = np.ascontiguousarray(ropeA.reshape(NT, 128, 2, 64).transpose(1, 0, 2, 3))
    invR = (1.0 / (f32(10000.0) ** np.linspace(0.0, 1.0, 64, dtype=f32))).astype(f32)
    angR = (pos[:, None] * invR[None, :]).astype(f32).astype(np.float64)
    cR, sR = np.cos(angR).astype(f32), np.sin(angR).astype(f32)
    ropeR = np.stack([np.concatenate([cR, cR], 1), np.concatenate([-sR, sR], 1)], 1)
    ropeR = np.ascontiguousarray(ropeR.reshape(NT, 128, 2, 128).transpose(1, 0, 2, 3))
    n = np.arange(128, dtype=np.float64)
    retc = np.zeros((128, 8), f32)
    for r in range(4):
        lg = np.log(1.0 - 2.0 ** (-5.0 - r))
        retc[:, 2 * r] = np.exp((n + 1.0) * lg)
        retc[:, 2 * r + 1] = (128.0 ** -0.5) * np.exp(-(n + 1.0) * lg)
    jj, ii = np.meshgrid(np.arange(128), np.arange(128), indexing='ij')
    retmask = (ii >= jj).astype(f32)
    return dict(ropeA=ropeA, ropeR=ropeR, retc=retc, retmask=retmask, identf=np.eye(128, dtype=f32))


def core_inputs(b, S, inp, tabs):
    f = lambda a: np.ascontiguousarray(np.asarray(a, dtype=np.float32))
    col = lambda v, n: f(np.asarray(v, np.float32).reshape(n, 128).T)
    m = dict(tabs)
    m['x'] = f(inp['x'][b, :S])
    m['c_col'] = col(inp['c'][b], 8)
    m['w_ada'] = f(inp['w_ada'][0])
    m['b_ada_c'] = col(inp['b_ada'][0], 48)
    m['g1c'] = col(inp['g_norm1'][0], 8)
    m['g2c'] = col(inp['g_norm2'][0], 8)
    m['w_in'] = f(inp['w_in'][0])
    gq, gk = np.asarray(inp['g_q'][0], np.float32), np.asarray(inp['g_k'][0], np.float32)
    m['gqk_row'] = f(np.concatenate([gq, gq, gk, gk])[None, :])
    m['lam_row'] = f(np.concatenate([inp['lambda_q1'][0], inp['lambda_q2'][0],
                                     inp['lambda_k1'][0], inp['lambda_k2'][0]])[None, :])
    m['gda_row'] = f(np.asarray(inp['g_da_out'][0])[None, :])
    m['gret_row'] = f(np.asarray(inp['g_ret_out'][0])[None, :])
    m['w_out'] = f(inp['w_out'][0])
    m['w_up'] = f(inp['w_up'][0])
    m['w_down'] = f(inp['w_down'][0])
    return m


def kernel(**inputs):
    S = 4096
    inp = {k: np.asarray(v) for k, v in inputs.items()}
    B = inp['x'].shape[0]
    tabs = host_tables(S)
    nc = build_program(S)
    in_maps = [core_inputs(b, S, inp, tabs) for b in range(B)]
    res = run_bass_kernel_spmd(nc, in_maps, core_ids=list(range(B)))
    out = np.stack([np.asarray(res.results[b]['out']) for b in range(B)], 0)
    return out.astype(np.float32)
```

```python
import math
from contextlib import ExitStack

import numpy as np
import concourse.bass as bass
import concourse.mybir as mybir
from concourse.bass_utils import run_bass_kernel_spmd

F32 = mybir.dt.float32
BF16 = mybir.dt.bfloat16
AF = mybir.ActivationFunctionType
ALU = mybir.AluOpType
AX = mybir.AxisListType

D = 1024
DFF = 4096
EPS = 1e-6
LAMBDA_INIT = 0.2
C_QA, C_KA, C_VA, C_QR, C_KR, C_VR, C_GR, C_GA, C_GB = 0, 1024, 2048, 3072, 3584, 4096, 5120, 6144, 7168
ARENA_BYTES = 212736
STRICT_SAME_ENGINE = True
FUSE_WAIT = True


class Sched:
    CE = ('pe', 'act', 'dve', 'pool')

    def __init__(self, nc, es):
        self.nc, self.es = nc, es
        self.q = {e: [] for e in ('pe', 'act', 'dve', 'pool', 'sp')}
        self.sem = {e: es.enter_context(nc.semaphore('s_' + e)) for e in self.CE}
        self.cnt = {e: 0 for e in self.CE}
        self.lastw, self.reads = {}, {}
        self.dsem = {}
        self.waited = {e: {} for e in self.q}
        self.bufs, self.bufkeys, self.inherit, self.known = [], {}, {}, set()

    @staticmethod
    def _merge(d, ev):
        k = id(ev[0])
        if k not in d or d[k][1] < ev[1]:
            d[k] = ev

    def new_buffer(self, name, lo, hi):
        inh = {}
        keep = []
        for (lo2, hi2, n2) in self.bufs:
            if lo2 < hi and lo < hi2:
                for k in self.bufkeys.get(n2, ()):
                    ev = self.lastw.get(k)
                    if ev is not None:
                        self._merge(inh, ev)
                    for ev in self.reads.get(k, {}).values():
                        self._merge(inh, ev)
                for ev in self.inherit.get(n2, {}).values():
                    self._merge(inh, ev)
                if lo <= lo2 and hi2 <= hi:
                    continue
            keep.append((lo2, hi2, n2))
        keep.append((lo, hi, name))
        self.bufs = keep
        self.inherit[name] = inh

    def _touch(self, k):
        if k in self.known:
            return
        self.known.add(k)
        name = k[0] if isinstance(k, tuple) else k
        self.bufkeys.setdefault(name, set()).add(k)
        self.reads[k] = dict(self.inherit.get(name, {}))

    def op(self, eng, fn, r=(), w=(), dma=None, ndma=1, sig=True):
        waits = {}
        K = self.waited[eng]

        def need(ev, raw):
            if ev is None:
                return
            sem, val, src, clk = ev
            if src == eng and (eng == 'pe' or (not raw and not STRICT_SAME_ENGINE)):
                return
            k = id(sem)
            if K.get(k, 0) >= val:
                return
            if k not in waits or waits[k][1] < val:
                waits[k] = ev

        for k in r:
            self._touch(k)
            need(self.lastw.get(k), True)
        for k in w:
            self._touch(k)
            need(self.lastw.get(k), False)
            for ev in self.reads[k].values():
                need(ev, False)
        sel = sorted(waits.values(), key=lambda ev: -len(ev[3]))
        final = []
        for ev in sel:
            if K.get(id(ev[0]), 0) >= ev[1]:
                continue
            final.append((ev[0], ev[1]))
            for kk, vv in ev[3].items():
                if K.get(kk, 0) < vv:
                    K[kk] = vv
        if dma is not None:
            if dma not in self.dsem:
                self.dsem[dma] = [self.es.enter_context(self.nc.semaphore('d_' + dma)), 0]
            ds = self.dsem[dma]
            ds[1] += 16 * ndma
            clk = dict(K)
            clk[id(ds[0])] = ds[1]
            ev = (ds[0], ds[1], 'dma', clk)
            evh = ('dma', ds[0], ndma)
        elif sig:
            self.cnt[eng] += 1
            clk = dict(K)
            clk[id(self.sem[eng])] = self.cnt[eng]
            ev = (self.sem[eng], self.cnt[eng], eng, clk)
            evh = ('ce', self.sem[eng])
        else:
            ev, evh = None, ('none',)
        self.q[eng].append((final, fn, evh))
        for k in r:
            if ev is not None:
                self._merge(self.reads[k], ev)
        for k in w:
            self.lastw[k] = ev
            self.reads[k] = {}
        return ev

    def emit(self, eng, e):
        class Rec:
            def __init__(s_, tgt):
                s_.tgt, s_.first = tgt, None

            def __getattr__(s_, name):
                attr = getattr(s_.tgt, name)
                if not callable(attr):
                    return attr

                def w(*a, **k):
                    r = attr(*a, **k)
                    if s_.first is None:
                        s_.first = r
                    return r
                return w
        for waits, fn, evh in self.q[eng]:
            fused = None
            if FUSE_WAIT and waits:
                fused = waits[-1]
                waits = waits[:-1]
            for sem, val in waits:
                e.wait_ge(sem, val)
            if fused is not None:
                rec = Rec(e)
                ins = fn(rec)
                rec.first._wait_ge(fused[0], fused[1])
            else:
                ins = fn(e)
            if evh[0] == 'ce':
                ins.then_inc(evh[1], 1)
            elif evh[0] == 'dma':
                if not isinstance(ins, (list, tuple)):
                    ins = [ins]
                assert len(ins) == evh[2], (len(ins), evh[2])
                for i in ins:
                    i.then_inc(evh[1], 16)

    def run_block(self):
        with self.nc.Block() as block:
            @block.tensor
            def _(e):
                self.emit('pe', e)

            @block.scalar
            def _(e):
                self.emit('act', e)

            @block.vector
            def _(e):
                self.emit('dve', e)

            @block.gpsimd
            def _(e):
                self.emit('pool', e)

            @block.sync
            def _(e):
                self.emit('sp', e)


class Buf:
    def __init__(self, name, ap):
        self.name, self.ap = name, ap

    def k(self, *sub):
        return (self.name,) + sub


class KB:
    def __init__(self, nc, es, S, dbg=()):
        self.nc, self.es, self.S, self.dbg = nc, es, S, set(dbg)
        self.NT = S // 128
        self.sch = Sched(nc, es)
        self.arena = es.enter_context(nc.sbuf_tensor('arena', [128, ARENA_BYTES // 4], F32))
        self.top = 0
        self.nalloc = 0
        self.pbig = es.enter_context(nc.psum_tensor('pbig', [128, 8 * 512], F32))
        self.banks = [self.pbig[:, i * 512:(i + 1) * 512] for i in range(8)]
        self.outkeys = []
        self.deferred = []
        self.dr = {}

    def alloc(self, name, shape, dtype):
        esz = 4 if dtype == F32 else 2
        n = int(np.prod(shape[1:]))
        nbytes = (n * esz + 63) // 64 * 64
        lo, hi = self.top, self.top + nbytes
        assert hi <= ARENA_BYTES, ('SBUF arena overflow', name, hi)
        self.top = hi
        self.nalloc += 1
        uname = '%s#%d' % (name, self.nalloc)
        self.sch.new_buffer(uname, lo, hi)
        v = self.arena[:, lo // 4: hi // 4]
        if dtype != F32:
            v = v.bitcast(dtype)
        v = v[0:shape[0], 0:n]
        if len(shape) == 3:
            v = v.rearrange("p (a b) -> p a b", b=shape[2])
        elif len(shape) == 4:
            v = v.rearrange("p (a b c) -> p a b c", b=shape[2], c=shape[3])
        return Buf(uname, v)

    def ps(self, i):
        return ('ps', i)

    def bank_bf(self, i, lo, hi):
        return self.banks[i][:, lo:hi].bitcast(BF16)

    def op(self, *a, **k):
        return self.sch.op(*a, **k)

    def dram_in(self, name, shape, dtype=F32):
        t = self.nc.dram_tensor(name, list(shape), dtype, kind="ExternalInput").ap()
        self.dr[name] = t
        return t

    def dump(self, name, buf, keys, shape, src=None):
        if name not in self.dbg:
            return
        src = buf.ap if src is None else src
        t = self.nc.dram_tensor('dbg_' + name, list(shape), src.dtype, kind="ExternalOutput").ap()
        self.op('sp', lambda e: e.dma_start(out=t, in_=src), r=list(keys), w=[('dbgout', name)], dma='dbg_' + name)
        self.outkeys.append(('dbgout', name))

    @staticmethod
    def k2(buf, t):
        return [buf.k(t, 0), buf.k(t, 1)]

    def pump(self, n=1):
        for _ in range(n):
            if not self.deferred:
                return
            self.deferred.pop(0)()

    def flush(self):
        while self.deferred:
            self.deferred.pop(0)()

    def setup(self):
        nc, S, NT = self.nc, self.S, self.NT
        di = self.dram_in
        self.x = di('x', [S, D])
        di('c_col', [128, 8]); di('w_ada', [D, 6 * D]); di('b_ada_c', [128, 48])
        di('g1c', [128, 8]); di('g2c', [128, 8]); di('w_in', [D, 8192])
        di('gqk_row', [1, 256]); di('lam_row', [1, 256]); di('gda_row', [1, 128]); di('gret_row', [1, 256])
        di('w_out', [D, D]); di('w_up', [D, DFF]); di('w_down', [DFF, D])
        di('ropeA', [128, NT, 2, 64]); di('ropeR', [128, NT, 2, 128])
        di('retmask', [128, 128]); di('retc', [128, 8]); di('identf', [128, 128])
        self.out = nc.dram_tensor('out', [S, D], F32, kind="ExternalOutput").ap()
        self.mT_d = nc.dram_tensor('mT_d', [8, 128, S], BF16, kind="Internal").ap()
        self.wup_s = nc.dram_tensor('wup_s', [8, 128, 8, 512], BF16, kind="Internal").ap()
        self.wdn_s = nc.dram_tensor('wdn_s', [2, 8, 128, 4, 512], BF16, kind="Internal").ap()

        A = self.alloc
        dr = self.dr
        self.c_col = A('c_col', [128, 8], F32)
        self.b_ada_c = A('b_ada_c', [128, 48], F32)
        self.g1c = A('g1c', [128, 8], F32)
        self.g2c = A('g2c', [128, 8], F32)
        self.identf = A('identf', [128, 128], F32)
        self.ident = A('ident', [128, 128], BF16)
        self.onesf = A('onesf', [128, 128], F32)
        self.mhalf = A('mhalf', [128, 16], F32)
        self.epsc = A('epsc', [128, 1], F32)
        self.modc = A('modc', [128, 48], F32)
        self.A1 = A('A1', [128, 8], F32)
        self.A2 = A('A2', [128, 8], F32)
        self.retmask = A('retmask', [128, 128], F32)
        self.retc = A('retc', [128, 8], F32)
        self.gqk_b = A('gqk_b', [128, 256], F32)
        self.lam_b = A('lam_b', [128, 256], F32)
        self.gda_b = A('gda_b', [128, 128], F32)
        self.gret_b = A('gret_b', [128, 256], F32)
        self.lamc = A('lamc', [128, 1], F32)
        self.lamt = A('lamt', [128, 4], F32)
        self.lamp = A('lamp', [128, 128], F32)

        loads = [(self.c_col, dr['c_col']), (self.b_ada_c, dr['b_ada_c']), (self.g1c, dr['g1c']),
                 (self.g2c, dr['g2c']), (self.identf, dr['identf']), (self.retmask, dr['retmask']),
                 (self.retc, dr['retc']),
                 (self.gqk_b, dr['gqk_row'][0:1, :].broadcast_to([128, 256])),
                 (self.lam_b, dr['lam_row'][0:1, :].broadcast_to([128, 256])),
                 (self.gda_b, dr['gda_row'][0:1, :].broadcast_to([128, 128])),
                 (self.gret_b, dr['gret_row'][0:1, :].broadcast_to([128, 256]))]

        def ld(e):
            return [e.dma_start(out=b.ap, in_=src) for b, src in loads]
        self.op('sp', ld, w=[b.k() for b, _ in loads], dma='const', ndma=len(loads))
        self.op('dve', lambda e: e.tensor_copy(out=self.ident.ap, in_=self.identf.ap),
                r=[self.identf.k()], w=[self.ident.k()])
        self.op('pool', lambda e: e.memset(self.onesf.ap, 1.0), w=[self.onesf.k()])
        self.op('pool', lambda e: e.memset(self.mhalf.ap, -0.5), w=[self.mhalf.k()])
        self.op('pool', lambda e: e.memset(self.epsc.ap, EPS), w=[self.epsc.k()])
        self.op('dve', lambda e: e.tensor_scalar(out=self.gda_b.ap, in0=self.gda_b.ap,
                                                 scalar1=(1.0 - LAMBDA_INIT) * 0.5, scalar2=None, op0=ALU.mult),
                r=[self.gda_b.k()], w=[self.gda_b.k()])
        self.op('dve', lambda e: e.tensor_scalar(out=self.gret_b.ap, in0=self.gret_b.ap,
                                                 scalar1=0.25, scalar2=None, op0=ALU.mult),
                r=[self.gret_b.k()], w=[self.gret_b.k()])
        lb, lp, lt = self.lam_b, self.lamp, self.lamt
        self.op('dve', lambda e: e.tensor_tensor(out=lp.ap, in0=lb.ap[:, 0:128], in1=lb.ap[:, 128:256], op=ALU.mult),
                r=[lb.k()], w=[lp.k()])
        self.op('dve', lambda e: e.tensor_reduce(out=lt.ap[:, 0:2], in_=lp.ap.rearrange("p (a b) -> p a b", b=64),
                                                 axis=AX.X, op=ALU.add), r=[lp.k()], w=[lt.k(0)])
        self.op('act', lambda e: e.activation(out=lt.ap[:, 2:4], in_=lt.ap[:, 0:2], func=AF.Exp),
                r=[lt.k(0)], w=[lt.k(1)])
        self.op('dve', lambda e: e.tensor_tensor(out=self.lamc.ap, in0=lt.ap[:, 2:3], in1=lt.ap[:, 3:4],
                                                 op=ALU.subtract), r=[lt.k(1)], w=[self.lamc.k()])
        self.op('dve', lambda e: e.tensor_scalar(out=self.lamc.ap, in0=self.lamc.ap, scalar1=LAMBDA_INIT,
                                                 scalar2=None, op0=ALU.add), r=[self.lamc.k()], w=[self.lamc.k()])
        self.persist_top = self.top

    def ffn_scratch(self):
        dr = self.dr
        for c in range(8):
            def up(c=c):
                src = dr['w_up'][:, c * 512:(c + 1) * 512].rearrange("(dc p) f -> p dc f", p=128)
                self.op('pool', lambda e: e.dma_start(out=self.wup_s[c], in_=src), w=[('wup_s',)], dma='wup_s')
            self.bg.append(up)
        for nb in range(2):
            for c in range(8):
                def dn(nb=nb, c=c):
                    src = dr['w_down'][c * 512:(c + 1) * 512, nb * 512:(nb + 1) * 512].rearrange(
                        "(fb p) n -> p fb n", p=128)
                    self.op('pool', lambda e: e.dma_start(out=self.wdn_s[nb, c], in_=src), w=[('wdn_s',)], dma='wdn_s')
                self.bg.append(dn)

    def phase_mod(self):
        A, op, dr = self.alloc, self.op, self.dr
        th = A('th', [128, 8], F32)
        scf = A('scf', [128, 8], F32)
        scb = A('scb', [128, 8], BF16)
        was = [A('wa0', [128, 8, 1024], BF16), A('wa1', [128, 8, 1024], BF16)]
        cc = self.c_col
        op('act', lambda e: e.activation(out=th.ap, in_=cc.ap, func=AF.Tanh, scale=0.5), r=[cc.k()], w=[th.k()])
        op('dve', lambda e: e.scalar_tensor_tensor(out=scf.ap, in0=th.ap, scalar=1.0, in1=cc.ap,
                                                   op0=ALU.add, op1=ALU.mult), r=[th.k(), cc.k()], w=[scf.k()])
        op('dve', lambda e: e.tensor_scalar(out=scb.ap, in0=scf.ap, scalar1=0.5, scalar2=None, op0=ALU.mult),
           r=[scf.k()], w=[scb.k()])
        bank = self.banks[7]
        mc = self.modc

        def dma(v):
            wa = was[v % 2]
            src = dr['w_ada'][:, v * 1024:(v + 1) * 1024].rearrange("(kc p) n -> p kc n", p=128)
            op('pool', lambda e: e.dma_start(out=wa.ap, in_=src), w=[wa.k()], dma='wa%d' % (v % 2))

        def mmv(v):
            wa = was[v % 2]

            def mm(e):
                for j in range(8):
                    for kc in range(8):
                        last = e.matmul(bank[:, v * 8 + j: v * 8 + j + 1], lhsT=wa.ap[:, kc, j * 128:(j + 1) * 128],
                                        rhs=scb.ap[:, kc:kc + 1], start=(kc == 0), stop=(kc == 7))
                return last
            op('pe', mm, r=[wa.k(), scb.k()], w=[self.ps(7)])

        def fin():
            op('dve', lambda e: e.tensor_tensor(out=mc.ap[:, 16:48], in0=bank[:, 16:48], in1=self.b_ada_c.ap[:, 16:48],
                                                op=ALU.add), r=[self.b_ada_c.k()], w=[self.ps(7), mc.k(1)])
            op('dve', lambda e: e.scalar_tensor_tensor(out=self.A2.ap, in0=mc.ap[:, 32:40], scalar=1.0, in1=self.g2c.ap,
                                                       op0=ALU.add, op1=ALU.mult), r=[mc.k(1), self.g2c.k()], w=[self.A2.k()])
            self.dump('modc', mc, [mc.k(0), mc.k(1)], [128, 48])

        dma(0); dma(1); mmv(0); mmv(1); dma(2); dma(3)
        op('dve', lambda e: e.tensor_tensor(out=mc.ap[:, 0:16], in0=bank[:, 0:16], in1=self.b_ada_c.ap[:, 0:16], op=ALU.add),
           r=[self.b_ada_c.k()], w=[self.ps(7), mc.k(0)])
        op('dve', lambda e: e.scalar_tensor_tensor(out=self.A1.ap, in0=mc.ap[:, 8:16], scalar=1.0, in1=self.g1c.ap,
                                                   op0=ALU.add, op1=ALU.mult), r=[mc.k(0), self.g1c.k()], w=[self.A1.k()])
        self.mod_rest = {6: [lambda: mmv(2), lambda: dma(4)], 12: [lambda: mmv(3), lambda: dma(5)],
                         18: [lambda: mmv(4)], 24: [lambda: mmv(5), fin]}

    def norm_to_T(self, xt_ap, xkey, t, stats, xn, A_, Bcol, dst_fn, dst_key, banks, tag):
        op = self.op
        ss, ms, rstd, junk = stats
        op('act', lambda e: e.activation(out=junk.ap, in_=xt_ap, func=AF.Square, accum_out=ss.ap[:, t:t + 1]),
           r=[xkey], w=[junk.k(), ss.k(t)])
        op('pool', lambda e: e.tensor_scalar(out=ms.ap[:, t:t + 1], in0=ss.ap[:, t:t + 1], scalar1=1.0 / D,
                                             scalar2=EPS, op0=ALU.mult, op1=ALU.add), r=[ss.k(t)], w=[ms.k(t)])
        op('pool', lambda e: e.tensor_tensor(out=rstd.ap[:, t:t + 1], in0=ms.ap[:, t:t + 1],
                                             in1=self.mhalf.ap[:, 0:1], op=ALU.pow),
           r=[ms.k(t), self.mhalf.k()], w=[rstd.k(t)])
        op('dve', lambda e: e.tensor_scalar(out=xn.ap, in0=xt_ap, scalar1=rstd.ap[:, t:t + 1], scalar2=None,
                                            op0=ALU.mult), r=[xkey, rstd.k(t)], w=[xn.k()])
        ba, bb = banks
        pa = self.bank_bf(ba, 0, 256).rearrange("p (a b) -> p a b", b=128)
        pb = self.bank_bf(bb, 0, 256).rearrange("p (a b) -> p a b", b=128)

        def tr(e):
            for dc in range(8):
                pt = pa if dc < 4 else pb
                last = e.transpose(out=pt[:, dc % 4, :], in_=xn.ap[:, dc * 128:(dc + 1) * 128], identity=self.ident.ap)
            return last
        op('pe', tr, r=[xn.k(), self.ident.k()], w=[self.ps(ba), self.ps(bb)])
        da = dst_fn(0, 4)
        db = dst_fn(4, 8)

        def ev_act(e):
            for dc in range(4):
                last = e.activation(out=da[:, dc, :], in_=pa[:, dc, :], func=AF.Identity,
                                    scale=A_.ap[:, dc:dc + 1], bias=Bcol[:, dc:dc + 1])
            return last
        op('act', ev_act, r=[A_.k(), self.modc.k(1)], w=[self.ps(ba), dst_key[0]])

        def ev_dve(e):
            for dc in range(4, 8):
                last = e.tensor_scalar(out=db[:, dc - 4, :], in0=pb[:, dc - 4, :], scalar1=A_.ap[:, dc:dc + 1],
                                       scalar2=Bcol[:, dc:dc + 1], op0=ALU.mult, op1=ALU.add)
            return last
        op('dve', ev_dve, r=[A_.k(), self.modc.k(1)], w=[self.ps(bb), dst_key[1]])

    def phase_norm1(self):
        A, op, NT = self.alloc, self.op, self.NT
        NG = NT // 4
        xg = [A('xg%d' % i, [128, 4, D], F32) for i in range(2)]
        xns = [A('xn%d' % i, [128, D], BF16) for i in range(3)]
        junk = A('junk', [128, D], BF16)
        ss = A('ss1', [128, NT], F32)
        ms = A('ms1', [128, NT], F32)
        rstd = A('rstd1', [128, NT], F32)
        hm = self.hmixT
        A_, Bcol = self.A1, self.modc.ap[:, 0:8]

        def sa(t):
            g, k = divmod(t, 4)
            xb = xg[g % 2]
            op('sp', lambda e: e.dma_start(out=xb.ap[:, k, :], in_=self.x[t * 128:(t + 1) * 128, :]),
               w=[xb.k(k)], dma='xg%d_%d' % (g % 2, k))
            op('act', lambda e: e.activation(out=junk.ap, in_=xb.ap[:, k, :], func=AF.Square,
                                             accum_out=ss.ap[:, t:t + 1]), r=[xb.k(k)], w=[junk.k(), ss.k(t)])

        def sb(g):
            sl = slice(4 * g, 4 * g + 4)
            op('act', lambda e: e.activation(out=ms.ap[:, sl], in_=ss.ap[:, sl], func=AF.Sqrt, scale=1.0 / D, bias=self.epsc.ap),
               r=[ss.k(t) for t in range(4 * g, 4 * g + 4)] + [self.epsc.k()], w=[ms.k(g)])
            op('dve', lambda e: e.reciprocal(out=rstd.ap[:, sl], in_=ms.ap[:, sl]), r=[ms.k(g)], w=[rstd.k(g)])

        def sc(t):
            g, k = divmod(t, 4)
            xb = xg[g % 2]
            xn = xns[t % 3]
            op('dve', lambda e: e.tensor_scalar(out=xn.ap, in0=xb.ap[:, k, :], scalar1=rstd.ap[:, t:t + 1], scalar2=None,
                                                op0=ALU.mult), r=[xb.k(k), rstd.k(g)], w=[xn.k()])
            ba, bb = (0, 1) if t % 2 == 0 else (2, 3)
            pa = self.bank_bf(ba, 0, 256).rearrange("p (a b) -> p a b", b=128)
            pb = self.bank_bf(bb, 0, 256).rearrange("p (a b) -> p a b", b=128)

            def tr(e):
                for dc in range(8):
                    pt = pa if dc < 4 else pb
                    last = e.transpose(out=pt[:, dc % 4, :], in_=xn.ap[:, dc * 128:(dc + 1) * 128], identity=self.ident.ap)
                return last
            op('pe', tr, r=[xn.k(), self.ident.k()], w=[self.ps(ba), self.ps(bb)])

            def ev_act(e):
                for dc in range(4):
                    last = e.activation(out=hm.ap[:, dc, t * 128:(t + 1) * 128], in_=pa[:, dc, :], func=AF.Identity,
                                        scale=A_.ap[:, dc:dc + 1], bias=Bcol[:, dc:dc + 1])
                return last
            op('act', ev_act, r=[A_.k(), self.modc.k(0)], w=[self.ps(ba), hm.k(t, 0)])

            def ev_dve(e):
                for dc in range(4, 8):
                    last = e.tensor_scalar(out=hm.ap[:, dc, t * 128:(t + 1) * 128], in0=pb[:, dc - 4, :],
                                           scalar1=A_.ap[:, dc:dc + 1], scalar2=Bcol[:, dc:dc + 1],
                                           op0=ALU.mult, op1=ALU.add)
                return last
            op('dve', ev_dve, r=[A_.k(), self.modc.k(0)], w=[self.ps(bb), hm.k(t, 1)])

        for i in range(NT + 6):
            if i < NT:
                sa(i)
                if i % 4 == 3:
                    sb(i // 4)
            if 0 <= i - 6 < NT:
                sc(i - 6)
            for f in self.mod_rest.pop(i, []):
                f()
        for i in sorted(self.mod_rest):
            for f in self.mod_rest[i]:
                f()
        self.mod_rest = {}
        self.dump('hmixT', hm, sum([self.k2(hm, t) for t in range(NT)], []), [128, 8, self.S])

    def load_w(self, slot, blocks):
        dr = self.dr
        off = 0
        srcs = []
        for c0, n in blocks:
            srcs.append((off, n, dr['w_in'][:, c0:c0 + n].rearrange("(dc p) n -> p dc n", p=128)))
            off += n
        ws = self.wslots[slot]

        def ld(e):
            return [e.dma_start(out=ws.ap[:, :, o:o + n], in_=src) for o, n, src in srcs]
        self.op('pool', ld, w=[ws.k()], dma='wslot%d' % slot, ndma=len(srcs))

    def ret_head(self, r, slot):
        A, op, NT, S = self.alloc, self.op, self.NT, self.S
        mark = self.top
        ws = self.wslots[slot]
        hm = self.hmixT
        ropeR = A('ropeR', [128, NT, 2, 128], F32)
        npc = 4 if NT >= 4 else 1
        cpp = NT // npc
        for pc in range(npc):
            op('sp', lambda e, pc=pc: e.dma_start(out=ropeR.ap[:, pc * cpp:(pc + 1) * cpp],
                                                  in_=self.dr['ropeR'][:, pc * cpp:(pc + 1) * cpp]),
               w=[ropeR.k(pc)], dma='ropeR%d' % pc)
        qkf = [A('qkf%d' % i, [128, 256], F32) for i in range(3)]
        tq = [[A('tq%d_%d' % (i, j), [128, 128], F32) for j in range(2)] for i in range(2)]
        tk = [[A('tk%d_%d' % (i, j), [128, 128], F32) for j in range(2)] for i in range(2)]
        qt = [A('qt%d' % i, [128, 128], BF16) for i in range(3)]
        kt = [A('kt%d' % i, [128, 128], BF16) for i in range(3)]
        kp = [A('kp%d' % i, [128, 128], BF16) for i in range(4)]
        vb = [A('vb%d' % i, [128, 256], BF16) for i in range(10)]
        qkT = [A('qkT%d' % i, [128, 256], BF16) for i in range(6)]
        scm = [A('scm%d' % i, [128, 128], BF16) for i in range(4)]
        ysb = [A('ysb%d' % i, [128, 256], F32) for i in range(4)]
        St = A('St', [128, 256], F32)
        Sbf = [A('Sbf%d' % i, [128, 256], BF16) for i in range(4)]
        junk = A('rjunk', [128, 256], BF16)
        ssr = A('ssr', [128, NT], F32)
        msr = A('msr', [128, NT], F32)
        rsr = A('rsr', [128, NT], F32)
        g = 1.0 - 2.0 ** (-5.0 - r)
        gC = float(np.exp(128.0 * np.log(g)))
        idec = self.retc.ap[:, 2 * r:2 * r + 1]
        kdec = self.retc.ap[:, 2 * r + 1:2 * r + 2]
        orr = self.orr
        mask = self.retmask
        h3 = lambda ap: ap.rearrange("p (h d) -> p h d", h=2)

        def swapped(ap):
            a = [list(x) for x in ap.ap]
            return bass.AP(ap.tensor, ap.offset + 64, [a[0], [-64, 2], [1, 64]])

        def s_proj(c):
            pb = c % 2

            def mm(e):
                for dc in range(8):
                    last = e.matmul(self.banks[pb][:, 0:512], lhsT=hm.ap[:, dc, c * 128:(c + 1) * 128],
                                    rhs=ws.ap[:, dc, 0:512], start=(dc == 0), stop=(dc == 7))
                return last
            op('pe', mm, r=self.k2(hm, c) + [ws.k()], w=[self.ps(pb)])

        def s_evac(c):
            pb = c % 2
            P = self.banks[pb]
            f_, v_ = qkf[c % 3], vb[c % 10]
            op('act', lambda e: e.activation(out=f_.ap, in_=P[:, 0:256], func=AF.Copy), w=[self.ps(pb), f_.k()])
            op('act', lambda e: e.activation(out=v_.ap, in_=P[:, 256:512], func=AF.Copy), w=[self.ps(pb), v_.k()])

        def s_rope(c):
            b = c % 2
            f_ = qkf[c % 3]
            rk = ropeR.k(c // cpp)
            cc = ropeR.ap[:, c, 0, :]
            sn = ropeR.ap[:, c, 1, :]
            for (src, dec, tt) in ((f_.ap[:, 0:128], idec, tq[b]), (f_.ap[:, 128:256], kdec, tk[b])):
                op('dve', lambda e, src=src, dec=dec, tt=tt: e.scalar_tensor_tensor(
                    out=tt[0].ap, in0=src, scalar=dec, in1=cc, op0=ALU.mult, op1=ALU.mult),
                   r=[rk, self.retc.k(), f_.k()], w=[tt[0].k()])
                op('dve', lambda e, src=src, dec=dec, tt=tt: e.scalar_tensor_tensor(
                    out=h3(tt[1].ap), in0=swapped(src), scalar=dec, in1=h3(sn), op0=ALU.mult, op1=ALU.mult),
                   r=[rk, self.retc.k(), f_.k()], w=[tt[1].k()])

        def s_add(c):
            b = c % 2
            for tt, dst in ((tq[b], qt[c % 3]), (tk[b], kt[c % 3])):
                op('pool', lambda e, tt=tt, dst=dst: e.tensor_tensor(out=dst.ap, in0=tt[0].ap, in1=tt[1].ap, op=ALU.add),
                   r=[tt[0].k(), tt[1].k()], w=[dst.k()])

        def s_tr(c):
            q_, k_, kp_ = qt[c % 3], kt[c % 3], kp[c % 4]
            op('act', lambda e: e.activation(out=kp_.ap, in_=k_.ap, func=AF.Copy, scale=gC), r=[k_.k()], w=[kp_.k()])
            pT = self.bank_bf(4, 0, 128)

            def tr(e):
                e.transpose(out=pT[:, 0:128], in_=q_.ap, identity=self.ident.ap)
                return e.transpose(out=pT[:, 128:256], in_=k_.ap, identity=self.ident.ap)
            op('pe', tr, r=[q_.k(), k_.k(), self.ident.k()], w=[self.ps(4)])

        def s_trev(c):
            T_ = qkT[c % 6]
            pT = self.bank_bf(4, 0, 128)
            op('act', lambda e: e.activation(out=T_.ap, in_=pT, func=AF.Copy), w=[self.ps(4), T_.k()])

        def s_sc(c):
            T_, v_, kp_ = qkT[c % 6], vb[c % 10], kp[c % 4]
            op('pe', lambda e: e.matmul(self.banks[5][:, 0:128], lhsT=T_.ap[:, 128:256], rhs=T_.ap[:, 0:128],
                                        start=True, stop=True), r=[T_.k()], w=[self.ps(5)])
            if c < NT - 1:
                kb = 3 if c % 2 == 0 else 7
                op('pe', lambda e: e.matmul(self.banks[kb][:, 0:256], lhsT=kp_.ap, rhs=v_.ap, start=True, stop=True),
                   r=[kp_.k(), v_.k()], w=[self.ps(kb)])

        def s_mask(c):
            sc_ = scm[c % 4]
            op('dve', lambda e: e.tensor_tensor(out=sc_.ap, in0=self.banks[5][:, 0:128], in1=mask.ap, op=ALU.mult),
               r=[mask.k()], w=[self.ps(5), sc_.k()])
            if c < NT - 1:
                kb = 3 if c % 2 == 0 else 7
                if c == 0:
                    op('dve', lambda e: e.tensor_copy(out=St.ap, in_=self.banks[kb][:, 0:256]), w=[self.ps(kb), St.k()])
                else:
                    op('dve', lambda e: e.scalar_tensor_tensor(out=St.ap, in0=St.ap, scalar=gC, in1=self.banks[kb][:, 0:256],
                                                               op0=ALU.mult, op1=ALU.add),
                       r=[St.k()], w=[self.ps(kb), St.k()])

        def s_sbf(c):
            if c < NT - 1:
                sb_ = Sbf[(c + 1) % 4]
                op('act', lambda e: e.activation(out=sb_.ap, in_=St.ap, func=AF.Copy), r=[St.k()], w=[sb_.k()])

        def s_y(c):
            T_, sc_, v_ = qkT[c % 6], scm[c % 4], vb[c % 10]
            yb = 2 if c % 2 == 0 else 6
            sb_ = Sbf[c % 4]

            def ymm(e):
                last = e.matmul(self.banks[yb][:, 0:256], lhsT=sc_.ap, rhs=v_.ap, start=True, stop=(c == 0))
                if c > 0:
                    last = e.matmul(self.banks[yb][:, 0:256], lhsT=T_.ap[:, 0:128], rhs=sb_.ap, start=False, stop=True)
                return last
            op('pe', ymm, r=[sc_.k(), v_.k(), T_.k()] + ([sb_.k()] if c > 0 else []), w=[self.ps(yb)])

        def s_yev(c):
            yb = 2 if c % 2 == 0 else 6
            y_ = ysb[c % 4]
            op('act', lambda e: e.activation(out=y_.ap, in_=self.banks[yb][:, 0:256], func=AF.Copy), w=[self.ps(yb), y_.k()])

        def s_ysq(c):
            y_ = ysb[c % 4]
            op('act', lambda e: e.activation(out=junk.ap, in_=y_.ap, func=AF.Square, accum_out=ssr.ap[:, c:c + 1]),
               r=[y_.k()], w=[junk.k(), ssr.k(c)])

        def s_ysqrt(c):
            op('act', lambda e: e.activation(out=msr.ap[:, c:c + 1], in_=ssr.ap[:, c:c + 1], func=AF.Sqrt,
                                             scale=1.0 / 256, bias=self.epsc.ap), r=[ssr.k(c), self.epsc.k()], w=[msr.k(c)])

        def s_orr(c):
            y_ = ysb[c % 4]
            op('dve', lambda e: e.reciprocal(out=rsr.ap[:, c:c + 1], in_=msr.ap[:, c:c + 1]), r=[msr.k(c)], w=[rsr.k(c)])
            op('dve', lambda e: e.scalar_tensor_tensor(out=orr.ap[:, c, :], in0=y_.ap, scalar=rsr.ap[:, c:c + 1],
                                                       in1=self.gret_b.ap, op0=ALU.mult, op1=ALU.mult),
               r=[rsr.k(c), self.gret_b.k(), y_.k()], w=[orr.k(c)])

        stages = [(1, s_evac), (2, s_rope), (3, s_add), (5, s_trev), (4, s_tr), (8, s_sbf), (7, s_mask), (6, s_sc),
                  (10, s_yev), (9, s_y), (10, s_ysq), (10, s_ysqrt), (11, s_orr)]
        for i in range(NT + 12):
            for d, f in stages:
                if 0 <= i - d < NT:
                    f(i - d)
            if i < NT:
                s_proj(i)
        self.dump('orr%d' % r, orr, [orr.k(c) for c in range(NT)], [128, NT, 256])
        self.top = mark

    def da_head(self, h, slot):
        A, op, NT, S = self.alloc, self.op, self.NT, self.S
        NQ = S // 512
        mark = self.top
        ws = self.wslots[slot]
        hm = self.hmixT
        ropeA = self.ropeA
        QKT = A('QKT', [128, 2, S], BF16)
        V1 = A('V1', [128, NT, 129], BF16)
        exps = [A('exp%d' % i, [128, 2, 512], BF16) for i in range(4)]
        ss4 = A('ss4', [128, NT, 4], F32)
        ms4 = A('ms4', [128, NT, 4], F32)
        rs4 = A('rs4', [128, NT, 4], F32)
        markP = self.top
        sqj = [A('sqj%d' % i, [128, 256], F32) for i in range(2)]
        u = [A('u%d' % i, [128, 256], F32) for i in range(8)]
        t1 = [A('t1%d' % i, [128, 256], F32) for i in range(2)]
        t2 = [A('t2%d' % i, [128, 256], F32) for i in range(2)]
        o_ = [A('o%d' % i, [128, 256], F32) for i in range(2)]
        qk = [A('qk%d' % i, [128, 256], BF16) for i in range(4)]
        orr = self.orr
        hh = h % 2

        op('pool', lambda e: e.memset(V1.ap[:, :, 128:129], 1.0), w=[V1.k('ones')])

        g4 = lambda ap: ap.rearrange("p (g d) -> p g d", g=4)
        g42 = lambda ap: ap.rearrange("p (g h d) -> p g h d", g=4, h=2)

        def swapped(ap):
            a = [list(x) for x in ap.ap]
            return bass.AP(ap.tensor, ap.offset + 32, [a[0], [64, 4], [-32, 2], [1, 32]])

        def p1(t):
            pb = t % 2

            def mm(e):
                for dc in range(8):
                    last = e.matmul(self.banks[pb][:, 0:384], lhsT=hm.ap[:, dc, t * 128:(t + 1) * 128],
                                    rhs=ws.ap[:, dc, 0:384], start=(dc == 0), stop=(dc == 7))
                return last
            op('pe', mm, r=self.k2(hm, t) + [ws.k()], w=[self.ps(pb)])

        def pA(t):
            pb = t % 2
            P = self.banks[pb]
            sq, uu_ = sqj[t % 2], u[t % 8]
            op('act', lambda e: e.activation(out=uu_.ap, in_=P[:, 0:256], func=AF.Copy), w=[self.ps(pb), uu_.k()])
            op('act', lambda e: e.activation(out=V1.ap[:, t, 0:128], in_=P[:, 256:384], func=AF.Copy),
               w=[self.ps(pb), V1.k(t)])
            op('act', lambda e: e.activation(out=sq.ap, in_=P[:, 0:256], func=AF.Square), w=[self.ps(pb), sq.k()])

        def pB(t):
            sq, uu_ = sqj[t % 2], u[t % 8]
            op('dve', lambda e: e.tensor_reduce(out=ss4.ap[:, t, :], in_=g4(sq.ap), axis=AX.X, op=ALU.add),
               r=[sq.k()], w=[ss4.k(t)])
            op('dve', lambda e: e.tensor_tensor(out=uu_.ap, in0=uu_.ap, in1=self.gqk_b.ap, op=ALU.mult),
               r=[self.gqk_b.k(), uu_.k()], w=[uu_.k()])

        def pC(t):
            op('act', lambda e: e.activation(out=ms4.ap[:, t, :], in_=ss4.ap[:, t, :], func=AF.Sqrt, scale=1.0 / 64,
                                             bias=self.epsc.ap), r=[ss4.k(t), self.epsc.k()], w=[ms4.k(t)])

        def pD(t):
            b = t % 2
            uu_ = u[t % 8]
            op('dve', lambda e: e.reciprocal(out=rs4.ap[:, t, :], in_=ms4.ap[:, t, :]), r=[ms4.k(t)], w=[rs4.k(t)])
            cc = ropeA.ap[:, t, 0:1, :].broadcast_to([128, 4, 64])
            sn = ropeA.ap[:, t, 1, :].rearrange("p (h d) -> p h d", h=2).unsqueeze(1).broadcast_to([128, 4, 2, 32])
            op('pool', lambda e: e.tensor_tensor(out=g4(t1[b].ap), in0=g4(uu_.ap), in1=cc, op=ALU.mult),
               r=[uu_.k(), ropeA.k()], w=[t1[b].k()])
            op('dve', lambda e: e.tensor_tensor(out=g42(t2[b].ap), in0=swapped(uu_.ap), in1=sn, op=ALU.mult),
               r=[uu_.k(), ropeA.k()], w=[t2[b].k()])

        def pE(t):
            b = t % 2
            op('pool', lambda e: e.tensor_tensor(out=o_[b].ap, in0=t1[b].ap, in1=t2[b].ap, op=ALU.add),
               r=[t1[b].k(), t2[b].k()], w=[o_[b].k()])

        def pF(t):
            b = t % 2
            rb = rs4.ap[:, t, :].unsqueeze(2).broadcast_to([128, 4, 64])
            q_ = qk[t % 4]
            op('dve', lambda e: e.tensor_tensor(out=g4(q_.ap), in0=g4(o_[b].ap), in1=rb, op=ALU.mult),
               r=[o_[b].k(), rs4.k(t)], w=[q_.k()])

        def pG(t):
            q_ = qk[t % 4]
            tb = 2 + (t % 2)
            pT = self.bank_bf(tb, 0, 128).rearrange("p (a b) -> p a b", b=128)

            def tr(e):
                e.transpose(out=pT[:, 0, :], in_=q_.ap[:, 0:128], identity=self.ident.ap)
                return e.transpose(out=pT[:, 1, :], in_=q_.ap[:, 128:256], identity=self.ident.ap)
            op('pe', tr, r=[q_.k(), self.ident.k()], w=[self.ps(tb)])
            op('act', lambda e: e.activation(out=QKT.ap[:, :, t * 128:(t + 1) * 128], in_=pT, func=AF.Copy),
               w=[self.ps(tb), QKT.k(t)])

        stages = [(1, pA), (2, pB), (2, pC), (3, pD), (4, pE), (4, pF), (5, pG)]
        for i in range(NT + 6):
            for d, f in stages:
                if 0 <= i - d < NT:
                    f(i - d)
            if i < NT:
                p1(i)
        self.dump('QKT%d' % h, QKT, [QKT.k(t) for t in range(NT)], [128, 2, S])
        self.dump('V1_%d' % h, V1, [V1.k(t) for t in range(NT)] + [V1.k('ones')], [128, NT, 129])

        self.top = markP
        accS = [A('accS%d' % i, [128, 8, 129], F32) for i in range(4)]
        rr = A('rr', [128, NT, 4], F32)
        sso = A('sso', [128, NT], F32)
        mso = A('mso', [128, NT], F32)
        rso = A('rso', [128, NT], F32)
        tt = [A('tt%d' % i, [128, 128], F32) for i in range(2)]
        oo = [A('oo%d' % i, [128, 128], F32) for i in range(2)]
        ojk = A('ojk', [128, 128], F32)
        on = [A('on%d' % i, [128, 128], F32) for i in range(2)]
        tg = [A('tg%d' % i, [128, 384], F32) for i in range(2)]
        uu = [A('uu%d' % i, [128, 128], F32) for i in range(2)]
        u2 = [A('u2%d' % i, [128, 128], F32) for i in range(2)]
        u3 = [A('u3%d' % i, [128, 128], F32) for i in range(2)]
        m1 = [A('m1%d' % i, [128, 128], F32) for i in range(2)]
        mg = [A('mg%d' % i, [128, 128], BF16) for i in range(2)]
        mst = [A('mst%d' % i, [128, 512], BF16) for i in range(4)]
        acc_loc = [(4 + i // 3, (i % 3) * 129) for i in range(8)]

        def acc_ap(qb, m):
            bk, off = acc_loc[qb * 2 + m]
            return self.banks[bk][:, off:off + 129]

        def sS(j, kt, step):
            i = kt - 4 * j
            c0 = 128 * i if i > 0 else 0
            sa, sb_ = (0, 1) if step % 2 == 0 else (2, 3)

            def mm(e):
                e.matmul(self.banks[sa][:, c0:512], lhsT=QKT.ap[0:64, 1, kt * 128:(kt + 1) * 128],
                         rhs=QKT.ap[0:64, 0, j * 512 + c0:(j + 1) * 512], start=True, stop=True)
                return e.matmul(self.banks[sb_][:, c0:512], lhsT=QKT.ap[64:128, 1, kt * 128:(kt + 1) * 128],
                                rhs=QKT.ap[64:128, 0, j * 512 + c0:(j + 1) * 512], start=True, stop=True)
            rk = [QKT.k(kt)] + [QKT.k(4 * j + q) for q in range(4)]
            op('pe', mm, r=rk, w=[self.ps(sa), self.ps(sb_)])
            ex = exps[step % 4]
            src = self.pbig[:, sa * 512:(sa + 2) * 512].rearrange("p (m c) -> p m c", m=2)[:, :, c0:512]
            op('act', lambda e: e.activation(out=ex.ap[:, :, c0:512], in_=src, func=AF.Exp, scale=0.125),
               w=[self.ps(sa), self.ps(sb_), ex.k()])
            if i >= 0:
                op('pool', lambda e: e.memset(ex.ap[64:128, :, c0:c0 + 64], 0.0), w=[ex.k()])

        def sV(j, kt, step):
            i = kt - 4 * j
            ex = exps[step % 4]
            qb0 = max(i, 0)

            def mm(e):
                for qb in range(qb0, 4):
                    for m in range(2):
                        idx = qb * 2 + m
                        last = e.matmul(acc_ap(qb, m), lhsT=ex.ap[:, m, qb * 128:(qb + 1) * 128], rhs=V1.ap[:, kt, :],
                                        start=(kt == 0 and idx % 3 == 0), stop=(kt == 4 * j + qb),
                                        skip_group_check=True)
                return last
            op('pe', mm, r=[ex.k(), V1.k(kt), V1.k('ones')], w=[self.ps(4), self.ps(5), self.ps(6)])

        def merge_steps(j):
            ab = accS[j % 4]
            av = ab.ap
            flat = av.rearrange("p i c -> p (i c)")
            op('dve', lambda e: e.tensor_copy(out=flat[:, 0:387], in_=self.banks[4][:, 0:387]), w=[self.ps(4), ab.k(0)])
            op('dve', lambda e: e.tensor_copy(out=flat[:, 387:774], in_=self.banks[5][:, 0:387]), w=[self.ps(5), ab.k(1)])
            op('dve', lambda e: e.tensor_copy(out=flat[:, 774:1032], in_=self.banks[6][:, 0:258]), w=[self.ps(6), ab.k(2)])
            ms_ = mst[j % 4]
            steps = []
            stages = []
            for qb in range(4):
                tb = 4 * j + qb
                b = qb % 2

                def M1(qb=qb, tb=tb, b=b):
                    op('dve', lambda e: e.reciprocal(out=rr.ap[:, tb, 0:2], in_=av[:, 2 * qb:2 * qb + 2, 128]),
                       r=[ab.k(0), ab.k(1), ab.k(2)], w=[rr.k(tb, 0)])
                    op('dve', lambda e: e.tensor_tensor(out=rr.ap[:, tb, 2:3], in0=rr.ap[:, tb, 1:2], in1=self.lamc.ap,
                                                        op=ALU.mult), r=[rr.k(tb, 0), self.lamc.k()], w=[rr.k(tb, 1)])
                    op('dve', lambda e: e.tensor_scalar(out=tt[b].ap, in0=av[:, 2 * qb + 1, 0:128],
                                                        scalar1=rr.ap[:, tb, 2:3], scalar2=None, op0=ALU.mult),
                       r=[ab.k(0), ab.k(1), ab.k(2), rr.k(tb, 1)], w=[tt[b].k()])
                    op('dve', lambda e: e.scalar_tensor_tensor(out=oo[b].ap, in0=av[:, 2 * qb, 0:128],
                                                               scalar=rr.ap[:, tb, 0:1], in1=tt[b].ap,
                                                               op0=ALU.mult, op1=ALU.subtract),
                       r=[ab.k(0), ab.k(1), ab.k(2), rr.k(tb, 0), tt[b].k()], w=[oo[b].k()])
                    op('dve', lambda e: e.scalar_tensor_tensor(out=ojk.ap, in0=oo[b].ap, scalar=1.0, in1=oo[b].ap,
                                                               op0=ALU.mult, op1=ALU.mult,
                                                               accum_out=sso.ap[:, tb:tb + 1]),
                       r=[oo[b].k()], w=[ojk.k(), sso.k(tb)])
                    op('pool', lambda e: e.tensor_scalar(out=mso.ap[:, tb:tb + 1], in0=sso.ap[:, tb:tb + 1],
                                                         scalar1=1.0 / 128, scalar2=EPS, op0=ALU.mult, op1=ALU.add),
                       r=[sso.k(tb)], w=[mso.k(tb)])
                    op('pool', lambda e: e.tensor_tensor(out=rso.ap[:, tb:tb + 1], in0=mso.ap[:, tb:tb + 1],
                                                         in1=self.mhalf.ap[:, 0:1], op=ALU.pow),
                       r=[mso.k(tb), self.mhalf.k()], w=[rso.k(tb)])
                    op('dve', lambda e: e.scalar_tensor_tensor(out=on[b].ap, in0=oo[b].ap, scalar=rso.ap[:, tb:tb + 1],
                                                               in1=self.gda_b.ap, op0=ALU.mult, op1=ALU.mult),
                       r=[oo[b].k(), rso.k(tb), self.gda_b.k()], w=[on[b].k()])

                def M2(qb=qb, tb=tb, b=b):
                    def mm(e):
                        for dc in range(8):
                            last = e.matmul(self.banks[7][:, 0:384], lhsT=hm.ap[:, dc, tb * 128:(tb + 1) * 128],
                                            rhs=ws.ap[:, dc, 384:768], start=(dc == 0), stop=(dc == 7))
                        return last
                    op('pe', mm, r=self.k2(hm, tb) + [ws.k()], w=[self.ps(7)])
                    op('act', lambda e: e.activation(out=tg[b].ap, in_=self.banks[7][:, 0:384], func=AF.Tanh, scale=0.5),
                       w=[self.ps(7), tg[b].k()])
                    op('dve', lambda e: e.scalar_tensor_tensor(out=uu[b].ap, in0=tg[b].ap[:, 256:384], scalar=1.0,
                                                               in1=self.banks[7][:, 256:384], op0=ALU.add, op1=ALU.mult),
                       r=[tg[b].k()], w=[self.ps(7), uu[b].k()])

                def M3(qb=qb, tb=tb, b=b):
                    op('pool', lambda e: e.tensor_tensor(out=u2[b].ap, in0=uu[b].ap,
                                                         in1=orr.ap[:, tb, hh * 128:(hh + 1) * 128], op=ALU.mult),
                       r=[uu[b].k(), orr.k(tb)], w=[u2[b].k()])
                    op('dve', lambda e: e.scalar_tensor_tensor(out=u3[b].ap, in0=tg[b].ap[:, 128:256], scalar=1.0,
                                                               in1=u2[b].ap, op0=ALU.add, op1=ALU.mult),
                       r=[tg[b].k(), u2[b].k()], w=[u3[b].k()])
                    op('dve', lambda e: e.scalar_tensor_tensor(out=m1[b].ap, in0=tg[b].ap[:, 0:128], scalar=1.0,
                                                               in1=on[b].ap, op0=ALU.add, op1=ALU.mult),
                       r=[tg[b].k(), on[b].k()], w=[m1[b].k()])
                    op('pool', lambda e: e.tensor_tensor(out=mg[b].ap, in0=m1[b].ap, in1=u3[b].ap, op=ALU.add),
                       r=[m1[b].k(), u3[b].k()], w=[mg[b].k()])

                def M3b(qb=qb, tb=tb, b=b):
                    pT = self.bank_bf(7, 384, 448)
                    op('pe', lambda e: e.transpose(out=pT, in_=mg[b].ap, identity=self.ident.ap),
                       r=[mg[b].k(), self.ident.k()], w=[self.ps(7)])
                    op('dve', lambda e: e.tensor_copy(out=ms_.ap[:, qb * 128:(qb + 1) * 128], in_=pT),
                       w=[self.ps(7), ms_.k(qb)])
                stages.append((M1, M2, M3, M3b))
            for slot in range(4 + 3):
                for k in (3, 2, 1, 0):
                    if 0 <= slot - k < 4:
                        steps.append(stages[slot - k][k])

            def M4():
                op('sp', lambda e: e.dma_start(out=self.mT_d[h, :, j * 512:(j + 1) * 512], in_=ms_.ap),
                   r=[ms_.k(q) for q in range(4)], w=[('mT_d', h, j)], dma='mst%d' % (j % 4))
            steps.append(M4)
            return steps

        step = 0
        for j in range(NQ):
            nk = 4 * j + 4
            per = 1 if len(self.deferred) <= 2 * nk + 17 else 2
            sS(j, 0, step)
            sS(j, 1, step + 1)
            for kt in range(nk):
                if kt + 2 < nk:
                    sS(j, kt + 2, step + 2)
                sV(j, kt, step)
                step += 1
                self.pump(per)
            while len(self.deferred) > 36:
                self.pump(1)
            self.deferred = self.deferred + merge_steps(j)
            if self.bg:
                self.bg.pop(0)()
        self.flush()
        self.top = mark

    def tail(self):
        A, op, S, dr = self.alloc, self.op, self.S, self.dr
        NTT = S // 512
        self.top = self.persist_top
        G1b = A('G1b', [128, D], F32)
        G2b = A('G2b', [128, D], F32)
        wout = A('wout', [128, 8, D], BF16)
        diag = [A('diag%d' % i, [128, 128], F32) for i in range(2)]
        x1 = [A('x1_%d' % i, [128, 4, D], F32) for i in range(3)]
        mTt = [A('mTt%d' % i, [128, 8, 512], BF16) for i in range(2)]
        hffT = A('hffT', [128, 8, 512], BF16)
        hT = A('hT', [128, 32, 512], BF16)
        xn2 = [A('xn2_%d' % i, [128, D], BF16) for i in range(2)]
        rl = [A('rl%d' % i, [128, 512], F32) for i in range(2)]
        yg = [A('yg%d' % i, [128, D], F32) for i in range(2)]
        wupc = [A('wupc%d' % i, [128, 8, 512], BF16) for i in range(3)]
        wdnc = [A('wdnc%d' % i, [128, 4, 512], BF16) for i in range(6)]
        junk = A('tjunk', [128, D], BF16)
        ss = A('ss2', [128, S // 128], F32)
        ms = A('ms2', [128, S // 128], F32)
        rstd = A('rstd2', [128, S // 128], F32)

        src = dr['w_out'].rearrange("(g p) n -> p g n", p=128)

        def ldw(e):
            return [e.dma_start(out=wout.ap[:, g0:g0 + 4, :], in_=src[:, g0:g0 + 4, :]) for g0 in (0, 4)]
        op('pool', ldw, w=[wout.k()], dma='wout', ndma=2)

        for gi, (G, c0) in enumerate(((G1b, 16), (G2b, 40))):
            for half in range(2):
                for q in range(4):
                    dc = half * 4 + q
                    dg = diag[dc % 2]
                    op('dve', lambda e, dg=dg, dc=dc, c0=c0: e.tensor_scalar(
                        out=dg.ap, in0=self.identf.ap, scalar1=self.modc.ap[:, c0 + dc:c0 + dc + 1], scalar2=None,
                        op0=ALU.mult), r=[self.identf.k(), self.modc.k(1)], w=[dg.k()])
                    op('pe', lambda e, dg=dg, q=q, half=half: e.matmul(
                        self.banks[half][:, q * 128:(q + 1) * 128], lhsT=self.onesf.ap, rhs=dg.ap, start=True, stop=True),
                       r=[self.onesf.k(), dg.k()], w=[self.ps(half)])
                op('act', lambda e, G=G, half=half: e.activation(out=G.ap[:, half * 512:(half + 1) * 512],
                                                                 in_=self.banks[half][:, 0:512], func=AF.Copy),
                   w=[self.ps(half), G.k(half)])

        sctr = {'up': 0, 'dn': 0}

        pf = {'up': 0, 'dn': 0}
        n_up, n_dn = 8 * NTT, 16 * NTT

        def issue_up(upto):
            while pf['up'] < min(upto, n_up):
                g = pf['up']
                wb = wupc[g % 3]
                c = g % 8
                op('sp', lambda e, wb=wb, c=c: e.dma_start(out=wb.ap, in_=self.wup_s[c]), r=[('wup_s',)], w=[wb.k()],
                   dma=wb.name)
                pf['up'] += 1

        def issue_dn(upto):
            while pf['dn'] < min(upto, n_dn):
                g = pf['dn']
                wb = wdnc[g % 6]
                nb, c = divmod(g % 16, 8)
                op('sp', lambda e, wb=wb, c=c, nb=nb: e.dma_start(out=wb.ap, in_=self.wdn_s[nb, c]),
                   r=[('wdn_s',)], w=[wb.k()], dma=wb.name)
                pf['dn'] += 1

        def load_tile(i):
            xb = x1[i % 3]
            xsrc = self.x[i * 512:(i + 1) * 512, :].rearrange("(s p) d -> p s d", p=128)
            op('sp', lambda e: e.dma_start(out=xb.ap, in_=xsrc), w=[xb.k(s) for s in range(4)], dma='x1_%d' % (i % 3))
            mt = mTt[i % 2]
            msrc = self.mT_d[:, :, i * 512:(i + 1) * 512].rearrange("g p t -> p g t")
            op('sp', lambda e: e.dma_start(out=mt.ap, in_=msrc), r=[('mT_d', g, i) for g in range(8)],
               w=[mt.k()], dma='mTt%d' % (i % 2))

        def O_steps(i):
            xb = x1[i % 3]
            mt = mTt[i % 2]
            Oa, Ob1, Ob2 = [], [], []
            for s in range(4):
                t = i * 4 + s
                xn = xn2[s % 2]

                def fa(s=s, t=t):
                    def mm(e):
                        for nb in range(2):
                            for g in range(8):
                                last = e.matmul(self.banks[6 + nb][:, 0:512], lhsT=mt.ap[:, g, s * 128:(s + 1) * 128],
                                                rhs=wout.ap[:, g, nb * 512:(nb + 1) * 512], start=(g == 0), stop=(g == 7))
                        return last
                    op('pe', mm, r=[mt.k(), wout.k()], w=[self.ps(6), self.ps(7)])
                    y = yg[s % 2]
                    for nb in range(2):
                        op('dve', lambda e, nb=nb: e.tensor_tensor(out=y.ap[:, nb * 512:(nb + 1) * 512],
                                                                    in0=self.banks[6 + nb][:, 0:512],
                                                                    in1=G1b.ap[:, nb * 512:(nb + 1) * 512], op=ALU.mult),
                           r=[G1b.k(nb)], w=[self.ps(6 + nb), y.k(nb)])
                    op('pool', lambda e: e.tensor_tensor(out=xb.ap[:, s, :], in0=y.ap, in1=xb.ap[:, s, :], op=ALU.add),
                       r=[y.k(0), y.k(1), xb.k(s)], w=[xb.k(s)])
                    op('act', lambda e: e.activation(out=junk.ap, in_=xb.ap[:, s, :], func=AF.Square,
                                                     accum_out=ss.ap[:, t:t + 1]), r=[xb.k(s)], w=[junk.k(), ss.k(t)])

                def fb1(s=s, t=t, xn=xn):
                    op('act', lambda e: e.activation(out=ms.ap[:, t:t + 1], in_=ss.ap[:, t:t + 1], func=AF.Sqrt,
                                                     scale=1.0 / D, bias=self.epsc.ap), r=[ss.k(t), self.epsc.k()], w=[ms.k(t)])
                    op('dve', lambda e: e.reciprocal(out=rstd.ap[:, t:t + 1], in_=ms.ap[:, t:t + 1]), r=[ms.k(t)], w=[rstd.k(t)])
                    op('dve', lambda e: e.tensor_scalar(out=xn.ap, in0=xb.ap[:, s, :], scalar1=rstd.ap[:, t:t + 1],
                                                        scalar2=None, op0=ALU.mult), r=[xb.k(s), rstd.k(t)], w=[xn.k()])

                def fb2(s=s, xn=xn):
                    pa = self.bank_bf(6, 0, 256).rearrange("p (a b) -> p a b", b=128)
                    pb = self.bank_bf(7, 0, 256).rearrange("p (a b) -> p a b", b=128)

                    def tr(e):
                        for dc in range(8):
                            pt = pa if dc < 4 else pb
                            last = e.transpose(out=pt[:, dc % 4, :], in_=xn.ap[:, dc * 128:(dc + 1) * 128],
                                               identity=self.ident.ap)
                        return last
                    op('pe', tr, r=[xn.k(), self.ident.k()], w=[self.ps(6), self.ps(7)])
                    A_, Bc = self.A2, self.modc.ap[:, 24:32]

                    def ev_act(e):
                        for dc in range(4):
                            last = e.activation(out=hffT.ap[:, dc, s * 128:(s + 1) * 128], in_=pa[:, dc, :],
                                                func=AF.Identity, scale=A_.ap[:, dc:dc + 1], bias=Bc[:, dc:dc + 1])
                        return last
                    op('act', ev_act, r=[A_.k(), self.modc.k(1)], w=[self.ps(6), hffT.k(s, 0)])

                    def ev_dve(e):
                        for dc in range(4, 8):
                            last = e.tensor_scalar(out=hffT.ap[:, dc, s * 128:(s + 1) * 128], in0=pb[:, dc - 4, :],
                                                   scalar1=A_.ap[:, dc:dc + 1], scalar2=Bc[:, dc:dc + 1],
                                                   op0=ALU.mult, op1=ALU.add)
                        return last
                    op('dve', ev_dve, r=[A_.k(), self.modc.k(1)], w=[self.ps(7), hffT.k(s, 1)])
                Oa.append(fa); Ob1.append(fb1); Ob2.append(fb2)
            return [Oa[0], Ob1[0], Oa[1], Ob2[0], Ob1[1], Oa[2], Ob2[1], Ob1[2], Oa[3], Ob2[2], Ob1[3], (lambda: None), Ob2[3]]

        def U(i):
            for c in range(8):
                g = i * 8 + c
                issue_up(g + 3)
                if c >= 4:
                    issue_dn(i * 16 + (c - 3))
                wb = wupc[g % 3]
                for fl in range(4):
                    fb = 4 * c + fl
                    bk = 4 + fb % 2

                    def mm(e, fl=fl, bk=bk, wb=wb):
                        for dc in range(8):
                            last = e.matmul(self.banks[bk][:, 0:512], lhsT=wb.ap[:, dc, fl * 128:(fl + 1) * 128],
                                            rhs=hffT.ap[:, dc, :], start=(dc == 0), stop=(dc == 7))
                        return last
                    op('pe', mm, r=[wb.k()] + sum([self.k2(hffT, s) for s in range(4)], []), w=[self.ps(bk)])
                    r_ = rl[fb % 2]
                    op('act', lambda e, bk=bk, r_=r_: e.activation(out=r_.ap, in_=self.banks[bk][:, 0:512], func=AF.Relu),
                       w=[self.ps(bk), r_.k()])
                    op('dve', lambda e, r_=r_, fb=fb: e.tensor_tensor(out=hT.ap[:, fb, :], in0=r_.ap, in1=r_.ap, op=ALU.mult),
                       r=[r_.k()], w=[hT.k(fb)])

        def Dn(i):
            xb = x1[i % 3]
            for nb in range(2):
                for c in range(8):
                    g = i * 16 + nb * 8 + c
                    issue_dn(g + 5)
                    if nb == 1 and c >= 5:
                        issue_up((i + 1) * 8 + (c - 4))
                    wb = wdnc[g % 6]

                    def mm(e, wb=wb, c=c):
                        for s in range(4):
                            for fl in range(4):
                                last = e.matmul(self.banks[s][:, 0:512], lhsT=hT.ap[:, 4 * c + fl, s * 128:(s + 1) * 128],
                                                rhs=wb.ap[:, fl, :], start=(c == 0 and fl == 0), stop=(c == 7 and fl == 3))
                        return last
                    op('pe', mm, r=[wb.k()] + [hT.k(4 * c + fl) for fl in range(4)], w=[self.ps(s) for s in range(4)])
                    self.pump(1)
                for s in range(4):
                    y = yg[s % 2]
                    op('dve', lambda e, s=s, y=y, nb=nb: e.tensor_tensor(out=y.ap[:, 0:512], in0=self.banks[s][:, 0:512],
                                                                         in1=G2b.ap[:, nb * 512:(nb + 1) * 512], op=ALU.mult),
                       r=[G2b.k(nb)], w=[self.ps(s), y.k(0)])
                    op('pool', lambda e, s=s, y=y, nb=nb: e.tensor_tensor(
                        out=xb.ap[:, s, nb * 512:(nb + 1) * 512], in0=y.ap[:, 0:512],
                        in1=xb.ap[:, s, nb * 512:(nb + 1) * 512], op=ALU.add), r=[y.k(0), xb.k(s)], w=[xb.k(s)])
            dst = self.out[i * 512:(i + 1) * 512, :].rearrange("(s p) d -> p s d", p=128)
            op('pool', lambda e: e.dma_start(out=dst, in_=xb.ap), r=[xb.k(s) for s in range(4)], w=[('out', i)],
               dma='ost%d' % (i % 3))
            self.outkeys.append(('out', i))

        load_tile(0)
        st0 = O_steps(0)
        for k in (0, 2, 1, 5, 4, 3, 8, 7, 6, 10, 9, 12):
            st0[k]()
        for i in range(NTT):
            if i + 1 < NTT:
                load_tile(i + 1)
                self.deferred = O_steps(i + 1)
            U(i)
            self.pump(1)
            Dn(i)
            self.flush()

    def build(self, upto='all'):
        self.setup()
        A = self.alloc
        self.hmixT = A('hmixT', [128, 8, self.S], BF16)
        self.orr = A('orr', [128, self.NT, 256], BF16)
        self.ropeA = A('ropeA', [128, self.NT, 2, 64], F32)
        self.wslots = [A('wslot0', [128, 8, 768], BF16), A('wslot1', [128, 8, 768], BF16)]
        self.mixer_top = self.top
        self.bg = []
        phases = []
        for r in range(4):
            phases += [('ret', r), ('da', 2 * r), ('da', 2 * r + 1)]
        if upto == 'ret0':
            phases = phases[:1]
        elif upto == 'da0':
            phases = phases[:2]

        def wblocks(ph):
            kind, i = ph
            if kind == 'ret':
                return [(C_QR + i * 128, 128), (C_KR + i * 128, 128), (C_VR + i * 256, 256)]
            return [(C_QA + i * 128, 128), (C_KA + i * 128, 128), (C_VA + i * 128, 128),
                    (C_GA + i * 128, 128), (C_GB + i * 128, 128), (C_GR + i * 128, 128)]
        self.phase_mod()
        self.load_w(0, wblocks(phases[0]))
        self.op('sp', lambda e: e.dma_start(out=self.ropeA.ap, in_=self.dr['ropeA']), w=[self.ropeA.k()], dma='ropeA')
        self.phase_norm1()
        self.top = self.mixer_top
        if upto == 'norm1':
            return self.finish()
        self.ffn_scratch()
        for n, ph in enumerate(phases):
            if n + 1 < len(phases):
                self.load_w((n + 1) % 2, wblocks(phases[n + 1]))
            if ph[0] == 'ret':
                self.ret_head(ph[1], n % 2)
            else:
                self.da_head(ph[1], n % 2)
        while self.bg:
            self.bg.pop(0)()
        if upto != 'all':
            return self.finish()
        self.tail()
        return self.finish()

    def finish(self):
        self.flush()
        keys = list(self.outkeys)
        self.op('sp', lambda e: e.nop(), r=keys + [('wup_s',), ('wdn_s',)], sig=False)
        self.sch.run_block()


def build_program(S=4096, dbg=(), upto='all'):
    nc = bass.Bass("TRN2", target_bir_lowering=False)
    with ExitStack() as es:
        kb = KB(nc, es, S, dbg)
        kb.build(upto)
    return nc


def host_tables(S):
    NT = S // 128
    pos = np.arange(S, dtype=np.float32)
    f32 = np.float32
    invA = (10000.0 ** (-np.arange(0, 64, 2, dtype=f32) / f32(64))).astype(f32)
    angA = (pos[:, None] * invA[None, :]).astype(f32).astype(np.float64)
    cA, sA = np.cos(angA).astype(f32), np.sin(angA).astype(f32)
    ropeA = np.stack([np.concatenate([cA, cA], 1), np.concatenate([-sA, sA], 1)], 1)
    ropeA = np.ascontiguousarray(ropeA.reshape(NT, 128, 2, 64).transpose(1, 0, 2, 3))
    invR = (1.0 / (f32(10000.0) ** np.linspace(0.0, 1.0, 64, dtype=f32))).astype(f32)
    angR = (pos[:, None] * invR[None, :]).astype(f32).astype(np.float64)
    cR, sR = np.cos(angR).astype(f32), np.sin(angR).astype(f32)
    ropeR = np.stack([np.concatenate([cR, cR], 1), np.concatenate([-sR, sR], 1)], 1)
    ropeR = np.ascontiguousarray(ropeR.reshape(NT, 128, 2, 128).transpose(1, 0, 2, 3))
    n = np.arange(128, dtype=np.float64)
    retc = np.zeros((128, 8), f32)
    for r in range(4):
        lg = np.log(1.0 - 2.0 ** (-5.0 - r))
        retc[:, 2 * r] = np.exp((n + 1.0) * lg)
        retc[:, 2 * r + 1] = (128.0 ** -0.5) * np.exp(-(n + 1.0) * lg)
    jj, ii = np.meshgrid(np.arange(128), np.arange(128), indexing='ij')
    retmask = (ii >= jj).astype(f32)
    return dict(ropeA=ropeA, ropeR=ropeR, retc=retc, retmask=retmask, identf=np.eye(128, dtype=f32))


def core_inputs(b, S, inp, tabs):
    f = lambda a: np.ascontiguousarray(np.asarray(a, dtype=np.float32))
    col = lambda v, n: f(np.asarray(v, np.float32).reshape(n, 128).T)
    m = dict(tabs)
    m['x'] = f(inp['x'][b, :S])
    m['c_col'] = col(inp['c'][b], 8)
    m['w_ada'] = f(inp['w_ada'][0])
    m['b_ada_c'] = col(inp['b_ada'][0], 48)
    m['g1c'] = col(inp['g_norm1'][0], 8)
    m['g2c'] = col(inp['g_norm2'][0], 8)
    m['w_in'] = f(inp['w_in'][0])
    gq, gk = np.asarray(inp['g_q'][0], np.float32), np.asarray(inp['g_k'][0], np.float32)
    m['gqk_row'] = f(np.concatenate([gq, gq, gk, gk])[None, :])
    m['lam_row'] = f(np.concatenate([inp['lambda_q1'][0], inp['lambda_q2'][0],
                                     inp['lambda_k1'][0], inp['lambda_k2'][0]])[None, :])
    m['gda_row'] = f(np.asarray(inp['g_da_out'][0])[None, :])
    m['gret_row'] = f(np.asarray(inp['g_ret_out'][0])[None, :])
    m['w_out'] = f(inp['w_out'][0])
    m['w_up'] = f(inp['w_up'][0])
    m['w_down'] = f(inp['w_down'][0])
    return m


def kernel(**inputs):
    S = 4096
    inp = {k: np.asarray(v) for k, v in inputs.items()}
    B = inp['x'].shape[0]
    tabs = host_tables(S)
    nc = build_program(S)
    in_maps = [core_inputs(b, S, inp, tabs) for b in range(B)]
    res = run_bass_kernel_spmd(nc, in_maps, core_ids=list(range(B)))
    out = np.stack([np.asarray(res.results[b]['out']) for b in range(B)], 0)
    return out.astype(np.float32)
```
